# Optimizing a Trainium2 kernel written in Bass

```python
import math
import jax, jax.numpy as jnp
from jax import lax
import numpy as np

D_MODEL = 1024
BATCH = 8
SEQ = 8192
DEPTH = 2

CTX_LEN = 256
GRID_W = 64
CHUNK = 128
ROPE_BASE = 10000.0
f32 = jnp.float32

MIX_WIDTH = 2 * D_MODEL
SSD_INNER = D_MODEL
SSD_HEAD_DIM = 64
SSD_HEADS = SSD_INNER // SSD_HEAD_DIM
SSD_GROUPS = 2
SSD_HPG = SSD_HEADS // SSD_GROUPS
SSD_STATE = 128
SSD_CONV = 5
SSD_XBC = SSD_INNER + 2 * SSD_GROUPS * SSD_STATE
SSD_COLS = SSD_INNER + SSD_XBC + 2 * SSD_HEADS
DIFF_WIDTH = D_MODEL // 2
DIFF_V_DIM = 128
DIFF_HEADS = DIFF_WIDTH // DIFF_V_DIM
DIFF_HEAD_DIM = DIFF_V_DIM // 2
DIFF_QK = DIFF_HEADS * 2 * DIFF_HEAD_DIM
DIFF_COLS = 2 * DIFF_QK + DIFF_WIDTH
RET_WIDTH = D_MODEL // 2
RET_V_DIM = 128
RET_HEADS = RET_WIDTH // RET_V_DIM
RET_QK_DIM = RET_V_DIM // 2
RET_QK = RET_HEADS * RET_QK_DIM
RET_COLS = 2 * RET_QK + 2 * RET_WIDTH
IN_COLS = SSD_COLS + DIFF_COLS + RET_COLS
D_FF = 11 * D_MODEL // 4
FFN_CONV = 3
ALPHA = (2.0 * DEPTH) ** 0.25
BETA = (8.0 * DEPTH) ** -0.25

kernel_name = "hybrid_ssd_diffattn_retention_dit_block"


def _layernorm(x, w, b, eps=1e-5):
    xf = x.astype(f32)
    mu = xf.mean(-1, keepdims=True)
    var = jnp.mean(jnp.square(xf - mu), -1, keepdims=True)
    return ((xf - mu) * lax.rsqrt(var + eps) * w + b).astype(x.dtype)


def _groupnorm(x, w, eps=1e-5):
    xf = x.astype(f32)
    mu = xf.mean(-1, keepdims=True)
    var = jnp.mean(jnp.square(xf - mu), -1, keepdims=True)
    return ((xf - mu) * lax.rsqrt(var + eps) * w).astype(x.dtype)


def _rmsnorm(x, w, eps=1e-5):
    xf = x.astype(f32)
    return (xf * lax.rsqrt(jnp.mean(jnp.square(xf), -1, keepdims=True) + eps) * w).astype(x.dtype)


def _dwconv(x, w, b):
    k = w.shape[0]
    p = k // 2
    t = x.shape[1]
    xp = jnp.pad(x, ((0, 0), (p, p), (0, 0)))
    out = xp[:, 0:t] * w[0]
    for j in range(1, k):
        out = out + xp[:, j:j + t] * w[j]
    return out + b


def _rotate_half(x):
    h = x.shape[-1] // 2
    return jnp.concatenate([-x[..., h:], x[..., :h]], axis=-1)


def _rope(x, ang):
    ang2 = jnp.concatenate([ang, ang], axis=-1)
    return x * jnp.cos(ang2).astype(x.dtype) + _rotate_half(x) * jnp.sin(ang2).astype(x.dtype)


def _axial_rope(x, ang_row, ang_col):
    h = x.shape[-1] // 2
    return jnp.concatenate([_rope(x[..., :h], ang_row), _rope(x[..., h:], ang_col)], axis=-1)


def _chunked_scan(q, k, v, log_a, h0):
    bsz, t, g, n = q.shape
    r, p = v.shape[3], v.shape[4]
    nc = t // CHUNK
    qc = q.astype(f32).reshape(bsz, nc, CHUNK, g, n)
    kc = k.astype(f32).reshape(bsz, nc, CHUNK, g, n)
    vc = v.astype(f32).reshape(bsz, nc, CHUNK, g, r, p)
    acum = jnp.cumsum(log_a.astype(f32).reshape(bsz, nc, CHUNK, g, r).transpose(0, 1, 3, 4, 2), axis=-1)
    tril = jnp.tril(jnp.ones((CHUNK, CHUNK), dtype=bool))
    seg = acum[..., :, None] - acum[..., None, :]
    decay = jnp.exp(jnp.where(tril, seg, -jnp.inf))
    scores = jnp.einsum('bclgn,bcsgn->bcgls', qc, kc)
    y_diag = jnp.einsum('bcgrls,bcsgrp->bclgrp', scores[:, :, :, None] * decay, vc)
    to_end = jnp.exp(acum[..., -1:] - acum)
    states = jnp.einsum('bcsgn,bcgrs,bcsgrp->bcgrpn', kc, to_end, vc)
    chunk_decay = jnp.exp(acum[..., -1])

    def step(h, inp):
        st, dec = inp
        return dec[..., None, None] * h + st, h

    h_last, h_start = lax.scan(step, h0.astype(f32),
                               (jnp.moveaxis(states, 1, 0), jnp.moveaxis(chunk_decay, 1, 0)))
    h_start = jnp.moveaxis(h_start, 0, 1)
    y_off = jnp.einsum('bclgn,bcgrpn,bcgrl->bclgrp', qc, h_start, jnp.exp(acum))
    y = (y_diag + y_off).reshape(bsz, t, g, r, p).astype(v.dtype)
    return y, h_last


def _bidir_scan(q, k, v_f, la_f, v_b, la_b, h0_f, h0_b):
    y_f, h_f = _chunked_scan(q, k, v_f, la_f, h0_f)
    flip = lambda a: jnp.flip(a, axis=1)
    y_b, h_b = _chunked_scan(flip(q), flip(k), flip(v_b), flip(la_b), h0_b)
    return y_f + flip(y_b), h_f, h_b


def _ssd_mixer(p_ssd, conv_w, conv_b, a_log, dt_bias, d_skip, norm_w, h0_f, h0_b):
    bsz, t, _ = p_ssd.shape
    z, xbc, dt = jnp.split(p_ssd, [SSD_INNER, SSD_INNER + SSD_XBC], axis=-1)
    xbc = jax.nn.silu(_dwconv(xbc, conv_w, conv_b))
    xs, bm, cm = jnp.split(xbc, [SSD_INNER, SSD_INNER + SSD_GROUPS * SSD_STATE], axis=-1)
    xs = xs.reshape(bsz, t, SSD_GROUPS, SSD_HPG, SSD_HEAD_DIM)
    bm = bm.reshape(bsz, t, SSD_GROUPS, SSD_STATE)
    cm = cm.reshape(bsz, t, SSD_GROUPS, SSD_STATE)
    dt = jax.nn.softplus(dt.astype(f32).reshape(bsz, t, 2, SSD_HEADS) + dt_bias.astype(f32))
    a = -jnp.exp(a_log.astype(f32))
    la = (dt * a).reshape(bsz, t, 2, SSD_GROUPS, SSD_HPG)
    dt = dt.reshape(bsz, t, 2, SSD_GROUPS, SSD_HPG)
    v_f = xs * dt[:, :, 0][..., None]
    v_b = xs * dt[:, :, 1][..., None]
    y, h_f, h_b = _bidir_scan(cm, bm, v_f, la[:, :, 0], v_b, la[:, :, 1], h0_f, h0_b)
    y = y + xs * d_skip.reshape(SSD_GROUPS, SSD_HPG, 1)
    y = y.reshape(bsz, t, SSD_INNER)
    y = _rmsnorm(y * jax.nn.silu(z), norm_w)
    return y.astype(p_ssd.dtype), h_f, h_b


def _ret_mixer(p_ret, ang, decay_raw, norm_w, h0_f, h0_b):
    bsz, t, _ = p_ret.shape
    q, k, v, g = jnp.split(p_ret, [RET_QK, 2 * RET_QK, 2 * RET_QK + RET_WIDTH], axis=-1)
    q = _rope(q.reshape(bsz, t, RET_HEADS, RET_QK_DIM), ang)
    k = _rope(k.reshape(bsz, t, RET_HEADS, RET_QK_DIM), ang) * (RET_QK_DIM ** -0.5)
    v = v.reshape(bsz, t, RET_HEADS, 1, RET_V_DIM)
    log_gamma = -jnp.exp(decay_raw.astype(f32))
    la_f = jnp.broadcast_to(log_gamma[0][:, None], (bsz, t, RET_HEADS, 1))
    la_b = jnp.broadcast_to(log_gamma[1][:, None], (bsz, t, RET_HEADS, 1))
    y, h_f, h_b = _bidir_scan(q, k, v, la_f, v, la_b, h0_f, h0_b)
    y = _groupnorm(y.reshape(bsz, t, RET_HEADS, RET_V_DIM), norm_w).reshape(bsz, t, RET_WIDTH)
    return (y * jax.nn.silu(g)).astype(p_ret.dtype), h_f, h_b


def _diff_split(p_diff, ang_row, ang_col):
    bsz, t, _ = p_diff.shape
    q, k, v = jnp.split(p_diff, [DIFF_QK, 2 * DIFF_QK], axis=-1)
    q = q.reshape(bsz, t, DIFF_HEADS, 2, DIFF_HEAD_DIM)
    k = k.reshape(bsz, t, DIFF_HEADS, 2, DIFF_HEAD_DIM)
    if ang_row is not None:
        q = _axial_rope(q, ang_row, ang_col)
        k = _axial_rope(k, ang_row, ang_col)
    q = q.transpose(0, 2, 3, 1, 4)
    k = k.transpose(0, 2, 3, 1, 4)
    v = v.reshape(bsz, t, DIFF_HEADS, DIFF_V_DIM).transpose(0, 2, 1, 3)
    return q, k, v


def _diff_attend(q, k, v, lam):
    s = jnp.einsum('bhmqd,bhmkd->bhmqk', q, k).astype(f32) * (DIFF_HEAD_DIM ** -0.5)
    p = jax.nn.softmax(s, axis=-1)
    a = p[:, :, 0] - lam * p[:, :, 1]
    return jnp.einsum('bhqk,bhkv->bhqv', a.astype(v.dtype), v)


def _diff_out(o, norm_w, lam_init):
    bsz, h, t, vd = o.shape
    o = _rmsnorm(o, norm_w) * (1.0 - lam_init)
    return o.transpose(0, 2, 1, 3).reshape(bsz, t, h * vd)


def _conv_ffn(h, w_up, conv_w, conv_b, w_down):
    u, v = jnp.split(h @ w_up, 2, axis=-1)
    return (jax.nn.gelu(_dwconv(u, conv_w, conv_b), approximate=False) * v) @ w_down


def setup_inputs(seed: int = 0) -> dict:
    key = jax.random.key(seed)
    ks = jax.random.split(key, 32)
    L, D = DEPTH, D_MODEL
    nrm = lambda k, shape, scale: jax.random.normal(k, shape, f32) * scale
    dt0 = jnp.exp(jax.random.uniform(ks[10], (L, 2, SSD_HEADS), f32, math.log(1e-3), math.log(1e-1)))
    ret_init = jnp.log(-jnp.log1p(-(2.0 ** (-5.0 - jnp.arange(RET_HEADS, dtype=f32)))))
    return {
        "x": nrm(ks[0], (BATCH, SEQ, D), 1.0),
        "c": nrm(ks[1], (BATCH, D), 1.0),
        "ctx": nrm(ks[2], (BATCH, CTX_LEN, D), 1.0),
        "c_ctx": nrm(ks[3], (D,), 1.0),
        "w_ada": nrm(ks[4], (L, D, 6 * D), 0.5 * D ** -0.5),
        "b_ada": nrm(ks[5], (L, 6 * D), 0.02),
        "w_in": nrm(ks[6], (L, D, IN_COLS), D ** -0.5),
        "ssd_conv_w": nrm(ks[7], (L, SSD_CONV, SSD_XBC), SSD_CONV ** -0.5),
        "ssd_conv_b": nrm(ks[8], (L, SSD_XBC), 0.02),
        "ssd_a_log": jnp.log(jax.random.uniform(ks[9], (L, 2, SSD_HEADS), f32, 1.0, 16.0)),
        "ssd_dt_bias": dt0 + jnp.log(-jnp.expm1(-dt0)),
        "ssd_d": 1.0 + nrm(ks[11], (L, SSD_HEADS), 0.1),
        "ssd_norm_w": 1.0 + nrm(ks[12], (L, SSD_INNER), 0.05),
        "diff_lambda": nrm(ks[13], (L, 4, DIFF_HEAD_DIM), 0.1),
        "diff_norm_w": 1.0 + nrm(ks[14], (L, DIFF_V_DIM), 0.05),
        "ret_decay": ret_init + nrm(ks[15], (L, 2, RET_HEADS), 0.05),
        "ret_norm_w": 1.0 + nrm(ks[16], (L, RET_V_DIM), 0.05),
        "w_out": nrm(ks[17], (L, MIX_WIDTH, D), BETA * MIX_WIDTH ** -0.5),
        "ln1_w": 1.0 + nrm(ks[18], (L, D), 0.05),
        "ln1_b": nrm(ks[19], (L, D), 0.02),
        "ffn_w_up": nrm(ks[20], (L, D, 2 * D_FF), D ** -0.5),
        "ffn_conv_w": nrm(ks[21], (L, FFN_CONV, D_FF), FFN_CONV ** -0.5),
        "ffn_conv_b": nrm(ks[22], (L, D_FF), 0.02),
        "ffn_w_down": nrm(ks[23], (L, D_FF, D), BETA * D_FF ** -0.5),
        "ln2_w": 1.0 + nrm(ks[24], (L, D), 0.05),
        "ln2_b": nrm(ks[25], (L, D), 0.02),
    }


def reference(x, c, ctx, c_ctx, w_ada, b_ada, w_in, ssd_conv_w, ssd_conv_b, ssd_a_log, ssd_dt_bias, ssd_d,
              ssd_norm_w, diff_lambda, diff_norm_w, ret_decay, ret_norm_w, w_out, ln1_w, ln1_b, ffn_w_up,
              ffn_conv_w, ffn_conv_b, ffn_w_down, ln2_w, ln2_b):
    bsz, s, _ = x.shape
    n_ctx = ctx.shape[1]
    rows = s // GRID_W
    row = jnp.repeat(jnp.arange(rows, dtype=f32), GRID_W)
    col = jnp.tile(jnp.arange(GRID_W, dtype=f32), rows)
    n_ax = DIFF_HEAD_DIM // 4
    inv_ax = 1.0 / (ROPE_BASE ** (jnp.arange(n_ax, dtype=f32) / n_ax))
    ang_row = (row[:, None] * inv_ax)[:, None, None, :]
    ang_col = (col[:, None] * inv_ax)[:, None, None, :]
    inv_ret = 1.0 / (ROPE_BASE ** jnp.linspace(0.0, 1.0, RET_QK_DIM // 2, dtype=f32))
    ang_ret_c = (jnp.arange(n_ctx, dtype=f32)[:, None] * inv_ret)[:, None, :]
    ang_ret_l = ((n_ctx + jnp.arange(s, dtype=f32))[:, None] * inv_ret)[:, None, :]
    silu_c = jax.nn.silu(c)
    silu_cc = jax.nn.silu(c_ctx)
    nb = s // CHUNK
    xc = ctx
    for li in range(DEPTH):
        last = li == DEPTH - 1
        sh_a, sc_a, g_a, sh_f, sc_f, g_f = jnp.split((silu_c @ w_ada[li] + b_ada[li])[:, None, :], 6, axis=-1)
        csh_a, csc_a, cg_a, csh_f, csc_f, cg_f = jnp.split((silu_cc @ w_ada[li] + b_ada[li])[None, None, :], 6, axis=-1)

        proj = (x * (1.0 + sc_a) + sh_a) @ w_in[li]
        proj_c = (xc * (1.0 + csc_a) + csh_a) @ w_in[li]
        p_ssd, p_diff, p_ret = jnp.split(proj, [SSD_COLS, SSD_COLS + DIFF_COLS], axis=-1)
        pc_ssd, pc_diff, pc_ret = jnp.split(proj_c, [SSD_COLS, SSD_COLS + DIFF_COLS], axis=-1)

        zero_ssd = jnp.zeros((bsz, SSD_GROUPS, SSD_HPG, SSD_HEAD_DIM, SSD_STATE), f32)
        ssd_args = (ssd_conv_w[li], ssd_conv_b[li], ssd_a_log[li], ssd_dt_bias[li], ssd_d[li], ssd_norm_w[li])
        yc_ssd, hs_f, hs_b = _ssd_mixer(pc_ssd, *ssd_args, zero_ssd, zero_ssd)
        y_ssd, _, _ = _ssd_mixer(p_ssd, *ssd_args, hs_f, hs_b)

        zero_ret = jnp.zeros((bsz, RET_HEADS, 1, RET_V_DIM, RET_QK_DIM), f32)
        yc_ret, hr_f, hr_b = _ret_mixer(pc_ret, ang_ret_c, ret_decay[li], ret_norm_w[li], zero_ret, zero_ret)
        y_ret, _, _ = _ret_mixer(p_ret, ang_ret_l, ret_decay[li], ret_norm_w[li], hr_f, hr_b)

        lam_init = 0.8 - 0.6 * math.exp(-0.3 * li)
        lq1, lk1, lq2, lk2 = diff_lambda[li].astype(f32)
        lam = jnp.exp(jnp.sum(lq1 * lk1)) - jnp.exp(jnp.sum(lq2 * lk2)) + lam_init
        q_l, k_l, v_l = _diff_split(p_diff, ang_row, ang_col)
        q_c, k_c, v_c = _diff_split(pc_diff, None, None)
        k_all = jnp.concatenate([k_l, k_c], axis=3)
        v_all = jnp.concatenate([v_l, v_c], axis=2)
        q_blocks = jnp.moveaxis(q_l.reshape(bsz, DIFF_HEADS, 2, nb, CHUNK, DIFF_HEAD_DIM), 3, 0)
        o_l = lax.map(lambda qb: _diff_attend(qb, k_all, v_all, lam), q_blocks)
        o_l = jnp.moveaxis(o_l, 0, 2).reshape(bsz, DIFF_HEADS, s, DIFF_V_DIM)
        y_diff = _diff_out(o_l, diff_norm_w[li], lam_init)

        y = jnp.concatenate([y_ssd, y_diff, y_ret], axis=-1) @ w_out[li]
        x = _layernorm(ALPHA * x + g_a * y, ln1_w[li], ln1_b[li])
        if not last:
            yc_diff = _diff_out(_diff_attend(q_c, k_c, v_c, lam), diff_norm_w[li], lam_init)
            yc = jnp.concatenate([yc_ssd, yc_diff, yc_ret], axis=-1) @ w_out[li]
            xc = _layernorm(ALPHA * xc + cg_a * yc, ln1_w[li], ln1_b[li])

        f = _conv_ffn(x * (1.0 + sc_f) + sh_f, ffn_w_up[li], ffn_conv_w[li], ffn_conv_b[li], ffn_w_down[li])
        x = _layernorm(ALPHA * x + g_f * f, ln2_w[li], ln2_b[li])
        if not last:
            fc = _conv_ffn(xc * (1.0 + csc_f) + csh_f, ffn_w_up[li], ffn_conv_w[li], ffn_conv_b[li], ffn_w_down[li])
            xc = _layernorm(ALPHA * xc + cg_f * fc, ln2_w[li], ln2_b[li])
    return x
```

```python
import math
from contextlib import ExitStack

import numpy as np
import ml_dtypes
import concourse.bass as bass
import concourse.mybir as mybir
from concourse.bass_utils import run_bass_kernel_spmd

F32 = mybir.dt.float32
BF16 = mybir.dt.bfloat16
AF = mybir.ActivationFunctionType
ALU = mybir.AluOpType
AX = mybir.AxisListType

D = 1024
NCTX = 256
GRID_W = 64
IN_COLS = 5664
DFF = 2816
ENGS = ("pe", "act", "dve", "pool", "sp")


class Res:
    __slots__ = ("name", "last_w", "readers", "sem", "sem_cnt", "base", "sw")

    def __init__(self, name=""):
        self.name = name
        self.last_w = None
        self.readers = []
        self.sem = None
        self.sem_cnt = 0
        self.base = 0
        self.sw = False


class Ins:
    __slots__ = ("eng", "idx", "fn", "deps", "dma", "sem_res", "count", "signal")

    def __init__(self, eng, idx, fn, dma, sem_res):
        self.eng = eng
        self.idx = idx
        self.fn = fn
        self.deps = []
        self.dma = dma
        self.sem_res = sem_res
        self.count = None
        self.signal = False


class Sched:
    def __init__(self, nc, eng_sem, dma_sems, eng_base, dma_base):
        self.nc = nc
        self.lists = {e: [] for e in ENGS}
        self.dma_res = []
        self.eng_sem = eng_sem
        self.dma_sems = dma_sems
        self.eng_base = eng_base
        self.dma_base = dma_base

    def add(self, eng, fn, reads=(), writes=(), dma=False, sem_res=None):
        lst = self.lists[eng]
        ins = Ins(eng, len(lst), fn, dma, sem_res)
        deps = {}
        for r in reads:
            d = r.last_w
            if d is not None:
                deps[id(d)] = d
        for w in writes:
            d = w.last_w
            if d is not None:
                deps[id(d)] = d
            for rd in w.readers:
                deps[id(rd)] = rd
        for r in reads:
            r.readers.append(ins)
        for w in writes:
            w.last_w = ins
            w.readers = []
        best = {}
        out = []
        for d in deps.values():
            if d is ins:
                continue
            if d.dma:
                out.append(d)
            else:
                if d.eng == eng and not dma:
                    if eng == "pe":
                        continue
                    if d.idx < ins.idx - 3:
                        continue
                b = best.get(d.eng)
                if b is None or d.idx > b.idx:
                    best[d.eng] = d
        out.extend(best.values())
        ins.deps = out
        if dma:
            if sem_res.sem is None:
                sem_res.sem = len(self.dma_res)
                sem_res.sw = (eng == "pool")
                self.dma_res.append(sem_res)
            assert sem_res.sw == (eng == "pool")
            sem_res.sem_cnt += 1
            ins.count = sem_res.sem_cnt
        lst.append(ins)
        return ins

    def emit(self):
        nc = self.nc
        for e in ENGS:
            for ins in self.lists[e]:
                for d in ins.deps:
                    d.signal = True
            for ins in reversed(self.lists[e]):
                if not ins.dma:
                    ins.signal = True
                    break
        final = {}
        for e in ENGS:
            c = self.eng_base[e]
            for ins in self.lists[e]:
                if (not ins.dma) and ins.signal:
                    c += 1
                    ins.count = c
            final[e] = c
        nsw = 0
        nhw = 0
        idxs = []
        for r in self.dma_res:
            if r.sw:
                nsw += 1
                idxs.append(len(self.dma_sems) - nsw)
            else:
                idxs.append(nhw)
                nhw += 1
        assert nhw + 24 <= len(self.dma_sems) and nsw <= 24, (nhw, nsw)
        self._idxs = idxs
        for i, r in zip(idxs, self.dma_res):
            r.sem = self.dma_sems[i]
            r.base = self.dma_base[i]
        stats = {}

        def run_engine(e, eo):
            seen = {}
            nw = 0
            for ins in self.lists[e]:
                for d in ins.deps:
                    if d.dma:
                        key = ("d", id(d.sem_res))
                        val = d.sem_res.base + 16 * d.count
                        sem = d.sem_res.sem
                    else:
                        key = ("e", d.eng)
                        val = d.count
                        sem = self.eng_sem[d.eng]
                    if seen.get(key, 0) >= val:
                        continue
                    seen[key] = val
                    eo.wait_ge(sem, val)
                    nw += 1
                bi = ins.fn(eo)
                if ins.dma:
                    bi.then_inc(ins.sem_res.sem, 16)
                elif ins.signal:
                    bi.then_inc(self.eng_sem[e], 1)
            for r in self.dma_res:
                eo.wait_ge(r.sem, r.base + 16 * r.sem_cnt)
            for e2 in ENGS:
                if final[e2] > self.eng_base[e2]:
                    eo.wait_ge(self.eng_sem[e2], final[e2])
            stats[e] = (len(self.lists[e]), nw)

        with nc.Block() as block:
            @block.tensor
            def _(eo):
                run_engine("pe", eo)

            @block.scalar
            def _(eo):
                run_engine("act", eo)

            @block.vector
            def _(eo):
                run_engine("dve", eo)

            @block.gpsimd
            def _(eo):
                run_engine("pool", eo)

            @block.sync
            def _(eo):
                run_engine("sp", eo)
        for e in ENGS:
            self.eng_base[e] = final[e]
        for i, r in zip(self._idxs, self.dma_res):
            self.dma_base[i] = r.base + 16 * r.sem_cnt
        return stats


class Buf:
    __slots__ = ("t", "r")

    def __init__(self, t, name=""):
        self.t = t
        self.r = Res(name)


class Ctx:
    def __init__(self, nc, NL, depth, dbg):
        self.nc = nc
        self.NL = NL
        self.T = NL + NCTX
        self.NCH = self.T // 128
        self.depth = depth
        self.dbg = dbg
        self.dram = {}
        self.S = None
        self.stats = []

    def dram_t(self, name, shape, dt, out=False):
        kind = "ExternalOutput" if (out or self.dbg) else "Internal"
        t = self.nc.dram_tensor(name, list(shape), dt, kind=kind).ap()
        b = Buf(t, name)
        self.dram[name] = b
        return b

    _chains = None

    def op(self, eng, meth, reads, writes, *args, **kw):
        if self._chains is not None:
            self._chains[-1].append((eng, meth, list(reads), list(writes), args, kw))
            return None
        return self.S.add(eng, lambda e: getattr(e, meth)(*args, **kw),
                          reads=[b.r for b in reads], writes=[b.r for b in writes])

    _hold = 0

    def begin_chain(self):
        if self._chains is None:
            self._chains = []
        self._chains.append([])

    def hold_chains(self):
        self._hold += 1

    def release_chains(self):
        self._hold -= 1
        self.flush_chains()

    def flush_chains(self):
        if self._hold > 0 or self._chains is None:
            return
        chains, self._chains = self._chains, None
        n = max(len(c) for c in chains)
        for i in range(n):
            for c in chains:
                if i < len(c):
                    ent = c[i]
                    if ent[0] == "__dma__":
                        self.dma(*ent[1:])
                    else:
                        eng, meth, reads, writes, args, kw = ent
                        self.op(eng, meth, reads, writes, *args, **kw)

    def dma(self, eng, out, in_, reads, writes, semb):
        if self._chains is not None:
            self._chains[-1].append(("__dma__", eng, out, in_, list(reads), list(writes), semb))
            return None
        return self.S.add(eng, lambda e: e.dma_start(out=out, in_=in_),
                          reads=[b.r for b in reads], writes=[b.r for b in writes], dma=True, sem_res=semb.r)

    def new_phase(self):
        self.S = Sched(self.nc, self.eng_sem, self.dma_sems, self.eng_base, self.dma_base)
        return self.S

    def end_phase(self, name):
        st = self.S.emit()
        self.stats.append((name, st))
        self.S = None


_UID = [0]


def _sb(es, nc, name, shape, dt):
    _UID[0] += 1
    name = "%s_%d" % (name, _UID[0])
    return Buf(es.enter_context(nc.sbuf_tensor(name, list(shape), dt)), name)


def _ps(es, nc, name, shape, dt):
    _UID[0] += 1
    name = "%s_%d" % (name, _UID[0])
    return Buf(es.enter_context(nc.psum_tensor(name, list(shape), dt)), name)


def phase_init(K, x_in, ctx_in, X):
    S = K.new_phase()
    S.add("sp", lambda e: e.dma_start(out=X.t[0:NCTX, :], in_=ctx_in), writes=[X.r], dma=True, sem_res=X.r)
    r2 = Res("x2")
    nparts = 4
    step = K.NL // nparts
    for i in range(nparts):
        rr = Res("xi%d" % i)
        S.add("sp" if i % 2 == 0 else "act",
              lambda e, i=i: e.dma_start(out=X.t[NCTX + i * step:NCTX + (i + 1) * step, :],
                                         in_=x_in[i * step:(i + 1) * step, :]),
              writes=[rr], dma=True, sem_res=rr)
    K.end_phase("init")


def phase_mod(K, li, c_fm, cc_fm, w_ada, b_ada, MOD, consts):
    nc = K.nc
    S = K.new_phase()
    with ExitStack() as es:
        cin = _sb(es, nc, "m_cin", [128, 16], F32)
        sil = _sb(es, nc, "m_sil", [128, 16], F32)
        silb = _sb(es, nc, "m_silb", [128, 16, 128], F32)
        brow = _sb(es, nc, "m_brow", [1, 6144], F32)
        ones = _sb(es, nc, "m_ones", [1, 128], F32)
        wblk = [_sb(es, nc, "m_w%d" % i, [128, 8, 512], F32) for i in range(2)]
        stage = [_sb(es, nc, "m_st%d" % i, [128, 512], F32) for i in range(2)]
        ps = [_ps(es, nc, "m_ps%d" % i, [128, 512], F32) for i in range(2)]
        S.add("sp", lambda e: e.dma_start(out=cin.t[:, 0:8], in_=c_fm), writes=[cin.r], dma=True, sem_res=cin.r)
        S.add("sp", lambda e: e.dma_start(out=cin.t[:, 8:16], in_=cc_fm), writes=[cin.r], dma=True, sem_res=cin.r)
        S.add("sp", lambda e: e.dma_start(out=brow.t[:], in_=b_ada[li:li + 1, :]), writes=[brow.r], dma=True, sem_res=brow.r)
        S.add("dve", lambda e: e.memset(ones.t[:], 1.0), writes=[ones.r])
        S.add("act", lambda e: e.activation(out=sil.t[:], in_=cin.t[:], func=AF.Silu), reads=[cin.r], writes=[sil.r])
        S.add("dve", lambda e: e.tensor_copy(out=silb.t[:], in_=sil.t[:].unsqueeze(2).broadcast_to([128, 16, 128])),
              reads=[sil.r], writes=[silb.r])
        k = 0
        for j in range(12):
            wb = wblk[j % 2]
            S.add("sp", lambda e, wb=wb, j=j: e.dma_start(
                out=wb.t[:], in_=w_ada[li, :, j * 512:(j + 1) * 512].rearrange("(kc p) n -> p kc n", p=128)),
                writes=[wb.r], dma=True, sem_res=wb.r)
            for src in range(2):
                p = ps[k % 2]
                st = stage[k % 2]
                for kc in range(8):
                    S.add("pe", lambda e, p=p, wb=wb, kc=kc, src=src: e.matmul(
                        p.t[:], silb.t[:, src * 8 + kc, :], wb.t[:, kc, :], start=(kc == 0), stop=False),
                        reads=[silb.r, wb.r], writes=[p.r])
                S.add("pe", lambda e, p=p, j=j: e.matmul(
                    p.t[:], ones.t[0:1, :], brow.t[0:1, j * 512:(j + 1) * 512], start=False, stop=True),
                    reads=[ones.r, brow.r], writes=[p.r])
                addone = 1.0 if j in (2, 3, 8, 9) else 0.0
                S.add("dve", lambda e, p=p, st=st, addone=addone: e.tensor_scalar(
                    out=st.t[:], in0=p.t[:], scalar1=addone, scalar2=None, op0=ALU.add),
                    reads=[p.r], writes=[st.r])
                S.add("sp", lambda e, st=st, src=src, j=j: e.dma_start(
                    out=MOD.t[src, :, j * 512:(j + 1) * 512], in_=st.t[:]),
                    reads=[st.r], writes=[MOD.r], dma=True, sem_res=st.r)
                k += 1
        K.end_phase("mod%d" % li)


def load_weight_bf16(S, wdst, w_src_kcn, nkc, split=4):
    res = []
    for kc in range(nkc):
        wb = Buf(None, "w")
        S.add("pool", lambda e, kc=kc: e.dma_start(out=wdst.t[:, kc, :], in_=w_src_kcn[kc * 128:(kc + 1) * 128, :]),
              writes=[wb.r], dma=True, sem_res=wb.r)
        res.append(wb)
    return res


PT_Z, PT_DT, PT_DQ, PT_DK, PT_DV, PT_RQ, PT_RK, PT_RV, PT_RG, PT_W = 0, 1024, 1056, 1568, 2080, 2592, 2848, 3104, 3616, 4128
TOK_GROUPS = [
    (PT_Z, 0, 512), (PT_Z + 512, 512, 512), (PT_DT, 2560, 32),
    (PT_DQ, 2592, 512), (PT_DK, 3104, 512), (PT_DV, 3616, 512),
    (PT_RQ, 4128, 512), (PT_RV, 4640, 512), (PT_RG, 5152, 512),
]


def phase_proj(K, name, X, MOD, sc_col, sh_col, w_src, ncols, tok_groups, PTOK, fm_col0, n_fm, FMB, consts):
    nc = K.nc
    S = K.new_phase()
    NCH = K.NCH
    with ExitStack() as es:
        W = _sb(es, nc, "ip_w", [128, 8, ncols], BF16)
        identb = _sb(es, nc, "ip_id", [128, 128], BF16)
        modt = [[_sb(es, nc, "ip_mod%d%d" % (s_, q), [128, 1024], F32) for q in range(2)] for s_ in range(2)]
        xt = [_sb(es, nc, "ip_x%d" % i, [128, 1024], F32) for i in range(2)]
        xmb = [_sb(es, nc, "ip_xb%d" % i, [128, 1024], BF16) for i in range(2)]
        xT4 = [_sb(es, nc, "ip_xT%d" % i, [128, 8, 512], BF16) for i in range(2)]
        xslot = [[Buf(xT4[i].t, "slot") for j in range(4)] for i in range(2)]
        stg = [_sb(es, nc, "ip_st%d" % i, [128, PT_W if tok_groups else 8], F32) for i in range(2)]
        stf = [_sb(es, nc, "ip_sf%d" % i, [128, 4, 512], F32) for i in range(3)]
        pst = [_ps(es, nc, "ip_pt%d" % i, [128, 8, 128], BF16) for i in range(2)]
        psg = [_ps(es, nc, "ip_pg%d" % i, [128, 512], F32) for i in range(4 if tok_groups else 1)]
        psf = [_ps(es, nc, "ip_pf%d" % i, [128, 512], F32) for i in range(2 if tok_groups else 4)]
        wres = load_weight_bf16(S, W, w_src, 8)
        K.dma("sp", identb.t[:], consts["identb"], [], [identb], identb)
        for s_ in range(2):
            for q in range(2):
                c0 = sc_col if q == 0 else sh_col
                K.dma("sp", modt[s_][q].t[:], MOD.t[s_, :, c0:c0 + 1024], [], [modt[s_][q]], modt[s_][q])

        def loads(c):
            K.dma("sp", xt[c % 2].t[:], X.t[c * 128:(c + 1) * 128, :], [], [xt[c % 2]], xt[c % 2])

        gi = 0
        fi = 0
        ei = 0
        loads(0)
        blocks = [list(range(i, min(i + 4, NCH))) for i in range(0, NCH, 4)]
        for bi, blk in enumerate(blocks):
            xb = xT4[bi % 2]
            slots = xslot[bi % 2]
            for j, c in enumerate(blk):
                if c + 1 < NCH:
                    loads(c + 1)
                b = c % 2
                src = 1 if c < NCTX // 128 else 0
                sl = slots[j]
                K.op("dve", "tensor_tensor", [xt[b], modt[src][0]], [xt[b]], out=xt[b].t[:], in0=xt[b].t[:], in1=modt[src][0].t[:], op=ALU.mult)
                K.op("pool", "tensor_tensor", [xt[b], modt[src][1]], [xmb[b]], out=xmb[b].t[:], in0=xt[b].t[:], in1=modt[src][1].t[:], op=ALU.add)
                for kc in range(8):
                    K.op("pe", "transpose", [xmb[b], identb], [pst[b]], pst[b].t[:, kc, :], xmb[b].t[:, kc * 128:(kc + 1) * 128], identb.t[:])
                K.op("act", "activation", [pst[b]], [sl], out=xb.t[:, :, j * 128:(j + 1) * 128], in_=pst[b].t[:], func=AF.Copy)
                for (dc, sc, wd) in tok_groups:
                    p = psg[gi % len(psg)]
                    for kc in range(8):
                        K.op("pe", "matmul", [sl] + wres, [p], p.t[:, 0:wd], xb.t[:, kc, j * 128:(j + 1) * 128], W.t[:, kc, sc:sc + wd],
                             start=(kc == 0), stop=(kc == 7))
                    if gi % 2 == 0:
                        K.op("act", "activation", [p], [stg[b]], out=stg[b].t[:, dc:dc + wd], in_=p.t[:, 0:wd], func=AF.Copy)
                    else:
                        K.op("dve", "tensor_copy", [p], [stg[b]], out=stg[b].t[:, dc:dc + wd], in_=p.t[:, 0:wd])
                    gi += 1
                if tok_groups:
                    K.dma("sp", PTOK.t[c * 128:(c + 1) * 128, :], stg[b].t[:], [stg[b]], [], stg[b])
            BW = len(blk) * 128
            for ct in range(n_fm):
                p = psf[fi % len(psf)]
                fi += 1
                col = fm_col0 + ct * 128
                for kc in range(8):
                    K.op("pe", "matmul", slots[0:len(blk)] + wres, [p], p.t[:, 0:BW], W.t[:, kc, col:col + 128], xb.t[:, kc, 0:BW],
                         start=(kc == 0), stop=(kc == 7))
                sb = stf[(ct // 4 + bi * ((n_fm + 3) // 4)) % 3]
                if ei % 2 == 0:
                    K.op("dve", "tensor_copy", [p], [sb], out=sb.t[:, ct % 4, 0:BW], in_=p.t[:, 0:BW])
                else:
                    K.op("act", "activation", [p], [sb], out=sb.t[:, ct % 4, 0:BW], in_=p.t[:, 0:BW], func=AF.Copy)
                ei += 1
                if ct % 4 == 3:
                    for j, c in enumerate(blk):
                        K.dma("sp", FMB.t[c, :, ct - 3:ct + 1, :], sb.t[:, :, j * 128:(j + 1) * 128], [sb], [], sb)
        K.end_phase(name)


def bc(ap, shape):
    return ap.broadcast_to(list(shape))


def phase_ssd_prep(K, li, PTOK, XBCT, P, XSB, BCT, DTLA, consts):
    nc = K.nc
    S = K.new_phase()
    NCH = K.NCH
    with ExitStack() as es:
        identb = _sb(es, nc, "sp_id", [128, 128], BF16)
        cw = _sb(es, nc, "sp_cw", [128, 12, 5], F32)
        cb = _sb(es, nc, "sp_cb", [128, 12], F32)
        dtb = _sb(es, nc, "sp_dtb", [128, 32], F32)
        alog = _sb(es, nc, "sp_alog", [128, 32], F32)
        aneg = _sb(es, nc, "sp_aneg", [128, 32], F32)
        xh = [_sb(es, nc, "sp_xh%d" % i, [128, 12, 132], F32) for i in range(3)]
        acc = [_sb(es, nc, "sp_acc%d" % i, [128, 12, 128], F32) for i in range(2)]
        accsl = [[Buf(acc[i].t, "accsl") for ct in range(12)] for i in range(2)]
        xc = [_sb(es, nc, "sp_xc%d" % i, [128, 12, 128], BF16) for i in range(2)]
        tok = [_sb(es, nc, "sp_tok%d" % i, [128, 1280], BF16) for i in range(2)]
        dtr = [_sb(es, nc, "sp_dtr%d" % i, [128, 32], F32) for i in range(2)]
        dl = [_sb(es, nc, "sp_dl%d" % i, [128, 64], F32) for i in range(2)]
        pt = [_ps(es, nc, "sp_pt%d" % i, [128, 8, 128], BF16) for i in range(2)]
        pb = [_ps(es, nc, "sp_pb%d" % i, [128, 2, 128], BF16) for i in range(2)]
        K.dma("sp", identb.t[:], consts["identb"], [], [identb], identb)
        K.dma("sp", cw.t[:], P["ssd_cw"][li], [], [cw], cw)
        K.dma("sp", cb.t[:], P["ssd_cb"][li], [], [cb], cb)
        K.dma("sp", dtb.t[:], P["ssd_dtb"][li], [], [dtb], dtb)
        K.dma("sp", alog.t[:], P["ssd_alog"][li], [], [alog], alog)
        K.op("act", "activation", [alog], [aneg], out=aneg.t[:], in_=alog.t[:], func=AF.Exp)
        K.op("dve", "tensor_scalar", [aneg], [aneg], out=aneg.t[:], in0=aneg.t[:], scalar1=-1.0, scalar2=None, op0=ALU.mult)
        def loads(c):
            K.dma("sp", xh[c % 3].t[:, :, 2:130], XBCT.t[c], [], [xh[c % 3]], xh[c % 3])
            K.dma("sp", dtr[c % 2].t[:], PTOK.t[c * 128:(c + 1) * 128, PT_DT:PT_DT + 32], [], [dtr[c % 2]], dtr[c % 2])

        def stA(c):
            b = c % 2
            xb_ = xh[c % 3]
            lv = c not in (0, 2)
            rv = c not in (1, NCH - 1)
            if lv:
                K.op("pool", "tensor_copy", [xh[(c - 1) % 3]], [xb_], out=xb_.t[:, :, 0:2], in_=xh[(c - 1) % 3].t[:, :, 128:130])
            else:
                K.op("pool", "memset", [], [xb_], xb_.t[:, :, 0:2], 0.0)
            if rv:
                K.op("pool", "tensor_copy", [xh[(c + 1) % 3]], [xb_], out=xb_.t[:, :, 130:132], in_=xh[(c + 1) % 3].t[:, :, 2:4])
            else:
                K.op("pool", "memset", [], [xb_], xb_.t[:, :, 130:132], 0.0)
            asl = accsl[b]
            for ct in range(12):
                K.op("dve", "tensor_scalar", [xb_, cw, cb], [asl[ct]], out=acc[b].t[:, ct, :], in0=xb_.t[:, ct, 0:128],
                     scalar1=cw.t[:, ct, 0:1], scalar2=cb.t[:, ct:ct + 1], op0=ALU.mult, op1=ALU.add)
            for j in range(1, 5):
                for ct in range(12):
                    K.op("dve", "scalar_tensor_tensor", [xb_, cw, asl[ct]], [asl[ct]], out=acc[b].t[:, ct, :], in0=xb_.t[:, ct, j:j + 128],
                         scalar=cw.t[:, ct, j:j + 1], in1=acc[b].t[:, ct, :], op0=ALU.mult, op1=ALU.add)
            K.op("act", "activation", asl, [xc[b]], out=xc[b].t[:], in_=acc[b].t[:], func=AF.Silu)
            K.op("dve", "tensor_tensor", [dtr[b], dtb], [dtr[b]], out=dtr[b].t[:], in0=dtr[b].t[:], in1=dtb.t[:], op=ALU.add)
            K.op("act", "activation", [dtr[b]], [dtr[b]], out=dtr[b].t[:], in_=dtr[b].t[:], func=AF.Exp)
            K.op("act", "activation", [dtr[b]], [dl[b]], out=dl[b].t[:, 0:32], in_=dtr[b].t[:], func=AF.Ln, bias=1.0)
            K.op("dve", "tensor_tensor", [dl[b], aneg], [dl[b]], out=dl[b].t[:, 32:64], in0=dl[b].t[:, 0:32], in1=aneg.t[:], op=ALU.mult)
            K.dma("sp", DTLA.t[c * 128:(c + 1) * 128, :], dl[b].t[:], [dl[b]], [], dl[b])

        def stB(c):
            b = c % 2
            for ct in range(8):
                K.op("pe", "transpose", [xc[b], identb], [pt[b]], pt[b].t[:, ct, :], xc[b].t[:, ct, :], identb.t[:])
            for ct in range(2):
                K.op("pe", "transpose", [xc[b], identb], [pb[b]], pb[b].t[:, ct, :], xc[b].t[:, 8 + ct, :], identb.t[:])
            K.op("act", "activation", [pt[b]], [tok[b]], out=tok[b].t[:, 0:1024], in_=pt[b].t[:].rearrange("p a b -> p (a b)"), func=AF.Copy)
            K.op("dve", "tensor_copy", [pb[b]], [tok[b]], out=tok[b].t[:, 1024:1280], in_=pb[b].t[:].rearrange("p a b -> p (a b)"))
            K.dma("sp", XSB.t[c * 128:(c + 1) * 128, :], tok[b].t[:], [tok[b]], [], tok[b])
            K.dma("sp", BCT.t[:, c * 128:(c + 1) * 128].rearrange("(ct p) t -> p ct t", p=128), xc[b].t[:, 8:12, :], [xc[b]], [], xc[b])

        loads(0)
        if NCH > 1:
            loads(1)
        stA(0)
        for c in range(NCH):
            stB(c)
            if c + 2 < NCH:
                loads(c + 2)
            if c + 1 < NCH:
                stA(c + 1)
        K.end_phase("ssdprep%d" % li)


def rope_tok(K, x, tabs, o, t1, nmap, half, cs_off, sn_off):
    raise NotImplementedError


def phase_ret_prep(K, li, PTOK, P, RQKT, RTOK, consts):
    nc = K.nc
    S = K.new_phase()
    NCH = K.NCH
    with ExitStack() as es:
        identb = _sb(es, nc, "rp_id", [128, 128], BF16)
        qk = [_sb(es, nc, "rp_qk%d" % i, [128, 512], F32) for i in range(2)]
        v = [_sb(es, nc, "rp_v%d" % i, [128, 512], F32) for i in range(2)]
        tab = [_sb(es, nc, "rp_tab%d" % i, [128, 128], F32) for i in range(2)]
        t1 = [_sb(es, nc, "rp_t1%d" % i, [128, 512], F32) for i in range(2)]
        o = [_sb(es, nc, "rp_o%d" % i, [128, 512], F32) for i in range(2)]
        ob = [_sb(es, nc, "rp_ob%d" % i, [128, 512], BF16) for i in range(2)]
        tk = [_sb(es, nc, "rp_tk%d" % i, [128, 768], BF16) for i in range(2)]
        qT = [_sb(es, nc, "rp_qT%d" % i, [64, 8, 128], BF16) for i in range(2)]
        pt = [_ps(es, nc, "rp_pt%d" % i, [64, 8, 128], BF16) for i in range(2)]
        K.dma("sp", identb.t[:], consts["identb"], [], [identb], identb)
        def loads(c):
            b = c % 2
            K.dma("sp", qk[b].t[:], PTOK.t[c * 128:(c + 1) * 128, PT_RQ:PT_RQ + 512], [], [qk[b]], qk[b])
            K.dma("sp", v[b].t[:], PTOK.t[c * 128:(c + 1) * 128, PT_RV:PT_RV + 512], [], [v[b]], v[b])
            K.dma("sp", tab[b].t[:], consts["ret_tab"][c * 128:(c + 1) * 128, :], [], [tab[b]], tab[b])

        loads(0)
        for c in range(NCH):
            b = c % 2
            if c + 1 < NCH:
                loads(c + 1)
            xv = qk[b].t[:].rearrange("p (m two h) -> p m two h", two=2, h=32)
            t1v = t1[b].t[:].rearrange("p (m two h) -> p m two h", two=2, h=32)
            sn = tab[b].t[:, 64:128].rearrange("p (two h) -> p two h", two=2)
            K.op("pool", "tensor_tensor", [qk[b], tab[b]], [t1[b]], out=t1v[:, :, 0, :], in0=xv[:, :, 1, :],
                 in1=bc(sn[:, 0:1, :], [128, 8, 32]), op=ALU.mult)
            K.op("pool", "tensor_tensor", [qk[b], tab[b]], [t1[b]], out=t1v[:, :, 1, :], in0=xv[:, :, 0, :],
                 in1=bc(sn[:, 1:2, :], [128, 8, 32]), op=ALU.mult)
            K.op("dve", "tensor_tensor", [qk[b], tab[b]], [o[b]], out=o[b].t[:].rearrange("p (m d) -> p m d", d=64),
                 in0=qk[b].t[:].rearrange("p (m d) -> p m d", d=64), in1=bc(tab[b].t[:, 0:64].unsqueeze(1), [128, 8, 64]), op=ALU.mult)
            K.op("dve", "tensor_tensor", [o[b], t1[b]], [ob[b]], out=ob[b].t[:, 0:256], in0=o[b].t[:, 0:256], in1=t1[b].t[:, 0:256], op=ALU.add)
            K.op("dve", "tensor_tensor", [o[b], t1[b]], [o[b]], out=o[b].t[:, 256:512], in0=o[b].t[:, 256:512], in1=t1[b].t[:, 256:512], op=ALU.add)
            K.op("dve", "tensor_scalar", [o[b]], [ob[b]], out=ob[b].t[:, 256:512], in0=o[b].t[:, 256:512], scalar1=0.125, scalar2=None, op0=ALU.mult)
            for h in range(8):
                K.op("pe", "transpose", [ob[b], identb], [pt[b]], pt[b].t[:, h, :], ob[b].t[:, h * 64:(h + 1) * 64], identb.t[:])
            K.op("act", "activation", [pt[b]], [qT[b]], out=qT[b].t[:], in_=pt[b].t[:], func=AF.Copy)
            K.dma("sp", RQKT.t[:, :, c * 128:(c + 1) * 128], qT[b].t[:], [qT[b]], [], qT[b])
            K.op("act", "activation", [v[b]], [tk[b]], out=tk[b].t[:, 0:512], in_=v[b].t[:], func=AF.Copy)
            K.op("pool", "tensor_copy", [ob[b]], [tk[b]], out=tk[b].t[:, 512:768], in_=ob[b].t[:, 256:512])
            K.dma("sp", RTOK.t[c * 128:(c + 1) * 128, :], tk[b].t[:], [tk[b]], [], tk[b])
        K.end_phase("retprep%d" % li)


class Fam:
    pass


def fam_ssd():
    f = Fam()
    f.name = "ssd"; f.H = 16; f.G = 2; f.N = 128; f.P = 64; f.J = 8; f.units = [[0], [1]]; f.YW = 1024
    return f


def fam_ret():
    f = Fam()
    f.name = "ret"; f.H = 4; f.G = 4; f.N = 64; f.P = 128; f.J = 4; f.units = [[0, 1, 2, 3]]; f.YW = 512
    return f


def phase_scan(K, li, fam, d, srcs, P, YS, consts):
    nc = K.nc
    S = K.new_phase()
    NCH = K.NCH
    ssd = fam.name == "ssd"
    H, G, N, PP, J = fam.H, fam.G, fam.N, fam.P, fam.J
    NU = len(fam.units)
    GU = len(fam.units[0])
    order = list(range(NCH)) if d == 0 else [1, 0] + list(range(NCH - 1, 1, -1))
    endcol = 127 if d == 0 else 0
    with ExitStack() as es:
        tri = _sb(es, nc, "sc_tri", [128, 128], F32)
        stri = _sb(es, nc, "sc_stri", [128, 128], F32)
        mask = _sb(es, nc, "sc_mask", [128, 128], F32)
        ones = _sb(es, nc, "sc_ones", [128, 128], F32)
        K.dma("sp", tri.t[:], consts["tri"][d], [], [tri], tri)
        K.dma("sp", stri.t[:], consts["stri"][d], [], [stri], stri)
        K.dma("sp", mask.t[:], consts["mask"][d], [], [mask], mask)
        K.dma("sp", ones.t[:], consts["ones"], [], [ones], ones)
        qkT = [_sb(es, nc, "sc_qkT%d" % i, [N, 2 * G, 128], BF16) for i in range(3)]
        tokw = 1280 if ssd else 768
        tok = [_sb(es, nc, "sc_tok%d" % i, [128, tokw], BF16) for i in range(3)]
        la = [_sb(es, nc, "sc_la%d" % i, [128, 64], F32) for i in range(3)]
        ecum = [_sb(es, nc, "sc_ecum%d" % i, [128, 2 * H], F32) for i in range(2)]
        sm = [[_sb(es, nc, "sc_sm%d_%d" % (i, u_), [128, GU, 128], F32) for u_ in range(NU)] for i in range(2)]
        rc = [_sb(es, nc, "sc_rc%d" % i, [128, J, 128], F32) for i in range(2)]
        E = [[_sb(es, nc, "sc_E%d_%d" % (i, u_), [128, J, 128], F32) for u_ in range(NU)] for i in range(2)]
        M = [[_sb(es, nc, "sc_M%d_%d" % (i, u_), [128, J, 128], BF16) for u_ in range(NU)] for i in range(2)]
        vd = [[_sb(es, nc, "sc_vd%d_%d" % (i, u_), [128, 512], BF16) for u_ in range(NU)] for i in range(2)]
        vs = [[_sb(es, nc, "sc_vs%d_%d" % (i, u_), [128, 512], BF16) for u_ in range(NU)] for i in range(2)]
        Y = [_sb(es, nc, "sc_Y%d" % i, [128, fam.YW], F32) for i in range(2)]
        Yp = [_sb(es, nc, "sc_Yp%d" % i, [128, fam.YW], F32) for i in range(3)]
        St = [_sb(es, nc, "sc_S%d" % u, [N, 512], F32) for u in range(NU)]
        Sb = [_sb(es, nc, "sc_Sb%d" % u, [N, 512], BF16) for u in range(NU)]
        lac = _sb(es, nc, "sc_lac", [128, 8], F32)
        p_sc = _ps(es, nc, "sc_psc", [128, GU, 128], F32)
        p_seg = _ps(es, nc, "sc_pseg", [128, J, 128], F32)
        p_cum = _ps(es, nc, "sc_pcum", [128, 2 * H], F32)
        p_yd = _ps(es, nc, "sc_pyd", [128, 512], F32)
        p_yo = _ps(es, nc, "sc_pyo", [128, 512], F32)
        p_st = _ps(es, nc, "sc_pst", [N, 512], F32)
        for u in range(NU):
            K.op("dve", "memset", [], [St[u]], St[u].t[:], 0.0)
            K.op("pool", "memset", [], [Sb[u]], Sb[u].t[:], 0.0)
        if not ssd:
            K.dma("sp", lac.t[:], P["ret_dec"][li], [], [lac], lac)
            K.op("act", "activation", [lac], [lac], out=lac.t[:], in_=lac.t[:], func=AF.Exp)
            K.op("dve", "tensor_scalar", [lac], [lac], out=lac.t[:], in0=lac.t[:], scalar1=-1.0, scalar2=None, op0=ALU.mult)
        def loads(ci):
            c = order[ci]
            b3 = ci % 3
            if ssd:
                K.dma("sp", qkT[b3].t[:], srcs["BCT"].t[:, c * 128:(c + 1) * 128].rearrange("(ct p) t -> p ct t", p=128), [], [qkT[b3]], qkT[b3])
                K.dma("sp", tok[b3].t[:], srcs["XSB"].t[c * 128:(c + 1) * 128, :], [], [tok[b3]], tok[b3])
                K.dma("sp", la[b3].t[:], srcs["DTLA"].t[c * 128:(c + 1) * 128, :], [], [la[b3]], la[b3])
            else:
                K.dma("sp", qkT[b3].t[:, 0:4, :], srcs["RQKT"].t[:, 4:8, c * 128:(c + 1) * 128], [], [qkT[b3]], qkT[b3])
                K.dma("sp", qkT[b3].t[:, 4:8, :], srcs["RQKT"].t[:, 0:4, c * 128:(c + 1) * 128], [], [qkT[b3]], qkT[b3])
                K.dma("sp", tok[b3].t[:], srcs["RTOK"].t[c * 128:(c + 1) * 128, :], [], [tok[b3]], tok[b3])
            if d == 1:
                K.dma("sp", Yp[b3].t[:], YS.t[c * 128:(c + 1) * 128, :], [], [Yp[b3]], Yp[b3])

        def ctxvars(ci):
            c = order[ci]
            b = ci % 2
            b3 = ci % 3
            if ssd:
                la_ap = la[b3].t[:, 32 + d * 16:32 + d * 16 + 16]
                dt_ap = la[b3].t[:, d * 16:d * 16 + 16]
                la_res = la[b3]
                kT = lambda g: qkT[b3].t[:, g, :]
                qT = lambda g: qkT[b3].t[:, 2 + g, :]
                ktok = lambda g: tok[b3].t[:, 1024 + g * 128:1024 + (g + 1) * 128]
            else:
                la_ap = lac.t[:, d * 4:d * 4 + 4]
                la_res = lac
                kT = lambda g: qkT[b3].t[:, g, :]
                qT = lambda g: qkT[b3].t[:, 4 + g, :]
                ktok = lambda g: tok[b3].t[:, 512 + g * 64:512 + (g + 1) * 64]
            return c, b, b3, la_ap, (dt_ap if ssd else None), la_res, kT, qT, ktok

        def stage1(ci):
            c, b, b3, la_ap, dt_ap, la_res, kT, qT, ktok = ctxvars(ci)
            K.op("pe", "matmul", [tri, la_res], [p_cum], p_cum.t[:, 0:H], tri.t[:], la_ap, start=True, stop=True)
            K.op("pe", "matmul", [ones, la_res], [p_cum], p_cum.t[:, H:2 * H], ones.t[:], la_ap, start=True, stop=True)
            K.op("act", "activation", [p_cum], [ecum[b]], out=ecum[b].t[:], in_=p_cum.t[:], func=AF.Exp)
            for u, groups in enumerate(fam.units):
                h0 = u * J
                for gi, g in enumerate(groups):
                    K.op("pe", "matmul", [qkT[b3]], [p_sc], p_sc.t[:, gi, :], kT(g), qT(g), start=True, stop=True)
                K.op("dve", "tensor_tensor", [p_sc, mask], [sm[b][u]], out=sm[b][u].t[:], in0=p_sc.t[:],
                     in1=bc(mask.t[:].unsqueeze(1), [128, GU, 128]), op=ALU.mult)
                K.op("dve", "tensor_tensor", [la_res, tri], [rc[b]], out=rc[b].t[:],
                     in0=bc(la_ap[:, h0:h0 + J].unsqueeze(2), [128, J, 128]),
                     in1=bc(tri.t[:].unsqueeze(1), [128, J, 128]), op=ALU.mult)
                for q4 in range(J // 4):
                    K.op("pe", "matmul", [stri, rc[b]], [p_seg], p_seg.t[:, q4 * 4:(q4 + 1) * 4, :], stri.t[:], rc[b].t[:, q4 * 4:(q4 + 1) * 4, :],
                         start=True, stop=True)
                K.op("act", "activation", [p_seg], [E[b][u]], out=E[b][u].t[:], in_=p_seg.t[:], func=AF.Exp)
                smv = bc(sm[b][u].t[:], [128, J, 128]) if GU == 1 else sm[b][u].t[:]
                K.op("dve", "tensor_tensor", [E[b][u], sm[b][u]], [M[b][u]], out=M[b][u].t[:], in0=E[b][u].t[:], in1=smv, op=ALU.mult)
                if ssd:
                    K.op("pool", "tensor_tensor", [tok[b3], la[b3]], [vd[b][u]], out=vd[b][u].t[:].rearrange("p (j q) -> p j q", q=PP),
                         in0=tok[b3].t[:, u * 512:(u + 1) * 512].rearrange("p (j q) -> p j q", q=PP),
                         in1=bc(dt_ap[:, h0:h0 + J].unsqueeze(2), [128, J, PP]), op=ALU.mult)
                    vd_ap = vd[b][u].t[:]
                    vd_res = vd[b][u]
                else:
                    vd_ap = tok[b3].t[:, 0:512]
                    vd_res = tok[b3]
                K.op("dve", "tensor_tensor", [vd_res, E[b][u]], [vs[b][u]], out=vs[b][u].t[:].rearrange("p (j q) -> p j q", q=PP),
                     in0=vd_ap.rearrange("p (j q) -> p j q", q=PP),
                     in1=bc(E[b][u].t[:, :, endcol:endcol + 1], [128, J, PP]), op=ALU.mult)

        def stage2(ci):
            c, b, b3, la_ap, dt_ap, la_res, kT, qT, ktok = ctxvars(ci)
            for u, groups in enumerate(fam.units):
                h0 = u * J
                if ssd:
                    vd_ap = vd[b][u].t[:]
                    vd_res = vd[b][u]
                else:
                    vd_ap = tok[b3].t[:, 0:512]
                    vd_res = tok[b3]
                for j in range(J):
                    K.op("pe", "matmul", [M[b][u], vd_res], [p_yd], p_yd.t[:, j * PP:(j + 1) * PP], M[b][u].t[:, j, :], vd_ap[:, j * PP:(j + 1) * PP],
                         start=True, stop=True)
                gw = 512 // GU
                for gi, g in enumerate(groups):
                    K.op("pe", "matmul", [qkT[b3], Sb[u]], [p_yo], p_yo.t[:, gi * gw:(gi + 1) * gw], qT(g), Sb[u].t[:, gi * gw:(gi + 1) * gw],
                         start=True, stop=True)
                ysl = Y[b].t[:, u * 512:(u + 1) * 512]
                K.op("dve", "tensor_tensor", [p_yo, ecum[b]], [Y[b]], out=ysl.rearrange("p (j q) -> p j q", q=PP),
                     in0=p_yo.t[:].rearrange("p (j q) -> p j q", q=PP),
                     in1=bc(ecum[b].t[:, h0:h0 + J].unsqueeze(2), [128, J, PP]), op=ALU.mult)
                K.op("dve", "tensor_tensor", [p_yd, Y[b]], [Y[b]], out=ysl, in0=ysl, in1=p_yd.t[:], op=ALU.add)
                if d == 1:
                    K.op("pool", "tensor_tensor", [Yp[b3], Y[b]], [Y[b]], out=ysl, in0=ysl, in1=Yp[b3].t[:, u * 512:(u + 1) * 512], op=ALU.add)
                for gi, g in enumerate(groups):
                    K.op("pe", "matmul", [tok[b3], vs[b][u]], [p_st], p_st.t[:, gi * gw:(gi + 1) * gw], ktok(g), vs[b][u].t[:, gi * gw:(gi + 1) * gw],
                         start=True, stop=True)
                K.op("pool", "tensor_tensor", [St[u], ecum[b]], [St[u]], out=St[u].t[:].rearrange("p (j q) -> p j q", q=PP),
                     in0=St[u].t[:].rearrange("p (j q) -> p j q", q=PP),
                     in1=bc(ecum[b].t[0:N, H + h0:H + h0 + J].unsqueeze(2), [N, J, PP]), op=ALU.mult)
                K.op("dve", "tensor_tensor", [St[u], p_st], [St[u]], out=St[u].t[:], in0=St[u].t[:], in1=p_st.t[:], op=ALU.add)
                K.op("act", "activation", [St[u]], [Sb[u]], out=Sb[u].t[:], in_=St[u].t[:], func=AF.Copy)
            K.dma("sp", YS.t[c * 128:(c + 1) * 128, :], Y[b].t[:], [Y[b]], [], Y[b])

        loads(0)
        if len(order) > 1:
            loads(1)
        stage1(0)
        for ci in range(len(order)):
            if ci + 1 < len(order):
                stage1(ci + 1)
            if ci + 2 < len(order):
                loads(ci + 2)
            stage2(ci)
        K.end_phase("scan_%s%d_%d" % (fam.name, li, d))


def rope_axial(K, x, tab, t1, o, nm):
    xv = x.t[:].rearrange("p (m hf two e) -> p m hf two e", hf=2, two=2, e=16)
    tv = t1.t[:].rearrange("p (m hf two e) -> p m hf two e", hf=2, two=2, e=16)
    sn = tab.t[:, 64:128].rearrange("p (hf two e) -> p hf two e", hf=2, two=2)
    K.op("pool", "tensor_tensor", [x, tab], [t1], out=tv[:, :, :, 0, :], in0=xv[:, :, :, 1, :],
         in1=bc(sn[:, :, 0, :].unsqueeze(1), [128, nm, 2, 16]), op=ALU.mult)
    K.op("pool", "tensor_tensor", [x, tab], [t1], out=tv[:, :, :, 1, :], in0=xv[:, :, :, 0, :],
         in1=bc(sn[:, :, 1, :].unsqueeze(1), [128, nm, 2, 16]), op=ALU.mult)
    K.op("dve", "tensor_tensor", [x, tab], [o], out=o.t[:].rearrange("p (m d) -> p m d", d=64),
         in0=x.t[:].rearrange("p (m d) -> p m d", d=64), in1=bc(tab.t[:, 0:64].unsqueeze(1), [128, nm, 64]), op=ALU.mult)
    K.op("dve", "tensor_tensor", [o, t1], [o], out=o.t[:], in0=o.t[:], in1=t1.t[:], op=ALU.add)


def phase_attn_prep(K, li, PTOK, KT, VA, QT, KMAX, KM8, NB, consts):
    nc = K.nc
    NCH = K.NCH
    S = K.new_phase()
    with ExitStack() as es:
        identb = _sb(es, nc, "ap_id", [128, 128], BF16)
        identf = _sb(es, nc, "ap_idf", [128, 128], F32)
        ones = _sb(es, nc, "ap_ones", [128, 128], F32)
        x = [_sb(es, nc, "ap_x%d" % i, [128, 512], F32) for i in range(2)]
        v = [_sb(es, nc, "ap_v%d" % i, [128, 512], F32) for i in range(2)]
        tab = [_sb(es, nc, "ap_tab%d" % i, [128, 128], F32) for i in range(2)]
        t1 = [_sb(es, nc, "ap_t1%d" % i, [128, 512], F32) for i in range(2)]
        o = [_sb(es, nc, "ap_o%d" % i, [128, 512], F32) for i in range(2)]
        sq = [_sb(es, nc, "ap_sq%d" % i, [128, 512], F32) for i in range(2)]
        ks = [_sb(es, nc, "ap_ks%d" % i, [128, 8], F32) for i in range(2)]
        ka = [_sb(es, nc, "ap_ka%d" % i, [128, 512], BF16) for i in range(2)]
        va = [_sb(es, nc, "ap_va%d" % i, [128, 4, 129], BF16) for i in range(2)]
        kT = [_sb(es, nc, "ap_kT%d" % i, [128, 4, 128], BF16) for i in range(2)]
        kmx = _sb(es, nc, "ap_kmx", [128, 8], F32)
        km2 = _sb(es, nc, "ap_km2", [8, 1], F32)
        dg = _sb(es, nc, "ap_dg", [8, 8], F32)
        kbc = _sb(es, nc, "ap_kbc", [128, 8], F32)
        pt = [_ps(es, nc, "ap_pt%d" % i, [128, 4, 128], BF16) for i in range(2)]
        pk = _ps(es, nc, "ap_pk", [128, 128], F32)
        K.dma("sp", identb.t[:], consts["identb"], [], [identb], identb)
        K.dma("sp", identf.t[:], consts["identf"], [], [identf], identf)
        K.dma("sp", ones.t[:], consts["ones"], [], [ones], ones)
        K.op("dve", "memset", [], [kmx], kmx.t[:], 0.0)
        for i in range(2):
            K.op("pool", "memset", [], [va[i]], va[i].t[:], 1.0)
        def loads(c):
            b = c % 2
            K.dma("sp", x[b].t[:], PTOK.t[c * 128:(c + 1) * 128, PT_DK:PT_DK + 512], [], [x[b]], x[b])
            K.dma("sp", v[b].t[:], PTOK.t[c * 128:(c + 1) * 128, PT_DV:PT_DV + 512], [], [v[b]], v[b])
            K.dma("sp", tab[b].t[:], consts["diff_tab"][c * 128:(c + 1) * 128, :], [], [tab[b]], tab[b])

        loads(0)
        for c in range(NCH):
            b = c % 2
            if c + 1 < NCH:
                loads(c + 1)
            rope_axial(K, x[b], tab[b], t1[b], o[b], 8)
            K.op("act", "activation", [o[b]], [ka[b]], out=ka[b].t[:], in_=o[b].t[:], func=AF.Copy)
            K.op("pool", "tensor_tensor", [o[b]], [sq[b]], out=sq[b].t[:], in0=o[b].t[:], in1=o[b].t[:], op=ALU.mult)
            K.op("dve", "reduce_sum", [sq[b]], [ks[b]], out=ks[b].t[:], in_=sq[b].t[:].rearrange("p (m d) -> p m d", d=64), axis=AX.X)
            K.op("dve", "tensor_tensor", [ks[b], kmx], [kmx], out=kmx.t[:], in0=kmx.t[:], in1=ks[b].t[:], op=ALU.max)
            for m in range(4):
                K.op("pe", "transpose", [ka[b], identb], [pt[b]], pt[b].t[:, m, :], ka[b].t[:, m * 128:(m + 1) * 128], identb.t[:])
            K.op("act", "activation", [pt[b]], [kT[b]], out=kT[b].t[:], in_=pt[b].t[:], func=AF.Copy)
            K.dma("sp", KT.t[:, :, c * 128:(c + 1) * 128], kT[b].t[:], [kT[b]], [], kT[b])
            K.op("dve", "tensor_copy", [v[b]], [va[b]], out=va[b].t[:, :, 0:128], in_=v[b].t[:].rearrange("p (h d) -> p h d", d=128))
            K.dma("sp", VA.t[:, :, c, :].rearrange("h p w -> p h w"), va[b].t[:, :, 0:128], [va[b]], [], va[b])
        K.op("pe", "transpose", [kmx, identf], [pk], pk.t[0:8, :], kmx.t[:], identf.t[:])
        K.op("dve", "reduce_max", [pk], [km2], out=km2.t[:], in_=pk.t[0:8, :], axis=AX.X)
        K.op("act", "activation", [km2], [km2], out=km2.t[:], in_=km2.t[:], func=AF.Sqrt)
        K.op("dve", "tensor_scalar", [km2, identf], [dg], out=dg.t[:], in0=identf.t[0:8, 0:8], scalar1=km2.t[:, 0:1], scalar2=None, op0=ALU.mult)
        K.op("pe", "matmul", [ones, dg], [pk], pk.t[:, 0:8], ones.t[0:8, :], dg.t[:], start=True, stop=True)
        K.op("dve", "tensor_copy", [pk], [kbc], out=kbc.t[:], in_=pk.t[:, 0:8])
        K.dma("sp", KMAX.t[:], kbc.t[:], [kbc], [], kbc)
        K.dma("sp", KM8.t[:], km2.t[:], [km2], [], km2)
        K.end_phase("attnprep1_%d" % li)
    S = K.new_phase()
    with ExitStack() as es:
        identb = _sb(es, nc, "aq_id", [128, 128], BF16)
        kbc = _sb(es, nc, "aq_kbc", [128, 8], F32)
        x = [_sb(es, nc, "aq_x%d" % i, [128, 512], F32) for i in range(2)]
        tab = [_sb(es, nc, "aq_tab%d" % i, [128, 128], F32) for i in range(2)]
        t1 = [_sb(es, nc, "aq_t1%d" % i, [128, 512], F32) for i in range(2)]
        o = [_sb(es, nc, "aq_o%d" % i, [128, 512], F32) for i in range(2)]
        sq = [_sb(es, nc, "aq_sq%d" % i, [128, 512], F32) for i in range(2)]
        qs = [_sb(es, nc, "aq_qs%d" % i, [128, 8], F32) for i in range(2)]
        qa = [_sb(es, nc, "aq_qa%d" % i, [128, 512], BF16) for i in range(2)]
        qT = [_sb(es, nc, "aq_qT%d" % i, [128, 4, 128], BF16) for i in range(2)]
        pt = [_ps(es, nc, "aq_pt%d" % i, [128, 4, 128], BF16) for i in range(2)]
        K.dma("sp", identb.t[:], consts["identb"], [], [identb], identb)
        identf = _sb(es, nc, "aq_idf", [128, 128], F32)
        ones = _sb(es, nc, "aq_ones", [128, 128], F32)
        qmx = _sb(es, nc, "aq_qmx", [128, 8], F32)
        km8 = _sb(es, nc, "aq_km8", [8, 1], F32)
        qm8 = _sb(es, nc, "aq_qm8", [8, 1], F32)
        dg = _sb(es, nc, "aq_dg", [8, 8], F32)
        nbt = _sb(es, nc, "aq_nb", [128, 4], F32)
        pk = _ps(es, nc, "aq_pk", [128, 128], F32)
        K.dma("sp", identf.t[:], consts["identf"], [], [identf], identf)
        K.dma("sp", ones.t[:], consts["ones"], [], [ones], ones)
        K.dma("sp", km8.t[:], KM8.t[:], [], [km8], km8)
        K.op("dve", "memset", [], [qmx], qmx.t[:], 0.0)
        def loads(c):
            b = c % 2
            K.dma("sp", x[b].t[:], PTOK.t[c * 128:(c + 1) * 128, PT_DQ:PT_DQ + 512], [], [x[b]], x[b])
            K.dma("sp", tab[b].t[:], consts["diff_tab"][c * 128:(c + 1) * 128, :], [], [tab[b]], tab[b])

        loads(0)
        for c in range(NCH):
            b = c % 2
            if c + 1 < NCH:
                loads(c + 1)
            rope_axial(K, x[b], tab[b], t1[b], o[b], 8)
            K.op("act", "activation", [o[b]], [qa[b]], out=qa[b].t[:], in_=o[b].t[:], func=AF.Copy)
            K.op("pool", "tensor_tensor", [o[b]], [sq[b]], out=sq[b].t[:], in0=o[b].t[:], in1=o[b].t[:], op=ALU.mult)
            K.op("dve", "reduce_sum", [sq[b]], [qs[b]], out=qs[b].t[:], in_=sq[b].t[:].rearrange("p (m d) -> p m d", d=64), axis=AX.X)
            K.op("dve", "tensor_tensor", [qs[b], qmx], [qmx], out=qmx.t[:], in0=qmx.t[:], in1=qs[b].t[:], op=ALU.max)
            for m in range(4):
                K.op("pe", "transpose", [qa[b], identb], [pt[b]], pt[b].t[:, m, :], qa[b].t[:, m * 128:(m + 1) * 128], identb.t[:])
            K.op("act", "activation", [pt[b]], [qT[b]], out=qT[b].t[:], in_=pt[b].t[:], func=AF.Copy)
            K.dma("sp", QT.t[:, :, c * 128:(c + 1) * 128], qT[b].t[:], [qT[b]], [], qT[b])
        K.op("pe", "transpose", [qmx, identf], [pk], pk.t[0:8, :], qmx.t[:], identf.t[:])
        K.op("dve", "reduce_max", [pk], [qm8], out=qm8.t[:], in_=pk.t[0:8, :], axis=AX.X)
        K.op("act", "activation", [qm8], [qm8], out=qm8.t[:], in_=qm8.t[:], func=AF.Sqrt)
        K.op("dve", "tensor_tensor", [qm8, km8], [qm8], out=qm8.t[:], in0=qm8.t[:], in1=km8.t[:], op=ALU.mult)
        K.op("dve", "tensor_scalar", [qm8, identf], [dg], out=dg.t[:], in0=identf.t[0:8, 0:8], scalar1=qm8.t[:, 0:1], scalar2=None, op0=ALU.mult)
        K.op("pe", "matmul", [ones, dg], [pk], pk.t[:, 0:8], ones.t[0:8, :], dg.t[:], start=True, stop=True)
        K.op("dve", "tensor_copy", [pk], [kbc], out=kbc.t[:], in_=pk.t[:, 0:8])
        pkv = kbc.t[:].rearrange("p (h m) -> p h m", m=2)
        K.op("dve", "tensor_tensor", [kbc], [nbt], out=nbt.t[:], in0=pkv[:, :, 0], in1=pkv[:, :, 1], op=ALU.max)
        K.op("dve", "tensor_scalar", [nbt], [nbt], out=nbt.t[:], in0=nbt.t[:], scalar1=-0.125, scalar2=None, op0=ALU.mult)
        K.dma("sp", NB.t[:], nbt.t[:], [nbt], [], nbt)
        K.end_phase("attnprep2_%d" % li)


def phase_attn(K, li, KT, VA, QT, NB, P, OD, consts):
    nc = K.nc
    NCH = K.NCH
    T = K.T
    S = K.new_phase()
    lam_init = 0.8 - 0.6 * math.exp(-0.3 * li)
    with ExitStack() as es:
        dl = _sb(es, nc, "at_dl", [128, 256], F32)
        dp = _sb(es, nc, "at_dp", [128, 128], F32)
        ds = _sb(es, nc, "at_ds", [128, 2], F32)
        nlam = _sb(es, nc, "at_nlam", [128, 1], F32)
        identf = _sb(es, nc, "at_idf", [128, 128], F32)
        ones = _sb(es, nc, "at_ones", [128, 128], F32)
        kt = [_sb(es, nc, "at_kt%d" % i, [128, T], BF16) for i in range(2)]
        nb = _sb(es, nc, "at_nb", [128, 4], F32)
        vs = [_sb(es, nc, "at_vs%d" % i, [128, NCH, 128], BF16) for i in range(2)]
        qt = [_sb(es, nc, "at_qt%d" % i, [128, 512], BF16) for i in range(2)]
        pT = [_sb(es, nc, "at_pT%d" % i, [128, 2, 512], BF16) for i in range(3)]
        lacc = [_sb(es, nc, "at_la%d" % i, [128, 2, 512], F32) for i in range(2)]
        rl = [_sb(es, nc, "at_rl%d" % i, [1, 2, 512], F32) for i in range(2)]
        bcs = [_sb(es, nc, "at_bc%d" % i, [128, 2, 512], F32) for i in range(2)]
        t0 = [_sb(es, nc, "at_t0%d" % i, [128, 512], F32) for i in range(2)]
        t1 = [_sb(es, nc, "at_t1%d" % i, [128, 512], F32) for i in range(2)]
        od = [_sb(es, nc, "at_od%d" % i, [128, 4, 128], F32) for i in range(2)]
        p_s = [_ps(es, nc, "at_ps%d" % i, [128, 2, 512], F32) for i in range(2)]
        p_o = [[_ps(es, nc, "at_po%d%d" % (i, m), [128, 512], F32) for m in range(2)] for i in range(2)]
        K.dma("sp", dl.t[:], P["diff_lam"][li], [], [dl], dl)
        K.dma("sp", nb.t[:], NB.t[:], [], [nb], nb)
        K.dma("sp", identf.t[:], consts["identf"], [], [identf], identf)
        K.dma("sp", ones.t[:], consts["ones"], [], [ones], ones)
        dlv = dl.t[:].rearrange("p (a two d) -> p a two d", a=2, two=2)
        K.op("dve", "tensor_tensor", [dl], [dp], out=dp.t[:].rearrange("p (a d) -> p a d", a=2), in0=dlv[:, :, 0, :], in1=dlv[:, :, 1, :], op=ALU.mult)
        K.op("dve", "reduce_sum", [dp], [ds], out=ds.t[:], in_=dp.t[:].rearrange("p (a d) -> p a d", a=2), axis=AX.X)
        K.op("act", "activation", [ds], [ds], out=ds.t[:], in_=ds.t[:], func=AF.Exp)
        K.op("dve", "scalar_tensor_tensor", [ds], [nlam], out=nlam.t[:], in0=ds.t[:, 1:2], scalar=-lam_init, in1=ds.t[:, 0:1],
             op0=ALU.add, op1=ALU.subtract)
        qtiles = [(0, NCTX, 0, NCTX // 128)] + [(q0, 512, 0, NCH) for q0 in range(NCTX, T, 512)]
        si = 0
        pi = 0
        ti = 0
        pending = []
        for h in range(4):
            hb = h % 2
            K.dma("sp", kt[hb].t[:], KT.t[:, h, :], [], [kt[hb]], kt[hb])
            K.dma("sp", vs[hb].t[:], VA.t[h], [], [vs[hb]], vs[hb])
            for (q0, QW, kc0, kc1) in qtiles:
                tb = ti % 2
                ti += 1
                nqb = QW // 128
                K.dma("sp", qt[tb].t[:, 0:QW], QT.t[:, h, q0:q0 + QW], [], [qt[tb]], qt[tb])
                po = p_o[tb]
                la_ = lacc[tb]
                kcs = list(range(kc0, kc1))
                bufs = {}

                def qk(kc):
                    nonlocal si, pi
                    ps = p_s[si % 2]
                    pt_ = pT[pi % 3]
                    si += 1
                    pi += 1
                    bufs[kc] = (ps, pt_)
                    for m in range(2):
                        K.op("pe", "matmul", [kt[hb], qt[tb]], [ps], ps.t[:, m, 0:QW], kt[hb].t[64 * m:64 * m + 64, kc * 128:(kc + 1) * 128],
                             qt[tb].t[64 * m:64 * m + 64, 0:QW], start=True, stop=True)
                    K.op("act", "activation", [ps, nb], [pt_], out=pt_.t[:, :, 0:QW], in_=ps.t[:, :, 0:QW], func=AF.Exp, scale=0.125,
                         bias=nb.t[:, h:h + 1])

                def av(kc):
                    ps, pt_ = bufs.pop(kc)
                    for m in range(2):
                        K.op("pe", "matmul", [pt_, vs[hb]], [po[m]], po[m].t[:, 0:QW], vs[hb].t[:, kc, :], pt_.t[:, m, 0:QW],
                             start=(kc == kc0), stop=(kc == kc1 - 1))
                    if kc == kc0:
                        K.op("dve", "tensor_copy", [pt_], [la_], out=la_.t[:, :, 0:QW], in_=pt_.t[:, :, 0:QW])
                    else:
                        K.op("dve", "tensor_tensor", [pt_, la_], [la_], out=la_.t[:, :, 0:QW], in0=la_.t[:, :, 0:QW], in1=pt_.t[:, :, 0:QW], op=ALU.add)

                qk(kcs[0])
                for i_, kc in enumerate(kcs):
                    if i_ + 1 < len(kcs):
                        qk(kcs[i_ + 1])
                    av(kc)
                    if i_ == 2 and pending:
                        pending.pop(0)()
                    if i_ == 5 and pending:
                        pending.pop(0)()
                while pending:
                    pending.pop(0)()

                def fin_a(tb=tb, QW=QW, la_=la_):
                    nonlocal si
                    ps = p_s[si % 2]
                    si += 1
                    for m in range(2):
                        K.op("pe", "matmul", [ones, la_], [ps], ps.t[0:1, m, 0:QW], ones.t[:, 0:1], la_.t[:, m, 0:QW], start=True, stop=True)
                    K.op("dve", "reciprocal", [ps], [rl[tb]], out=rl[tb].t[:, :, 0:QW], in_=ps.t[0:1, :, 0:QW])
                    K.op("dve", "tensor_scalar", [rl[tb], nlam], [rl[tb]], out=rl[tb].t[:, 1, 0:QW], in0=rl[tb].t[:, 1, 0:QW], scalar1=nlam.t[0:1, 0:1], scalar2=None, op0=ALU.mult)
                    ps2 = p_s[si % 2]
                    si += 1
                    for m in range(2):
                        K.op("pe", "matmul", [ones, rl[tb]], [ps2], ps2.t[:, m, 0:QW], ones.t[0:1, :], rl[tb].t[0:1, m, 0:QW], start=True, stop=True)
                    K.op("act", "activation", [ps2], [bcs[tb]], out=bcs[tb].t[:, :, 0:QW], in_=ps2.t[:, :, 0:QW], func=AF.Copy)

                def fin_b(tb=tb, QW=QW, po=po, nqb=nqb, q0=q0, h=h):
                    nonlocal si
                    K.op("dve", "tensor_tensor", [po[0], bcs[tb]], [t0[tb]], out=t0[tb].t[:, 0:QW], in0=po[0].t[:, 0:QW], in1=bcs[tb].t[:, 0, 0:QW], op=ALU.mult)
                    K.op("dve", "tensor_tensor", [po[1], bcs[tb]], [t1[tb]], out=t1[tb].t[:, 0:QW], in0=po[1].t[:, 0:QW], in1=bcs[tb].t[:, 1, 0:QW], op=ALU.mult)
                    K.op("pool", "tensor_tensor", [t0[tb], t1[tb]], [t0[tb]], out=t0[tb].t[:, 0:QW], in0=t0[tb].t[:, 0:QW], in1=t1[tb].t[:, 0:QW], op=ALU.add)
                    ps3 = p_s[si % 2]
                    si += 1
                    for qb in range(nqb):
                        K.op("pe", "transpose", [t0[tb], identf], [ps3], ps3.t[:, 0, qb * 128:(qb + 1) * 128], t0[tb].t[:, qb * 128:(qb + 1) * 128], identf.t[:])
                    K.op("act", "activation", [ps3], [od[tb]], out=od[tb].t[:, 0:nqb, :], in_=ps3.t[:, 0, 0:QW].rearrange("p (a b) -> p a b", b=128), func=AF.Copy)
                    K.dma("sp", OD.t[q0:q0 + QW, h * 128:(h + 1) * 128].rearrange("(qb p) d -> p qb d", p=128), od[tb].t[:, 0:nqb, :], [od[tb]], [], od[tb])

                pending.extend([fin_a, fin_b])
        while pending:
            pending.pop(0)()
        K.end_phase("attn%d" % li)


ALPHA = (2.0 * 2) ** 0.25
EPS = 1e-5


def resid_ln(K, ps_halves, xres, gate, lnw, lnb, r, tmp, st, out_buf):
    for hf in range(2):
        K.op("dve", "tensor_tensor", [ps_halves[hf], gate], [r], out=r.t[:, hf * 512:(hf + 1) * 512], in0=ps_halves[hf].t[:],
             in1=gate.t[:, hf * 512:(hf + 1) * 512], op=ALU.mult)
    K.op("dve", "scalar_tensor_tensor", [xres, r], [r], out=r.t[:], in0=xres.t[:], scalar=ALPHA, in1=r.t[:], op0=ALU.mult, op1=ALU.add)
    K.op("dve", "reduce_sum", [r], [st], out=st.t[:, 0:1], in_=r.t[:], axis=AX.X)
    K.op("dve", "tensor_scalar", [st], [st], out=st.t[:, 0:1], in0=st.t[:, 0:1], scalar1=-1.0 / 1024, scalar2=None, op0=ALU.mult)
    K.op("act", "activation", [r, st], [r], out=r.t[:], in_=r.t[:], func=AF.Identity, bias=st.t[:, 0:1])
    K.op("act", "activation", [r], [tmp], out=tmp.t[:], in_=r.t[:], func=AF.Square)
    K.op("dve", "reduce_sum", [tmp], [st], out=st.t[:, 1:2], in_=tmp.t[:], axis=AX.X)
    K.op("act", "activation", [st], [st], out=st.t[:, 1:2], in_=st.t[:, 1:2], func=AF.Sqrt, scale=1.0 / 1024, bias=EPS)
    K.op("dve", "reciprocal", [st], [st], out=st.t[:, 1:2], in_=st.t[:, 1:2])
    K.op("dve", "scalar_tensor_tensor", [r, st, lnw], [tmp], out=tmp.t[:], in0=r.t[:], scalar=st.t[:, 1:2], in1=lnw.t[:], op0=ALU.mult, op1=ALU.mult)
    K.op("pool", "tensor_tensor", [tmp, lnb], [out_buf], out=out_buf.t[:], in0=tmp.t[:], in1=lnb.t[:], op=ALU.add)


def phase_post_outproj(K, li, X, MOD, PTOK, XSB, YSS, YSR, OD, P, w_out, X1, consts):
    nc = K.nc
    NCH = K.NCH
    S = K.new_phase()
    lam_init = 0.8 - 0.6 * math.exp(-0.3 * li)
    with ExitStack() as es:
        W = _sb(es, nc, "po_w", [128, 16, 1024], BF16)
        identb = _sb(es, nc, "po_id", [128, 128], BF16)
        gate = [_sb(es, nc, "po_g%d" % i, [128, 1024], F32) for i in range(2)]
        lnw = _sb(es, nc, "po_lnw", [128, 1024], F32)
        lnb = _sb(es, nc, "po_lnb", [128, 1024], F32)
        snw = _sb(es, nc, "po_snw", [128, 1024], F32)
        dsk = _sb(es, nc, "po_dsk", [128, 16], F32)
        rnw = _sb(es, nc, "po_rnw", [128, 128], F32)
        dnw = _sb(es, nc, "po_dnw", [128, 128], F32)
        ys = [_sb(es, nc, "po_ys%d" % i, [128, 1024], F32) for i in range(2)]
        xs = [_sb(es, nc, "po_xs%d" % i, [128, 1024], BF16) for i in range(2)]
        z = [_sb(es, nc, "po_z%d" % i, [128, 1024], F32) for i in range(2)]
        t = [_sb(es, nc, "po_t%d" % i, [128, 1024], F32) for i in range(2)]
        yr = [_sb(es, nc, "po_yr%d" % i, [128, 512], F32) for i in range(2)]
        rg = [_sb(es, nc, "po_rg%d" % i, [128, 512], F32) for i in range(2)]
        od = [_sb(es, nc, "po_od%d" % i, [128, 512], F32) for i in range(2)]
        t5 = [_sb(es, nc, "po_t5%d" % i, [128, 512], F32) for i in range(2)]
        t6 = [_sb(es, nc, "po_t6%d" % i, [128, 512], F32) for i in range(2)]
        st = [_sb(es, nc, "po_st%d" % i, [128, 16], F32) for i in range(2)]
        std = [_sb(es, nc, "po_std%d" % i, [128, 16], F32) for i in range(2)]
        str_ = [_sb(es, nc, "po_str%d" % i, [128, 16], F32) for i in range(2)]
        ycat = [_sb(es, nc, "po_yc%d" % i, [128, 2048], BF16) for i in range(2)]
        ycT = [_sb(es, nc, "po_ycT%d" % i, [128, 16, 128], BF16) for i in range(2)]
        xr = [_sb(es, nc, "po_xr%d" % i, [128, 1024], F32) for i in range(2)]
        r = [_sb(es, nc, "po_r%d" % i, [128, 1024], F32) for i in range(2)]
        ob = [_sb(es, nc, "po_ob%d" % i, [128, 1024], F32) for i in range(2)]
        pt = [_ps(es, nc, "po_pt%d" % i, [128, 8, 128], BF16) for i in range(2)]
        py = [_ps(es, nc, "po_py%d" % i, [128, 512], F32) for i in range(4)]
        wres = load_weight_bf16(S, W, w_out[li], 16)
        K.dma("sp", identb.t[:], consts["identb"], [], [identb], identb)
        for s_ in range(2):
            K.dma("sp", gate[s_].t[:], MOD.t[s_, :, 2048:3072], [], [gate[s_]], gate[s_])
        K.dma("sp", lnw.t[:], P["ln1_w"][li], [], [lnw], lnw)
        K.dma("sp", lnb.t[:], P["ln1_b"][li], [], [lnb], lnb)
        K.dma("sp", snw.t[:], P["ssd_nw"][li], [], [snw], snw)
        K.dma("sp", dsk.t[:], P["ssd_d"][li], [], [dsk], dsk)
        K.dma("sp", rnw.t[:], P["ret_nw"][li], [], [rnw], rnw)
        K.dma("sp", dnw.t[:], P["diff_nw"][li], [], [dnw], dnw)
        def loads(c):
            b = c % 2
            rows = slice(c * 128, (c + 1) * 128)
            K.dma("sp", ys[b].t[:], YSS.t[rows, :], [], [ys[b]], ys[b])
            K.dma("sp", xs[b].t[:], XSB.t[rows, 0:1024], [], [xs[b]], xs[b])
            K.dma("sp", z[b].t[:], PTOK.t[rows, PT_Z:PT_Z + 1024], [], [z[b]], z[b])
            K.dma("sp", yr[b].t[:], YSR.t[rows, :], [], [yr[b]], yr[b])
            K.dma("sp", rg[b].t[:], PTOK.t[rows, PT_RG:PT_RG + 512], [], [rg[b]], rg[b])
            K.dma("sp", od[b].t[:], OD.t[rows, :], [], [od[b]], od[b])
            K.dma("sp", xr[b].t[:], X.t[rows, :], [], [xr[b]], xr[b])

        def stA(c):
            b = c % 2
            K.begin_chain()
            K.op("pool", "tensor_tensor", [xs[b], dsk], [t[b]], out=t[b].t[:].rearrange("p (h q) -> p h q", q=64),
                 in0=xs[b].t[:].rearrange("p (h q) -> p h q", q=64), in1=bc(dsk.t[:].unsqueeze(2), [128, 16, 64]), op=ALU.mult)
            K.op("dve", "tensor_tensor", [ys[b], t[b]], [ys[b]], out=ys[b].t[:], in0=ys[b].t[:], in1=t[b].t[:], op=ALU.add)
            K.op("act", "activation", [z[b]], [z[b]], out=z[b].t[:], in_=z[b].t[:], func=AF.Silu)
            K.op("dve", "tensor_tensor", [ys[b], z[b]], [ys[b]], out=ys[b].t[:], in0=ys[b].t[:], in1=z[b].t[:], op=ALU.mult)
            K.op("pool", "tensor_tensor", [ys[b]], [t[b]], out=t[b].t[:], in0=ys[b].t[:], in1=ys[b].t[:], op=ALU.mult)
            K.op("dve", "reduce_sum", [t[b]], [st[b]], out=st[b].t[:, 0:1], in_=t[b].t[:], axis=AX.X)
            K.op("act", "activation", [st[b]], [st[b]], out=st[b].t[:, 0:1], in_=st[b].t[:, 0:1], func=AF.Sqrt, scale=1.0 / 1024, bias=EPS)
            K.op("dve", "reciprocal", [st[b]], [st[b]], out=st[b].t[:, 0:1], in_=st[b].t[:, 0:1])
            K.op("dve", "scalar_tensor_tensor", [ys[b], st[b], snw], [ycat[b]], out=ycat[b].t[:, 0:1024], in0=ys[b].t[:], scalar=st[b].t[:, 0:1],
                 in1=snw.t[:], op0=ALU.mult, op1=ALU.mult)
            K.begin_chain()
            odv = od[b].t[:].rearrange("p (h d) -> p h d", d=128)
            t5v = t5[b].t[:].rearrange("p (h d) -> p h d", d=128)
            K.op("pool", "tensor_tensor", [od[b]], [t5[b]], out=t5[b].t[:], in0=od[b].t[:], in1=od[b].t[:], op=ALU.mult)
            K.op("dve", "reduce_sum", [t5[b]], [std[b]], out=std[b].t[:, 4:8], in_=t5v, axis=AX.X)
            K.op("act", "activation", [std[b]], [std[b]], out=std[b].t[:, 4:8], in_=std[b].t[:, 4:8], func=AF.Sqrt, scale=1.0 / 128, bias=EPS)
            K.op("dve", "reciprocal", [std[b]], [std[b]], out=std[b].t[:, 4:8], in_=std[b].t[:, 4:8])
            K.op("dve", "tensor_tensor", [od[b], std[b]], [od[b]], out=odv, in0=odv, in1=bc(std[b].t[:, 4:8].unsqueeze(2), [128, 4, 128]), op=ALU.mult)
            K.op("dve", "scalar_tensor_tensor", [od[b], dnw], [ycat[b]], out=ycat[b].t[:, 1024:1536].rearrange("p (h d) -> p h d", d=128), in0=odv,
                 scalar=1.0 - lam_init, in1=bc(dnw.t[:].unsqueeze(1), [128, 4, 128]), op0=ALU.mult, op1=ALU.mult)
            K.begin_chain()
            yrv = yr[b].t[:].rearrange("p (h d) -> p h d", d=128)
            t6v = t6[b].t[:].rearrange("p (h d) -> p h d", d=128)
            K.op("dve", "reduce_sum", [yr[b]], [str_[b]], out=str_[b].t[:, 8:12], in_=yrv, axis=AX.X)
            K.op("dve", "tensor_scalar", [str_[b]], [str_[b]], out=str_[b].t[:, 8:12], in0=str_[b].t[:, 8:12], scalar1=1.0 / 128, scalar2=None, op0=ALU.mult)
            K.op("dve", "tensor_tensor", [yr[b], str_[b]], [yr[b]], out=yrv, in0=yrv, in1=bc(str_[b].t[:, 8:12].unsqueeze(2), [128, 4, 128]), op=ALU.subtract)
            K.op("pool", "tensor_tensor", [yr[b]], [t6[b]], out=t6[b].t[:], in0=yr[b].t[:], in1=yr[b].t[:], op=ALU.mult)
            K.op("dve", "reduce_sum", [t6[b]], [str_[b]], out=str_[b].t[:, 12:16], in_=t6v, axis=AX.X)
            K.op("act", "activation", [str_[b]], [str_[b]], out=str_[b].t[:, 12:16], in_=str_[b].t[:, 12:16], func=AF.Sqrt, scale=1.0 / 128, bias=EPS)
            K.op("dve", "reciprocal", [str_[b]], [str_[b]], out=str_[b].t[:, 12:16], in_=str_[b].t[:, 12:16])
            K.op("dve", "tensor_tensor", [yr[b], str_[b]], [yr[b]], out=yrv, in0=yrv, in1=bc(str_[b].t[:, 12:16].unsqueeze(2), [128, 4, 128]), op=ALU.mult)
            K.op("pool", "tensor_tensor", [yr[b], rnw], [yr[b]], out=yrv, in0=yrv, in1=bc(rnw.t[:].unsqueeze(1), [128, 4, 128]), op=ALU.mult)
            K.op("act", "activation", [rg[b]], [rg[b]], out=rg[b].t[:], in_=rg[b].t[:], func=AF.Silu)
            K.op("dve", "tensor_tensor", [yr[b], rg[b]], [ycat[b]], out=ycat[b].t[:, 1536:2048], in0=yr[b].t[:], in1=rg[b].t[:], op=ALU.mult)
            K.flush_chains()

        def stB(c):
            b = c % 2
            for half in range(2):
                for k8 in range(8):
                    kc = half * 8 + k8
                    K.op("pe", "transpose", [ycat[b], identb], [pt[half]], pt[half].t[:, k8, :], ycat[b].t[:, kc * 128:(kc + 1) * 128], identb.t[:])
                if half == 0:
                    K.op("act", "activation", [pt[half]], [ycT[b]], out=ycT[b].t[:, 0:8, :], in_=pt[half].t[:], func=AF.Copy)
                else:
                    K.op("dve", "tensor_copy", [pt[half]], [ycT[b]], out=ycT[b].t[:, 8:16, :], in_=pt[half].t[:])
            pys = [py[(c % 2) * 2], py[(c % 2) * 2 + 1]]
            for hf in range(2):
                for kc in range(16):
                    K.op("pe", "matmul", [ycT[b]] + wres, [pys[hf]], pys[hf].t[:], ycT[b].t[:, kc, :], W.t[:, kc, hf * 512:(hf + 1) * 512],
                         start=(kc == 0), stop=(kc == 15))

        def stC(c):
            b = c % 2
            src = 1 if c < NCTX // 128 else 0
            rows = slice(c * 128, (c + 1) * 128)
            pys = [py[(c % 2) * 2], py[(c % 2) * 2 + 1]]
            resid_ln(K, pys, xr[b], gate[src], lnw, lnb, r[b], t[b], st[b], ob[b])
            K.dma("sp", X1.t[rows, :], ob[b].t[:], [ob[b]], [], ob[b])

        loads(0)
        if NCH > 1:
            loads(1)
        stA(0)
        for c in range(NCH):
            stB(c)
            K.hold_chains()
            if c + 1 < NCH:
                stA(c + 1)
            K.begin_chain()
            stC(c)
            K.release_chains()
            if c + 2 < NCH:
                loads(c + 2)
        K.end_phase("post%d" % li)


def phase_ffn_down(K, li, X1, MOD, UVT, P, w_down, XN, out_ap, consts):
    nc = K.nc
    NCH = K.NCH
    S = K.new_phase()
    NT = DFF // 128
    with ExitStack() as es:
        W = _sb(es, nc, "fd_w", [128, NT, 1024], BF16)
        gate = [_sb(es, nc, "fd_g%d" % i, [128, 1024], F32) for i in range(2)]
        lnw = _sb(es, nc, "fd_lnw", [128, 1024], F32)
        lnb = _sb(es, nc, "fd_lnb", [128, 1024], F32)
        cw = _sb(es, nc, "fd_cw", [128, NT, 3], F32)
        cb = _sb(es, nc, "fd_cb", [128, NT], F32)
        uh = [_sb(es, nc, "fd_uh%d" % i, [128, NT, 130], F32) for i in range(3)]
        vv = [_sb(es, nc, "fd_v%d" % i, [128, NT, 128], F32) for i in range(2)]
        acc = [_sb(es, nc, "fd_acc%d" % i, [128, NT, 128], F32) for i in range(2)]
        accsl = [[Buf(acc[i].t, "accsl") for ct in range(NT)] for i in range(2)]
        gl = [_sb(es, nc, "fd_gl", [128, NT, 128], F32)] * 2
        gT = [_sb(es, nc, "fd_gT%d" % i, [128, NT, 128], BF16) for i in range(2)]
        xr = [_sb(es, nc, "fd_xr%d" % i, [128, 1024], F32) for i in range(2)]
        r = [_sb(es, nc, "fd_r", [128, 1024], F32)] * 2
        t = [_sb(es, nc, "fd_t", [128, 1024], F32)] * 2
        st = [_sb(es, nc, "fd_st%d" % i, [128, 2], F32) for i in range(2)]
        ob = [_sb(es, nc, "fd_ob%d" % i, [128, 1024], F32) for i in range(2)]
        py = [_ps(es, nc, "fd_py%d" % i, [128, 512], F32) for i in range(4)]
        wres = load_weight_bf16(S, W, w_down[li], NT)
        for s_ in range(2):
            K.dma("sp", gate[s_].t[:], MOD.t[s_, :, 5120:6144], [], [gate[s_]], gate[s_])
        K.dma("sp", lnw.t[:], P["ln2_w"][li], [], [lnw], lnw)
        K.dma("sp", lnb.t[:], P["ln2_b"][li], [], [lnb], lnb)
        K.dma("sp", cw.t[:], P["ffn_cw"][li], [], [cw], cw)
        K.dma("sp", cb.t[:], P["ffn_cb"][li], [], [cb], cb)
        def loadsA(c):
            K.dma("sp", uh[c % 3].t[:, :, 1:129], UVT.t[c, :, 0:NT, :], [], [uh[c % 3]], uh[c % 3])
            K.dma("sp", vv[c % 2].t[:], UVT.t[c, :, NT:2 * NT, :], [], [vv[c % 2]], vv[c % 2])

        def loadsC(c):
            K.dma("sp", xr[c % 2].t[:], X1.t[c * 128:(c + 1) * 128, :], [], [xr[c % 2]], xr[c % 2])

        def stA(c):
            b = c % 2
            ub = uh[c % 3]
            lv = c not in (0, 2)
            rv = c not in (1, NCH - 1)
            if lv:
                K.op("pool", "tensor_copy", [uh[(c - 1) % 3]], [ub], out=ub.t[:, :, 0:1], in_=uh[(c - 1) % 3].t[:, :, 128:129])
            else:
                K.op("pool", "memset", [], [ub], ub.t[:, :, 0:1], 0.0)
            if rv:
                K.op("pool", "tensor_copy", [uh[(c + 1) % 3]], [ub], out=ub.t[:, :, 129:130], in_=uh[(c + 1) % 3].t[:, :, 1:2])
            else:
                K.op("pool", "memset", [], [ub], ub.t[:, :, 129:130], 0.0)
            asl = accsl[b]
            for ct in range(NT):
                K.op("act", "activation", [ub, cw, cb], [asl[ct]], out=acc[b].t[:, ct, :], in_=ub.t[:, ct, 0:128], func=AF.Identity,
                     scale=cw.t[:, ct, 0:1], bias=cb.t[:, ct:ct + 1])
            for j in range(1, 3):
                for ct in range(NT):
                    K.op("dve", "scalar_tensor_tensor", [ub, cw, asl[ct]], [asl[ct]], out=acc[b].t[:, ct, :], in0=ub.t[:, ct, j:j + 128],
                         scalar=cw.t[:, ct, j:j + 1], in1=acc[b].t[:, ct, :], op0=ALU.mult, op1=ALU.add)
            K.op("act", "activation", asl, [gl[b]], out=gl[b].t[:], in_=acc[b].t[:], func=AF.Gelu)
            K.op("pool", "tensor_tensor", [gl[b], vv[b]], [gT[b]], out=gT[b].t[:], in0=gl[b].t[:], in1=vv[b].t[:], op=ALU.mult)

        def stB(c):
            b = c % 2
            pys = [py[(c % 2) * 2], py[(c % 2) * 2 + 1]]
            for hf in range(2):
                for kc in range(NT):
                    K.op("pe", "matmul", [gT[b]] + wres, [pys[hf]], pys[hf].t[:], gT[b].t[:, kc, :], W.t[:, kc, hf * 512:(hf + 1) * 512],
                         start=(kc == 0), stop=(kc == NT - 1))

        def stC(c):
            b = c % 2
            src = 1 if c < NCTX // 128 else 0
            rows = slice(c * 128, (c + 1) * 128)
            pys = [py[(c % 2) * 2], py[(c % 2) * 2 + 1]]
            resid_ln(K, pys, xr[b], gate[src], lnw, lnb, r[b], t[b], st[b], ob[b])
            if out_ap is None:
                K.dma("sp", XN.t[rows, :], ob[b].t[:], [ob[b]], [], ob[b])
            elif c >= NCTX // 128:
                K.dma("sp", out_ap[c * 128 - NCTX:(c + 1) * 128 - NCTX, :], ob[b].t[:], [ob[b]], [], ob[b])

        loadsA(0)
        loadsC(0)
        if NCH > 1:
            loadsA(1)
            loadsC(1)
        stA(0)
        for c in range(NCH):
            stB(c)
            if c + 2 < NCH:
                loadsA(c + 2)
            K.hold_chains()
            if c + 1 < NCH:
                K.begin_chain()
                stA(c + 1)
            K.begin_chain()
            stC(c)
            K.release_chains()
            if c + 2 < NCH:
                loadsC(c + 2)
        K.end_phase("ffndown%d" % li)

PARAM_SHAPES = {
    "ssd_cw": [128, 12, 5], "ssd_cb": [128, 12], "ssd_dtb": [128, 32], "ssd_alog": [128, 32],
    "ret_dec": [128, 8], "diff_lam": [128, 256], "ln1_w": [128, 1024], "ln1_b": [128, 1024],
    "ln2_w": [128, 1024], "ln2_b": [128, 1024], "ssd_nw": [128, 1024], "ssd_d": [128, 16],
    "ret_nw": [128, 128], "diff_nw": [128, 128], "ffn_cw": [128, 22, 3], "ffn_cb": [128, 22],
}
PHASES = ["mod", "inproj", "ssdprep", "scans", "retprep", "scanr", "attnprep", "attn", "post", "ffnup", "ffndown"]


def build_program(NL, depth, dbg=False, stop_after=None):
    nc = bass.Bass("TRN2", target_bir_lowering=False)
    K = Ctx(nc, NL, depth, dbg)
    T = K.T
    inp = {}

    def din(name, shape, dt=F32):
        inp[name] = nc.dram_tensor(name, list(shape), dt, kind="ExternalInput").ap()
        return inp[name]

    x_in = din("x", [NL, D])
    ctx_in = din("ctx", [NCTX, D])
    c_fm = din("c_fm", [128, 8])
    cc_fm = din("cc_fm", [128, 8])
    w_ada = din("w_ada", [depth, D, 6 * D])
    b_ada = din("b_ada", [depth, 6 * D])
    w_in = din("w_in", [depth, D, IN_COLS])
    w_out = din("w_out", [depth, 2 * D, D])
    w_up = din("ffn_w_up", [depth, D, 2 * DFF])
    w_down = din("ffn_w_down", [depth, DFF, D])
    P = {k: din("p_" + k, [depth] + v) for k, v in PARAM_SHAPES.items()}
    consts = {
        "identb": din("identb", [128, 128], BF16), "identf": din("identf", [128, 128]),
        "ones": din("ones", [128, 128]), "tri": din("tri", [2, 128, 128]), "stri": din("stri", [2, 128, 128]),
        "mask": din("mask", [2, 128, 128]), "ret_tab": din("ret_tab", [T, 128]), "diff_tab": din("diff_tab", [T, 128]),
    }
    out = nc.dram_tensor("out", [NL, D], F32, kind="ExternalOutput").ap()

    X = K.dram_t("X", [T, D], F32)
    X1 = K.dram_t("X1", [T, D], F32)
    MOD = K.dram_t("MOD", [2, 128, 6 * D], F32)
    PTOK = K.dram_t("PTOK", [T, PT_W], F32)
    XBCT = K.dram_t("XBCT", [K.NCH, 128, 12, 128], F32)
    XSB = K.dram_t("XSB", [T, 1280], BF16)
    BCT = K.dram_t("BCT", [512, T], BF16)
    DTLA = K.dram_t("DTLA", [T, 64], F32)
    RQKT = K.dram_t("RQKT", [64, 8, T], BF16)
    RTOK = K.dram_t("RTOK", [T, 768], BF16)
    YSS = K.dram_t("YSS", [T, 1024], F32)
    YSR = K.dram_t("YSR", [T, 512], F32)
    KT = K.dram_t("KT", [128, 4, T], BF16)
    VA = K.dram_t("VA", [4, 128, K.NCH, 128], BF16)
    QT = K.dram_t("QT", [128, 4, T], BF16)
    KMAX = K.dram_t("KMAX", [128, 8], F32)
    KM8 = K.dram_t("KM8", [8, 1], F32)
    NB = K.dram_t("NB", [128, 4], F32)
    OD = K.dram_t("OD", [T, 512], F32)
    UVT = K.dram_t("UVT", [K.NCH, 128, 44, 128], F32)

    with ExitStack() as es:
        K.eng_sem = {e: es.enter_context(nc.semaphore("s_" + e)) for e in ENGS}
        K.dma_sems = [es.enter_context(nc.semaphore("d%d" % i)) for i in range(64)]
        K.eng_base = {e: 0 for e in ENGS}
        K.dma_base = [0] * 64
        phase_init(K, x_in, ctx_in, X)
        done = False
        for li in range(depth):
            last = li == depth - 1
            steps = [
                ("mod", lambda: phase_mod(K, li, c_fm, cc_fm, w_ada, b_ada, MOD, consts)),
                ("inproj", lambda: phase_proj(K, "inproj%d" % li, X, MOD, 1024, 0, w_in[li], IN_COLS, TOK_GROUPS, PTOK, 1024, 12, XBCT, consts)),
                ("ssdprep", lambda: phase_ssd_prep(K, li, PTOK, XBCT, P, XSB, BCT, DTLA, consts)),
                ("scans", lambda: [phase_scan(K, li, fam_ssd(), d, {"BCT": BCT, "XSB": XSB, "DTLA": DTLA}, P, YSS, consts) for d in range(2)]),
                ("retprep", lambda: phase_ret_prep(K, li, PTOK, P, RQKT, RTOK, consts)),
                ("scanr", lambda: [phase_scan(K, li, fam_ret(), d, {"RQKT": RQKT, "RTOK": RTOK}, P, YSR, consts) for d in range(2)]),
                ("attnprep", lambda: phase_attn_prep(K, li, PTOK, KT, VA, QT, KMAX, KM8, NB, consts)),
                ("attn", lambda: phase_attn(K, li, KT, VA, QT, NB, P, OD, consts)),
                ("post", lambda: phase_post_outproj(K, li, X, MOD, PTOK, XSB, YSS, YSR, OD, P, w_out, X1, consts)),
                ("ffnup", lambda: phase_proj(K, "ffnup%d" % li, X1, MOD, 4096, 3072, w_up[li], 2 * DFF, [], None, 0, 44, UVT, consts)),
                ("ffndown", lambda: phase_ffn_down(K, li, X1, MOD, UVT, P, w_down, X, out if last else None, consts)),
            ]
            for nm, fn in steps:
                fn()
                if stop_after == nm:
                    done = True
                    break
            if done:
                break
    return nc, K


def _tables(T):
    f32 = np.float32
    n_lat = T - NCTX
    inv_ax = (f32(1.0) / (f32(10000.0) ** (np.arange(16, dtype=f32) / f32(16)))).astype(f32)
    i = np.arange(n_lat)
    row = (i // GRID_W).astype(f32)
    col = (i % GRID_W).astype(f32)
    ar = (row[:, None] * inv_ax[None, :]).astype(f32).astype(np.float64)
    ac = (col[:, None] * inv_ax[None, :]).astype(f32).astype(np.float64)
    dt = np.zeros((T, 128), f32)
    dt[:NCTX, 0:64] = 1.0
    dt[NCTX:, 0:64] = np.concatenate([np.cos(ar), np.cos(ar), np.cos(ac), np.cos(ac)], 1)
    dt[NCTX:, 64:128] = np.concatenate([-np.sin(ar), np.sin(ar), -np.sin(ac), np.sin(ac)], 1)
    inv_ret = (f32(1.0) / (f32(10000.0) ** np.linspace(0.0, 1.0, 32, dtype=f32))).astype(f32)
    pos = np.arange(T).astype(f32)
    a = (pos[:, None] * inv_ret[None, :]).astype(f32).astype(np.float64)
    rt = np.concatenate([np.cos(a), np.cos(a), -np.sin(a), np.sin(a)], 1).astype(f32)
    return dt, rt


def make_consts(T):
    f32 = np.float32
    t = np.arange(128)
    le = (t[:, None] <= t[None, :]).astype(f32)
    ge = (t[:, None] >= t[None, :]).astype(f32)
    gt = (t[:, None] > t[None, :]).astype(f32)
    lt = (t[:, None] < t[None, :]).astype(f32)
    dtab, rtab = _tables(T)
    return {
        "identb": np.eye(128, dtype=f32).astype(ml_dtypes.bfloat16), "identf": np.eye(128, dtype=f32),
        "ones": np.ones((128, 128), f32), "tri": np.stack([le, ge]), "stri": np.stack([gt, lt]),
        "mask": np.stack([le, ge]), "ret_tab": rtab, "diff_tab": dtab,
    }


def _rep(a, depth):
    a = np.asarray(a[:depth], np.float32).reshape(depth, 1, -1)
    return np.ascontiguousarray(np.broadcast_to(a, (depth, 128, a.shape[2])))


def make_in_maps(inputs, NL, depth, ncores):
    T = NL + NCTX
    cs = make_consts(T)
    L = depth
    shared = {
        "w_ada": np.ascontiguousarray(inputs["w_ada"][:L]), "b_ada": np.ascontiguousarray(inputs["b_ada"][:L]),
        "w_in": np.ascontiguousarray(inputs["w_in"][:L]), "w_out": np.ascontiguousarray(inputs["w_out"][:L]),
        "ffn_w_up": np.ascontiguousarray(inputs["ffn_w_up"][:L]), "ffn_w_down": np.ascontiguousarray(inputs["ffn_w_down"][:L]),
        "cc_fm": np.ascontiguousarray(inputs["c_ctx"].reshape(8, 128).T),
        "p_ssd_cw": np.ascontiguousarray(inputs["ssd_conv_w"][:L].reshape(L, 5, 12, 128).transpose(0, 3, 2, 1)),
        "p_ssd_cb": np.ascontiguousarray(inputs["ssd_conv_b"][:L].reshape(L, 12, 128).transpose(0, 2, 1)),
        "p_ssd_dtb": _rep(inputs["ssd_dt_bias"].reshape(-1, 32), L), "p_ssd_alog": _rep(inputs["ssd_a_log"].reshape(-1, 32), L),
        "p_ret_dec": _rep(inputs["ret_decay"].reshape(-1, 8), L), "p_diff_lam": _rep(inputs["diff_lambda"].reshape(-1, 256), L),
        "p_ln1_w": _rep(inputs["ln1_w"], L), "p_ln1_b": _rep(inputs["ln1_b"], L),
        "p_ln2_w": _rep(inputs["ln2_w"], L), "p_ln2_b": _rep(inputs["ln2_b"], L),
        "p_ssd_nw": _rep(inputs["ssd_norm_w"], L), "p_ssd_d": _rep(inputs["ssd_d"], L),
        "p_ret_nw": _rep(inputs["ret_norm_w"], L), "p_diff_nw": _rep(inputs["diff_norm_w"], L),
        "p_ffn_cw": np.ascontiguousarray(inputs["ffn_conv_w"][:L].reshape(L, 3, 22, 128).transpose(0, 3, 2, 1)),
        "p_ffn_cb": np.ascontiguousarray(inputs["ffn_conv_b"][:L].reshape(L, 22, 128).transpose(0, 2, 1)),
    }
    shared.update(cs)
    maps = []
    for b in range(ncores):
        m = dict(shared)
        m["x"] = np.ascontiguousarray(inputs["x"][b, :NL])
        m["ctx"] = np.ascontiguousarray(inputs["ctx"][b])
        m["c_fm"] = np.ascontiguousarray(inputs["c"][b].reshape(8, 128).T)
        maps.append(m)
    return maps


def kernel(**inputs):
    inputs = {k: np.asarray(v) for k, v in inputs.items()}
    NL = inputs["x"].shape[1]
    depth = inputs["w_ada"].shape[0]
    nb = inputs["x"].shape[0]
    nc, K = build_program(NL, depth)
    maps = make_in_maps(inputs, NL, depth, nb)
    res = run_bass_kernel_spmd(nc, maps, core_ids=list(range(nb)))
    return np.stack([np.asarray(r["out"], np.float32) for r in res.results], axis=0)
```

```python
import math
from contextlib import ExitStack

import numpy as np
import ml_dtypes
import concourse.bass as bass
import concourse.mybir as mybir
from concourse.bass_utils import run_bass_kernel_spmd

F32 = mybir.dt.float32
BF16 = mybir.dt.bfloat16
AF = mybir.ActivationFunctionType
ALU = mybir.AluOpType
AX = mybir.AxisListType

D = 1024
NCTX = 256
GRID_W = 64
IN_COLS = 5664
DFF = 2816
ENGS = ("pe", "act", "dve", "pool", "sp")


class Res:
    __slots__ = ("name", "last_w", "readers", "sem", "sem_cnt", "base", "sw")

    def __init__(self, name=""):
        self.name = name
        self.last_w = None
        self.readers = []
        self.sem = None
        self.sem_cnt = 0
        self.base = 0
        self.sw = False


class Ins:
    __slots__ = ("eng", "idx", "fn", "deps", "dma", "sem_res", "count", "signal")

    def __init__(self, eng, idx, fn, dma, sem_res):
        self.eng = eng
        self.idx = idx
        self.fn = fn
        self.deps = []
        self.dma = dma
        self.sem_res = sem_res
        self.count = None
        self.signal = False


class Sched:
    def __init__(self, nc, eng_sem, dma_sems, eng_base, dma_base):
        self.nc = nc
        self.lists = {e: [] for e in ENGS}
        self.dma_res = []
        self.eng_sem = eng_sem
        self.dma_sems = dma_sems
        self.eng_base = eng_base
        self.dma_base = dma_base

    def add(self, eng, fn, reads=(), writes=(), dma=False, sem_res=None):
        lst = self.lists[eng]
        ins = Ins(eng, len(lst), fn, dma, sem_res)
        deps = {}
        for r in reads:
            d = r.last_w
            if d is not None:
                deps[id(d)] = d
        for w in writes:
            d = w.last_w
            if d is not None:
                deps[id(d)] = d
            for rd in w.readers:
                deps[id(rd)] = rd
        for r in reads:
            r.readers.append(ins)
        for w in writes:
            w.last_w = ins
            w.readers = []
        best = {}
        out = []
        for d in deps.values():
            if d is ins:
                continue
            if d.dma:
                out.append(d)
            else:
                if d.eng == eng and not dma:
                    if eng == "pe":
                        continue
                    if d.idx < ins.idx - 3:
                        continue
                b = best.get(d.eng)
                if b is None or d.idx > b.idx:
                    best[d.eng] = d
        out.extend(best.values())
        ins.deps = out
        if dma:
            if sem_res.sem is None:
                sem_res.sem = len(self.dma_res)
                sem_res.sw = (eng == "pool")
                self.dma_res.append(sem_res)
            assert sem_res.sw == (eng == "pool")
            sem_res.sem_cnt += 1
            ins.count = sem_res.sem_cnt
        lst.append(ins)
        return ins

    def emit(self):
        nc = self.nc
        for e in ENGS:
            for ins in self.lists[e]:
                for d in ins.deps:
                    d.signal = True
            for ins in reversed(self.lists[e]):
                if not ins.dma:
                    ins.signal = True
                    break
        final = {}
        for e in ENGS:
            c = self.eng_base[e]
            for ins in self.lists[e]:
                if (not ins.dma) and ins.signal:
                    c += 1
                    ins.count = c
            final[e] = c
        nsw = 0
        nhw = 0
        idxs = []
        for r in self.dma_res:
            if r.sw:
                nsw += 1
                idxs.append(len(self.dma_sems) - nsw)
            else:
                idxs.append(nhw)
                nhw += 1
        assert nhw + 24 <= len(self.dma_sems) and nsw <= 24, (nhw, nsw)
        self._idxs = idxs
        for i, r in zip(idxs, self.dma_res):
            r.sem = self.dma_sems[i]
            r.base = self.dma_base[i]
        stats = {}

        def run_engine(e, eo):
            seen = {}
            nw = 0
            for ins in self.lists[e]:
                for d in ins.deps:
                    if d.dma:
                        key = ("d", id(d.sem_res))
                        val = d.sem_res.base + 16 * d.count
                        sem = d.sem_res.sem
                    else:
                        key = ("e", d.eng)
                        val = d.count
                        sem = self.eng_sem[d.eng]
                    if seen.get(key, 0) >= val:
                        continue
                    seen[key] = val
                    eo.wait_ge(sem, val)
                    nw += 1
                bi = ins.fn(eo)
                if ins.dma:
                    bi.then_inc(ins.sem_res.sem, 16)
                elif ins.signal:
                    bi.then_inc(self.eng_sem[e], 1)
            for r in self.dma_res:
                eo.wait_ge(r.sem, r.base + 16 * r.sem_cnt)
            for e2 in ENGS:
                if final[e2] > self.eng_base[e2]:
                    eo.wait_ge(self.eng_sem[e2], final[e2])
            stats[e] = (len(self.lists[e]), nw)

        with nc.Block() as block:
            @block.tensor
            def _(eo):
                run_engine("pe", eo)

            @block.scalar
            def _(eo):
                run_engine("act", eo)

            @block.vector
            def _(eo):
                run_engine("dve", eo)

            @block.gpsimd
            def _(eo):
                run_engine("pool", eo)

            @block.sync
            def _(eo):
                run_engine("sp", eo)
        for e in ENGS:
            self.eng_base[e] = final[e]
        for i, r in zip(self._idxs, self.dma_res):
            self.dma_base[i] = r.base + 16 * r.sem_cnt
        return stats


class Buf:
    __slots__ = ("t", "r")

    def __init__(self, t, name=""):
        self.t = t
        self.r = Res(name)


class Ctx:
    def __init__(self, nc, NL, depth, dbg):
        self.nc = nc
        self.NL = NL
        self.T = NL + NCTX
        self.NCH = self.T // 128
        self.depth = depth
        self.dbg = dbg
        self.dram = {}
        self.S = None
        self.stats = []

    def dram_t(self, name, shape, dt, out=False):
        kind = "ExternalOutput" if (out or self.dbg) else "Internal"
        t = self.nc.dram_tensor(name, list(shape), dt, kind=kind).ap()
        b = Buf(t, name)
        self.dram[name] = b
        return b

    _chains = None

    def op(self, eng, meth, reads, writes, *args, **kw):
        if self._chains is not None:
            self._chains[-1].append((eng, meth, list(reads), list(writes), args, kw))
            return None
        return self.S.add(eng, lambda e: getattr(e, meth)(*args, **kw),
                          reads=[b.r for b in reads], writes=[b.r for b in writes])

    def begin_chain(self):
        if self._chains is None:
            self._chains = []
        self._chains.append([])

    def flush_chains(self):
        chains, self._chains = self._chains, None
        n = max(len(c) for c in chains)
        for i in range(n):
            for c in chains:
                if i < len(c):
                    eng, meth, reads, writes, args, kw = c[i]
                    self.op(eng, meth, reads, writes, *args, **kw)

    def dma(self, eng, out, in_, reads, writes, semb):
        return self.S.add(eng, lambda e: e.dma_start(out=out, in_=in_),
                          reads=[b.r for b in reads], writes=[b.r for b in writes], dma=True, sem_res=semb.r)

    def new_phase(self):
        self.S = Sched(self.nc, self.eng_sem, self.dma_sems, self.eng_base, self.dma_base)
        return self.S

    def end_phase(self, name):
        st = self.S.emit()
        self.stats.append((name, st))
        self.S = None


_UID = [0]


def _sb(es, nc, name, shape, dt):
    _UID[0] += 1
    name = "%s_%d" % (name, _UID[0])
    return Buf(es.enter_context(nc.sbuf_tensor(name, list(shape), dt)), name)


def _ps(es, nc, name, shape, dt):
    _UID[0] += 1
    name = "%s_%d" % (name, _UID[0])
    return Buf(es.enter_context(nc.psum_tensor(name, list(shape), dt)), name)


def phase_init(K, x_in, ctx_in, X):
    S = K.new_phase()
    S.add("sp", lambda e: e.dma_start(out=X.t[0:NCTX, :], in_=ctx_in), writes=[X.r], dma=True, sem_res=X.r)
    r2 = Res("x2")
    nparts = 4
    step = K.NL // nparts
    for i in range(nparts):
        rr = Res("xi%d" % i)
        S.add("sp" if i % 2 == 0 else "act",
              lambda e, i=i: e.dma_start(out=X.t[NCTX + i * step:NCTX + (i + 1) * step, :],
                                         in_=x_in[i * step:(i + 1) * step, :]),
              writes=[rr], dma=True, sem_res=rr)
    K.end_phase("init")


def phase_mod(K, li, c_fm, cc_fm, w_ada, b_ada, MOD, consts):
    nc = K.nc
    S = K.new_phase()
    with ExitStack() as es:
        cin = _sb(es, nc, "m_cin", [128, 16], F32)
        sil = _sb(es, nc, "m_sil", [128, 16], F32)
        silb = _sb(es, nc, "m_silb", [128, 16, 128], F32)
        brow = _sb(es, nc, "m_brow", [1, 6144], F32)
        ones = _sb(es, nc, "m_ones", [1, 128], F32)
        wblk = [_sb(es, nc, "m_w%d" % i, [128, 8, 512], F32) for i in range(2)]
        stage = [_sb(es, nc, "m_st%d" % i, [128, 512], F32) for i in range(2)]
        ps = [_ps(es, nc, "m_ps%d" % i, [128, 512], F32) for i in range(2)]
        S.add("sp", lambda e: e.dma_start(out=cin.t[:, 0:8], in_=c_fm), writes=[cin.r], dma=True, sem_res=cin.r)
        S.add("sp", lambda e: e.dma_start(out=cin.t[:, 8:16], in_=cc_fm), writes=[cin.r], dma=True, sem_res=cin.r)
        S.add("sp", lambda e: e.dma_start(out=brow.t[:], in_=b_ada[li:li + 1, :]), writes=[brow.r], dma=True, sem_res=brow.r)
        S.add("dve", lambda e: e.memset(ones.t[:], 1.0), writes=[ones.r])
        S.add("act", lambda e: e.activation(out=sil.t[:], in_=cin.t[:], func=AF.Silu), reads=[cin.r], writes=[sil.r])
        S.add("dve", lambda e: e.tensor_copy(out=silb.t[:], in_=sil.t[:].unsqueeze(2).broadcast_to([128, 16, 128])),
              reads=[sil.r], writes=[silb.r])
        k = 0
        for j in range(12):
            wb = wblk[j % 2]
            S.add("sp", lambda e, wb=wb, j=j: e.dma_start(
                out=wb.t[:], in_=w_ada[li, :, j * 512:(j + 1) * 512].rearrange("(kc p) n -> p kc n", p=128)),
                writes=[wb.r], dma=True, sem_res=wb.r)
            for src in range(2):
                p = ps[k % 2]
                st = stage[k % 2]
                for kc in range(8):
                    S.add("pe", lambda e, p=p, wb=wb, kc=kc, src=src: e.matmul(
                        p.t[:], silb.t[:, src * 8 + kc, :], wb.t[:, kc, :], start=(kc == 0), stop=False),
                        reads=[silb.r, wb.r], writes=[p.r])
                S.add("pe", lambda e, p=p, j=j: e.matmul(
                    p.t[:], ones.t[0:1, :], brow.t[0:1, j * 512:(j + 1) * 512], start=False, stop=True),
                    reads=[ones.r, brow.r], writes=[p.r])
                addone = 1.0 if j in (2, 3, 8, 9) else 0.0
                S.add("dve", lambda e, p=p, st=st, addone=addone: e.tensor_scalar(
                    out=st.t[:], in0=p.t[:], scalar1=addone, scalar2=None, op0=ALU.add),
                    reads=[p.r], writes=[st.r])
                S.add("sp", lambda e, st=st, src=src, j=j: e.dma_start(
                    out=MOD.t[src, :, j * 512:(j + 1) * 512], in_=st.t[:]),
                    reads=[st.r], writes=[MOD.r], dma=True, sem_res=st.r)
                k += 1
        K.end_phase("mod%d" % li)


def load_weight_bf16(S, wdst, w_src_kcn, nkc, split=4):
    res = []
    for kc in range(nkc):
        wb = Buf(None, "w")
        S.add("pool", lambda e, kc=kc: e.dma_start(out=wdst.t[:, kc, :], in_=w_src_kcn[kc * 128:(kc + 1) * 128, :]),
              writes=[wb.r], dma=True, sem_res=wb.r)
        res.append(wb)
    return res


PT_Z, PT_DT, PT_DQ, PT_DK, PT_DV, PT_RQ, PT_RK, PT_RV, PT_RG, PT_W = 0, 1024, 1056, 1568, 2080, 2592, 2848, 3104, 3616, 4128
TOK_GROUPS = [
    (PT_Z, 0, 512), (PT_Z + 512, 512, 512), (PT_DT, 2560, 32),
    (PT_DQ, 2592, 512), (PT_DK, 3104, 512), (PT_DV, 3616, 512),
    (PT_RQ, 4128, 512), (PT_RV, 4640, 512), (PT_RG, 5152, 512),
]


def phase_proj(K, name, X, MOD, sc_col, sh_col, w_src, ncols, tok_groups, PTOK, fm_col0, n_fm, FMB, consts):
    nc = K.nc
    S = K.new_phase()
    NCH = K.NCH
    with ExitStack() as es:
        W = _sb(es, nc, "ip_w", [128, 8, ncols], BF16)
        identb = _sb(es, nc, "ip_id", [128, 128], BF16)
        modt = [[_sb(es, nc, "ip_mod%d%d" % (s_, q), [128, 1024], F32) for q in range(2)] for s_ in range(2)]
        xt = [_sb(es, nc, "ip_x%d" % i, [128, 1024], F32) for i in range(2)]
        xmb = [_sb(es, nc, "ip_xb%d" % i, [128, 1024], BF16) for i in range(2)]
        xT4 = [_sb(es, nc, "ip_xT%d" % i, [128, 8, 512], BF16) for i in range(2)]
        xslot = [[Buf(xT4[i].t, "slot") for j in range(4)] for i in range(2)]
        stg = [_sb(es, nc, "ip_st%d" % i, [128, PT_W if tok_groups else 8], F32) for i in range(2)]
        stf = [_sb(es, nc, "ip_sf%d" % i, [128, 4, 512], F32) for i in range(3)]
        pst = [_ps(es, nc, "ip_pt%d" % i, [128, 8, 128], BF16) for i in range(2)]
        psg = [_ps(es, nc, "ip_pg%d" % i, [128, 512], F32) for i in range(4 if tok_groups else 1)]
        psf = [_ps(es, nc, "ip_pf%d" % i, [128, 512], F32) for i in range(2 if tok_groups else 4)]
        wres = load_weight_bf16(S, W, w_src, 8)
        K.dma("sp", identb.t[:], consts["identb"], [], [identb], identb)
        for s_ in range(2):
            for q in range(2):
                c0 = sc_col if q == 0 else sh_col
                K.dma("sp", modt[s_][q].t[:], MOD.t[s_, :, c0:c0 + 1024], [], [modt[s_][q]], modt[s_][q])

        def loads(c):
            K.dma("sp", xt[c % 2].t[:], X.t[c * 128:(c + 1) * 128, :], [], [xt[c % 2]], xt[c % 2])

        gi = 0
        fi = 0
        ei = 0
        loads(0)
        blocks = [list(range(i, min(i + 4, NCH))) for i in range(0, NCH, 4)]
        for bi, blk in enumerate(blocks):
            xb = xT4[bi % 2]
            slots = xslot[bi % 2]
            for j, c in enumerate(blk):
                if c + 1 < NCH:
                    loads(c + 1)
                b = c % 2
                src = 1 if c < NCTX // 128 else 0
                sl = slots[j]
                K.op("dve", "tensor_tensor", [xt[b], modt[src][0]], [xt[b]], out=xt[b].t[:], in0=xt[b].t[:], in1=modt[src][0].t[:], op=ALU.mult)
                K.op("pool", "tensor_tensor", [xt[b], modt[src][1]], [xmb[b]], out=xmb[b].t[:], in0=xt[b].t[:], in1=modt[src][1].t[:], op=ALU.add)
                for kc in range(8):
                    K.op("pe", "transpose", [xmb[b], identb], [pst[b]], pst[b].t[:, kc, :], xmb[b].t[:, kc * 128:(kc + 1) * 128], identb.t[:])
                K.op("act", "activation", [pst[b]], [sl], out=xb.t[:, :, j * 128:(j + 1) * 128], in_=pst[b].t[:], func=AF.Copy)
                for (dc, sc, wd) in tok_groups:
                    p = psg[gi % len(psg)]
                    for kc in range(8):
                        K.op("pe", "matmul", [sl] + wres, [p], p.t[:, 0:wd], xb.t[:, kc, j * 128:(j + 1) * 128], W.t[:, kc, sc:sc + wd],
                             start=(kc == 0), stop=(kc == 7))
                    if gi % 2 == 0:
                        K.op("act", "activation", [p], [stg[b]], out=stg[b].t[:, dc:dc + wd], in_=p.t[:, 0:wd], func=AF.Copy)
                    else:
                        K.op("dve", "tensor_copy", [p], [stg[b]], out=stg[b].t[:, dc:dc + wd], in_=p.t[:, 0:wd])
                    gi += 1
                if tok_groups:
                    K.dma("sp", PTOK.t[c * 128:(c + 1) * 128, :], stg[b].t[:], [stg[b]], [], stg[b])
            BW = len(blk) * 128
            for ct in range(n_fm):
                p = psf[fi % len(psf)]
                fi += 1
                col = fm_col0 + ct * 128
                for kc in range(8):
                    K.op("pe", "matmul", slots[0:len(blk)] + wres, [p], p.t[:, 0:BW], W.t[:, kc, col:col + 128], xb.t[:, kc, 0:BW],
                         start=(kc == 0), stop=(kc == 7))
                sb = stf[(ct // 4 + bi * ((n_fm + 3) // 4)) % 3]
                if ei % 2 == 0:
                    K.op("dve", "tensor_copy", [p], [sb], out=sb.t[:, ct % 4, 0:BW], in_=p.t[:, 0:BW])
                else:
                    K.op("act", "activation", [p], [sb], out=sb.t[:, ct % 4, 0:BW], in_=p.t[:, 0:BW], func=AF.Copy)
                ei += 1
                if ct % 4 == 3:
                    for j, c in enumerate(blk):
                        K.dma("sp", FMB.t[c, :, ct - 3:ct + 1, :], sb.t[:, :, j * 128:(j + 1) * 128], [sb], [], sb)
        K.end_phase(name)


def bc(ap, shape):
    return ap.broadcast_to(list(shape))


def phase_ssd_prep(K, li, PTOK, XBCT, P, XSB, BCT, DTLA, consts):
    nc = K.nc
    S = K.new_phase()
    NCH = K.NCH
    with ExitStack() as es:
        identb = _sb(es, nc, "sp_id", [128, 128], BF16)
        cw = _sb(es, nc, "sp_cw", [128, 12, 5], F32)
        cb = _sb(es, nc, "sp_cb", [128, 12], F32)
        dtb = _sb(es, nc, "sp_dtb", [128, 32], F32)
        alog = _sb(es, nc, "sp_alog", [128, 32], F32)
        aneg = _sb(es, nc, "sp_aneg", [128, 32], F32)
        xh = [_sb(es, nc, "sp_xh%d" % i, [128, 12, 132], F32) for i in range(3)]
        acc = [_sb(es, nc, "sp_acc%d" % i, [128, 12, 128], F32) for i in range(2)]
        accsl = [[Buf(acc[i].t, "accsl") for ct in range(12)] for i in range(2)]
        xc = [_sb(es, nc, "sp_xc%d" % i, [128, 12, 128], BF16) for i in range(2)]
        tok = [_sb(es, nc, "sp_tok%d" % i, [128, 1280], BF16) for i in range(2)]
        dtr = [_sb(es, nc, "sp_dtr%d" % i, [128, 32], F32) for i in range(2)]
        dl = [_sb(es, nc, "sp_dl%d" % i, [128, 64], F32) for i in range(2)]
        pt = [_ps(es, nc, "sp_pt%d" % i, [128, 8, 128], BF16) for i in range(2)]
        pb = [_ps(es, nc, "sp_pb%d" % i, [128, 2, 128], BF16) for i in range(2)]
        K.dma("sp", identb.t[:], consts["identb"], [], [identb], identb)
        K.dma("sp", cw.t[:], P["ssd_cw"][li], [], [cw], cw)
        K.dma("sp", cb.t[:], P["ssd_cb"][li], [], [cb], cb)
        K.dma("sp", dtb.t[:], P["ssd_dtb"][li], [], [dtb], dtb)
        K.dma("sp", alog.t[:], P["ssd_alog"][li], [], [alog], alog)
        K.op("act", "activation", [alog], [aneg], out=aneg.t[:], in_=alog.t[:], func=AF.Exp)
        K.op("dve", "tensor_scalar", [aneg], [aneg], out=aneg.t[:], in0=aneg.t[:], scalar1=-1.0, scalar2=None, op0=ALU.mult)
        def loads(c):
            K.dma("sp", xh[c % 3].t[:, :, 2:130], XBCT.t[c], [], [xh[c % 3]], xh[c % 3])
            K.dma("sp", dtr[c % 2].t[:], PTOK.t[c * 128:(c + 1) * 128, PT_DT:PT_DT + 32], [], [dtr[c % 2]], dtr[c % 2])

        def stA(c):
            b = c % 2
            xb_ = xh[c % 3]
            lv = c not in (0, 2)
            rv = c not in (1, NCH - 1)
            if lv:
                K.op("pool", "tensor_copy", [xh[(c - 1) % 3]], [xb_], out=xb_.t[:, :, 0:2], in_=xh[(c - 1) % 3].t[:, :, 128:130])
            else:
                K.op("pool", "memset", [], [xb_], xb_.t[:, :, 0:2], 0.0)
            if rv:
                K.op("pool", "tensor_copy", [xh[(c + 1) % 3]], [xb_], out=xb_.t[:, :, 130:132], in_=xh[(c + 1) % 3].t[:, :, 2:4])
            else:
                K.op("pool", "memset", [], [xb_], xb_.t[:, :, 130:132], 0.0)
            asl = accsl[b]
            for ct in range(12):
                K.op("dve", "tensor_scalar", [xb_, cw, cb], [asl[ct]], out=acc[b].t[:, ct, :], in0=xb_.t[:, ct, 0:128],
                     scalar1=cw.t[:, ct, 0:1], scalar2=cb.t[:, ct:ct + 1], op0=ALU.mult, op1=ALU.add)
            for j in range(1, 5):
                for ct in range(12):
                    K.op("dve", "scalar_tensor_tensor", [xb_, cw, asl[ct]], [asl[ct]], out=acc[b].t[:, ct, :], in0=xb_.t[:, ct, j:j + 128],
                         scalar=cw.t[:, ct, j:j + 1], in1=acc[b].t[:, ct, :], op0=ALU.mult, op1=ALU.add)
            K.op("act", "activation", asl, [xc[b]], out=xc[b].t[:], in_=acc[b].t[:], func=AF.Silu)
            K.op("dve", "tensor_tensor", [dtr[b], dtb], [dtr[b]], out=dtr[b].t[:], in0=dtr[b].t[:], in1=dtb.t[:], op=ALU.add)
            K.op("act", "activation", [dtr[b]], [dtr[b]], out=dtr[b].t[:], in_=dtr[b].t[:], func=AF.Exp)
            K.op("act", "activation", [dtr[b]], [dl[b]], out=dl[b].t[:, 0:32], in_=dtr[b].t[:], func=AF.Ln, bias=1.0)
            K.op("dve", "tensor_tensor", [dl[b], aneg], [dl[b]], out=dl[b].t[:, 32:64], in0=dl[b].t[:, 0:32], in1=aneg.t[:], op=ALU.mult)
            K.dma("sp", DTLA.t[c * 128:(c + 1) * 128, :], dl[b].t[:], [dl[b]], [], dl[b])

        def stB(c):
            b = c % 2
            for ct in range(8):
                K.op("pe", "transpose", [xc[b], identb], [pt[b]], pt[b].t[:, ct, :], xc[b].t[:, ct, :], identb.t[:])
            for ct in range(2):
                K.op("pe", "transpose", [xc[b], identb], [pb[b]], pb[b].t[:, ct, :], xc[b].t[:, 8 + ct, :], identb.t[:])
            K.op("act", "activation", [pt[b]], [tok[b]], out=tok[b].t[:, 0:1024], in_=pt[b].t[:].rearrange("p a b -> p (a b)"), func=AF.Copy)
            K.op("dve", "tensor_copy", [pb[b]], [tok[b]], out=tok[b].t[:, 1024:1280], in_=pb[b].t[:].rearrange("p a b -> p (a b)"))
            K.dma("sp", XSB.t[c * 128:(c + 1) * 128, :], tok[b].t[:], [tok[b]], [], tok[b])
            K.dma("sp", BCT.t[:, c * 128:(c + 1) * 128].rearrange("(ct p) t -> p ct t", p=128), xc[b].t[:, 8:12, :], [xc[b]], [], xc[b])

        loads(0)
        if NCH > 1:
            loads(1)
        stA(0)
        for c in range(NCH):
            stB(c)
            if c + 2 < NCH:
                loads(c + 2)
            if c + 1 < NCH:
                stA(c + 1)
        K.end_phase("ssdprep%d" % li)


def rope_tok(K, x, tabs, o, t1, nmap, half, cs_off, sn_off):
    raise NotImplementedError


def phase_ret_prep(K, li, PTOK, P, RQKT, RTOK, consts):
    nc = K.nc
    S = K.new_phase()
    NCH = K.NCH
    with ExitStack() as es:
        identb = _sb(es, nc, "rp_id", [128, 128], BF16)
        qk = [_sb(es, nc, "rp_qk%d" % i, [128, 512], F32) for i in range(2)]
        v = [_sb(es, nc, "rp_v%d" % i, [128, 512], F32) for i in range(2)]
        tab = [_sb(es, nc, "rp_tab%d" % i, [128, 128], F32) for i in range(2)]
        t1 = [_sb(es, nc, "rp_t1%d" % i, [128, 512], F32) for i in range(2)]
        o = [_sb(es, nc, "rp_o%d" % i, [128, 512], F32) for i in range(2)]
        ob = [_sb(es, nc, "rp_ob%d" % i, [128, 512], BF16) for i in range(2)]
        tk = [_sb(es, nc, "rp_tk%d" % i, [128, 768], BF16) for i in range(2)]
        qT = [_sb(es, nc, "rp_qT%d" % i, [64, 8, 128], BF16) for i in range(2)]
        pt = [_ps(es, nc, "rp_pt%d" % i, [64, 8, 128], BF16) for i in range(2)]
        K.dma("sp", identb.t[:], consts["identb"], [], [identb], identb)
        def loads(c):
            b = c % 2
            K.dma("sp", qk[b].t[:], PTOK.t[c * 128:(c + 1) * 128, PT_RQ:PT_RQ + 512], [], [qk[b]], qk[b])
            K.dma("sp", v[b].t[:], PTOK.t[c * 128:(c + 1) * 128, PT_RV:PT_RV + 512], [], [v[b]], v[b])
            K.dma("sp", tab[b].t[:], consts["ret_tab"][c * 128:(c + 1) * 128, :], [], [tab[b]], tab[b])

        loads(0)
        for c in range(NCH):
            b = c % 2
            if c + 1 < NCH:
                loads(c + 1)
            xv = qk[b].t[:].rearrange("p (m two h) -> p m two h", two=2, h=32)
            t1v = t1[b].t[:].rearrange("p (m two h) -> p m two h", two=2, h=32)
            sn = tab[b].t[:, 64:128].rearrange("p (two h) -> p two h", two=2)
            K.op("pool", "tensor_tensor", [qk[b], tab[b]], [t1[b]], out=t1v[:, :, 0, :], in0=xv[:, :, 1, :],
                 in1=bc(sn[:, 0:1, :], [128, 8, 32]), op=ALU.mult)
            K.op("pool", "tensor_tensor", [qk[b], tab[b]], [t1[b]], out=t1v[:, :, 1, :], in0=xv[:, :, 0, :],
                 in1=bc(sn[:, 1:2, :], [128, 8, 32]), op=ALU.mult)
            K.op("dve", "tensor_tensor", [qk[b], tab[b]], [o[b]], out=o[b].t[:].rearrange("p (m d) -> p m d", d=64),
                 in0=qk[b].t[:].rearrange("p (m d) -> p m d", d=64), in1=bc(tab[b].t[:, 0:64].unsqueeze(1), [128, 8, 64]), op=ALU.mult)
            K.op("dve", "tensor_tensor", [o[b], t1[b]], [ob[b]], out=ob[b].t[:, 0:256], in0=o[b].t[:, 0:256], in1=t1[b].t[:, 0:256], op=ALU.add)
            K.op("dve", "tensor_tensor", [o[b], t1[b]], [o[b]], out=o[b].t[:, 256:512], in0=o[b].t[:, 256:512], in1=t1[b].t[:, 256:512], op=ALU.add)
            K.op("dve", "tensor_scalar", [o[b]], [ob[b]], out=ob[b].t[:, 256:512], in0=o[b].t[:, 256:512], scalar1=0.125, scalar2=None, op0=ALU.mult)
            for h in range(8):
                K.op("pe", "transpose", [ob[b], identb], [pt[b]], pt[b].t[:, h, :], ob[b].t[:, h * 64:(h + 1) * 64], identb.t[:])
            K.op("act", "activation", [pt[b]], [qT[b]], out=qT[b].t[:], in_=pt[b].t[:], func=AF.Copy)
            K.dma("sp", RQKT.t[:, :, c * 128:(c + 1) * 128], qT[b].t[:], [qT[b]], [], qT[b])
            K.op("act", "activation", [v[b]], [tk[b]], out=tk[b].t[:, 0:512], in_=v[b].t[:], func=AF.Copy)
            K.op("pool", "tensor_copy", [ob[b]], [tk[b]], out=tk[b].t[:, 512:768], in_=ob[b].t[:, 256:512])
            K.dma("sp", RTOK.t[c * 128:(c + 1) * 128, :], tk[b].t[:], [tk[b]], [], tk[b])
        K.end_phase("retprep%d" % li)


class Fam:
    pass


def fam_ssd():
    f = Fam()
    f.name = "ssd"; f.H = 16; f.G = 2; f.N = 128; f.P = 64; f.J = 8; f.units = [[0], [1]]; f.YW = 1024
    return f


def fam_ret():
    f = Fam()
    f.name = "ret"; f.H = 4; f.G = 4; f.N = 64; f.P = 128; f.J = 4; f.units = [[0, 1, 2, 3]]; f.YW = 512
    return f


def phase_scan(K, li, fam, d, srcs, P, YS, consts):
    nc = K.nc
    S = K.new_phase()
    NCH = K.NCH
    ssd = fam.name == "ssd"
    H, G, N, PP, J = fam.H, fam.G, fam.N, fam.P, fam.J
    NU = len(fam.units)
    GU = len(fam.units[0])
    order = list(range(NCH)) if d == 0 else [1, 0] + list(range(NCH - 1, 1, -1))
    endcol = 127 if d == 0 else 0
    with ExitStack() as es:
        tri = _sb(es, nc, "sc_tri", [128, 128], F32)
        stri = _sb(es, nc, "sc_stri", [128, 128], F32)
        mask = _sb(es, nc, "sc_mask", [128, 128], F32)
        ones = _sb(es, nc, "sc_ones", [128, 128], F32)
        K.dma("sp", tri.t[:], consts["tri"][d], [], [tri], tri)
        K.dma("sp", stri.t[:], consts["stri"][d], [], [stri], stri)
        K.dma("sp", mask.t[:], consts["mask"][d], [], [mask], mask)
        K.dma("sp", ones.t[:], consts["ones"], [], [ones], ones)
        qkT = [_sb(es, nc, "sc_qkT%d" % i, [N, 2 * G, 128], BF16) for i in range(2)]
        tokw = 1280 if ssd else 768
        tok = [_sb(es, nc, "sc_tok%d" % i, [128, tokw], BF16) for i in range(2)]
        la = [_sb(es, nc, "sc_la%d" % i, [128, 64], F32) for i in range(2)]
        ecum = [_sb(es, nc, "sc_ecum%d" % i, [128, 2 * H], F32) for i in range(2)]
        sm = [_sb(es, nc, "sc_sm%d" % i, [128, GU, 128], F32) for i in range(2)]
        rc = [_sb(es, nc, "sc_rc%d" % i, [128, J, 128], F32) for i in range(2)]
        E = [_sb(es, nc, "sc_E%d" % i, [128, J, 128], F32) for i in range(2)]
        M = [_sb(es, nc, "sc_M%d" % i, [128, J, 128], BF16) for i in range(2)]
        vd = [_sb(es, nc, "sc_vd%d" % i, [128, 512], BF16) for i in range(2)]
        vs = [_sb(es, nc, "sc_vs%d" % i, [128, 512], BF16) for i in range(2)]
        Y = [_sb(es, nc, "sc_Y%d" % i, [128, fam.YW], F32) for i in range(2)]
        Yp = [_sb(es, nc, "sc_Yp%d" % i, [128, fam.YW], F32) for i in range(2)]
        St = [_sb(es, nc, "sc_S%d" % u, [N, 512], F32) for u in range(NU)]
        Sb = [_sb(es, nc, "sc_Sb%d" % u, [N, 512], BF16) for u in range(NU)]
        lac = _sb(es, nc, "sc_lac", [128, 8], F32)
        p_sc = _ps(es, nc, "sc_psc", [128, GU, 128], F32)
        p_seg = _ps(es, nc, "sc_pseg", [128, J, 128], F32)
        p_cum = _ps(es, nc, "sc_pcum", [128, 2 * H], F32)
        p_yd = _ps(es, nc, "sc_pyd", [128, 512], F32)
        p_yo = _ps(es, nc, "sc_pyo", [128, 512], F32)
        p_st = _ps(es, nc, "sc_pst", [N, 512], F32)
        for u in range(NU):
            K.op("dve", "memset", [], [St[u]], St[u].t[:], 0.0)
            K.op("pool", "memset", [], [Sb[u]], Sb[u].t[:], 0.0)
        if not ssd:
            K.dma("sp", lac.t[:], P["ret_dec"][li], [], [lac], lac)
            K.op("act", "activation", [lac], [lac], out=lac.t[:], in_=lac.t[:], func=AF.Exp)
            K.op("dve", "tensor_scalar", [lac], [lac], out=lac.t[:], in0=lac.t[:], scalar1=-1.0, scalar2=None, op0=ALU.mult)
        def loads(ci):
            c = order[ci]
            b = ci % 2
            if ssd:
                K.dma("sp", qkT[b].t[:], srcs["BCT"].t[:, c * 128:(c + 1) * 128].rearrange("(ct p) t -> p ct t", p=128), [], [qkT[b]], qkT[b])
                K.dma("sp", tok[b].t[:], srcs["XSB"].t[c * 128:(c + 1) * 128, :], [], [tok[b]], tok[b])
                K.dma("sp", la[b].t[:], srcs["DTLA"].t[c * 128:(c + 1) * 128, :], [], [la[b]], la[b])
            else:
                K.dma("sp", qkT[b].t[:, 0:4, :], srcs["RQKT"].t[:, 4:8, c * 128:(c + 1) * 128], [], [qkT[b]], qkT[b])
                K.dma("sp", qkT[b].t[:, 4:8, :], srcs["RQKT"].t[:, 0:4, c * 128:(c + 1) * 128], [], [qkT[b]], qkT[b])
                K.dma("sp", tok[b].t[:], srcs["RTOK"].t[c * 128:(c + 1) * 128, :], [], [tok[b]], tok[b])
            if d == 1:
                K.dma("sp", Yp[b].t[:], YS.t[c * 128:(c + 1) * 128, :], [], [Yp[b]], Yp[b])

        loads(0)
        for ci, c in enumerate(order):
            b = ci % 2
            if ci + 1 < len(order):
                loads(ci + 1)
            if ssd:
                la_ap = la[b].t[:, 32 + d * 16:32 + d * 16 + 16]
                dt_ap = la[b].t[:, d * 16:d * 16 + 16]
                la_res = la[b]
                kT = lambda g: qkT[b].t[:, g, :]
                qT = lambda g: qkT[b].t[:, 2 + g, :]
                ktok = lambda g: tok[b].t[:, 1024 + g * 128:1024 + (g + 1) * 128]
            else:
                la_ap = lac.t[:, d * 4:d * 4 + 4]
                la_res = lac
                kT = lambda g: qkT[b].t[:, g, :]
                qT = lambda g: qkT[b].t[:, 4 + g, :]
                ktok = lambda g: tok[b].t[:, 512 + g * 64:512 + (g + 1) * 64]
            K.op("pe", "matmul", [tri, la_res], [p_cum], p_cum.t[:, 0:H], tri.t[:], la_ap, start=True, stop=True)
            K.op("pe", "matmul", [ones, la_res], [p_cum], p_cum.t[:, H:2 * H], ones.t[:], la_ap, start=True, stop=True)
            K.op("act", "activation", [p_cum], [ecum[b]], out=ecum[b].t[:], in_=p_cum.t[:], func=AF.Exp)
            for u, groups in enumerate(fam.units):
                h0 = u * J
                for gi, g in enumerate(groups):
                    K.op("pe", "matmul", [qkT[b]], [p_sc], p_sc.t[:, gi, :], kT(g), qT(g), start=True, stop=True)
                K.op("dve", "tensor_tensor", [p_sc, mask], [sm[b]], out=sm[b].t[:], in0=p_sc.t[:],
                     in1=bc(mask.t[:].unsqueeze(1), [128, GU, 128]), op=ALU.mult)
                K.op("pool", "tensor_tensor", [la_res, tri], [rc[b]], out=rc[b].t[:],
                     in0=bc(la_ap[:, h0:h0 + J].unsqueeze(2), [128, J, 128]),
                     in1=bc(tri.t[:].unsqueeze(1), [128, J, 128]), op=ALU.mult)
                for q4 in range(J // 4):
                    K.op("pe", "matmul", [stri, rc[b]], [p_seg], p_seg.t[:, q4 * 4:(q4 + 1) * 4, :], stri.t[:], rc[b].t[:, q4 * 4:(q4 + 1) * 4, :],
                         start=True, stop=True)
                K.op("act", "activation", [p_seg], [E[b]], out=E[b].t[:], in_=p_seg.t[:], func=AF.Exp)
                smv = bc(sm[b].t[:], [128, J, 128]) if GU == 1 else sm[b].t[:]
                K.op("dve", "tensor_tensor", [E[b], sm[b]], [M[b]], out=M[b].t[:], in0=E[b].t[:], in1=smv, op=ALU.mult)
                if ssd:
                    K.op("pool", "tensor_tensor", [tok[b], la[b]], [vd[b]], out=vd[b].t[:].rearrange("p (j q) -> p j q", q=PP),
                         in0=tok[b].t[:, u * 512:(u + 1) * 512].rearrange("p (j q) -> p j q", q=PP),
                         in1=bc(dt_ap[:, h0:h0 + J].unsqueeze(2), [128, J, PP]), op=ALU.mult)
                    vd_ap = vd[b].t[:]
                    vd_res = vd[b]
                else:
                    vd_ap = tok[b].t[:, 0:512]
                    vd_res = tok[b]
                K.op("dve", "tensor_tensor", [vd_res, E[b]], [vs[b]], out=vs[b].t[:].rearrange("p (j q) -> p j q", q=PP),
                     in0=vd_ap.rearrange("p (j q) -> p j q", q=PP),
                     in1=bc(E[b].t[:, :, endcol:endcol + 1], [128, J, PP]), op=ALU.mult)
                for j in range(J):
                    K.op("pe", "matmul", [M[b], vd_res], [p_yd], p_yd.t[:, j * PP:(j + 1) * PP], M[b].t[:, j, :], vd_ap[:, j * PP:(j + 1) * PP],
                         start=True, stop=True)
                gw = 512 // GU
                for gi, g in enumerate(groups):
                    K.op("pe", "matmul", [qkT[b], Sb[u]], [p_yo], p_yo.t[:, gi * gw:(gi + 1) * gw], qT(g), Sb[u].t[:, gi * gw:(gi + 1) * gw],
                         start=True, stop=True)
                ysl = Y[b].t[:, u * 512:(u + 1) * 512]
                K.op("dve", "tensor_tensor", [p_yo, ecum[b]], [Y[b]], out=ysl.rearrange("p (j q) -> p j q", q=PP),
                     in0=p_yo.t[:].rearrange("p (j q) -> p j q", q=PP),
                     in1=bc(ecum[b].t[:, h0:h0 + J].unsqueeze(2), [128, J, PP]), op=ALU.mult)
                K.op("dve", "tensor_tensor", [p_yd, Y[b]], [Y[b]], out=ysl, in0=ysl, in1=p_yd.t[:], op=ALU.add)
                if d == 1:
                    K.op("pool", "tensor_tensor", [Yp[b], Y[b]], [Y[b]], out=ysl, in0=ysl, in1=Yp[b].t[:, u * 512:(u + 1) * 512], op=ALU.add)
                for gi, g in enumerate(groups):
                    K.op("pe", "matmul", [tok[b], vs[b]], [p_st], p_st.t[:, gi * gw:(gi + 1) * gw], ktok(g), vs[b].t[:, gi * gw:(gi + 1) * gw],
                         start=True, stop=True)
                K.op("pool", "tensor_tensor", [St[u], ecum[b]], [St[u]], out=St[u].t[:].rearrange("p (j q) -> p j q", q=PP),
                     in0=St[u].t[:].rearrange("p (j q) -> p j q", q=PP),
                     in1=bc(ecum[b].t[0:N, H + h0:H + h0 + J].unsqueeze(2), [N, J, PP]), op=ALU.mult)
                K.op("dve", "tensor_tensor", [St[u], p_st], [St[u]], out=St[u].t[:], in0=St[u].t[:], in1=p_st.t[:], op=ALU.add)
                K.op("act", "activation", [St[u]], [Sb[u]], out=Sb[u].t[:], in_=St[u].t[:], func=AF.Copy)
            K.dma("sp", YS.t[c * 128:(c + 1) * 128, :], Y[b].t[:], [Y[b]], [], Y[b])
        K.end_phase("scan_%s%d_%d" % (fam.name, li, d))


def rope_axial(K, x, tab, t1, o, nm):
    xv = x.t[:].rearrange("p (m hf two e) -> p m hf two e", hf=2, two=2, e=16)
    tv = t1.t[:].rearrange("p (m hf two e) -> p m hf two e", hf=2, two=2, e=16)
    sn = tab.t[:, 64:128].rearrange("p (hf two e) -> p hf two e", hf=2, two=2)
    K.op("pool", "tensor_tensor", [x, tab], [t1], out=tv[:, :, :, 0, :], in0=xv[:, :, :, 1, :],
         in1=bc(sn[:, :, 0, :].unsqueeze(1), [128, nm, 2, 16]), op=ALU.mult)
    K.op("pool", "tensor_tensor", [x, tab], [t1], out=tv[:, :, :, 1, :], in0=xv[:, :, :, 0, :],
         in1=bc(sn[:, :, 1, :].unsqueeze(1), [128, nm, 2, 16]), op=ALU.mult)
    K.op("dve", "tensor_tensor", [x, tab], [o], out=o.t[:].rearrange("p (m d) -> p m d", d=64),
         in0=x.t[:].rearrange("p (m d) -> p m d", d=64), in1=bc(tab.t[:, 0:64].unsqueeze(1), [128, nm, 64]), op=ALU.mult)
    K.op("dve", "tensor_tensor", [o, t1], [o], out=o.t[:], in0=o.t[:], in1=t1.t[:], op=ALU.add)


def phase_attn_prep(K, li, PTOK, KT, VA, QT, KMAX, KM8, NB, consts):
    nc = K.nc
    NCH = K.NCH
    S = K.new_phase()
    with ExitStack() as es:
        identb = _sb(es, nc, "ap_id", [128, 128], BF16)
        identf = _sb(es, nc, "ap_idf", [128, 128], F32)
        ones = _sb(es, nc, "ap_ones", [128, 128], F32)
        x = [_sb(es, nc, "ap_x%d" % i, [128, 512], F32) for i in range(2)]
        v = [_sb(es, nc, "ap_v%d" % i, [128, 512], F32) for i in range(2)]
        tab = [_sb(es, nc, "ap_tab%d" % i, [128, 128], F32) for i in range(2)]
        t1 = [_sb(es, nc, "ap_t1%d" % i, [128, 512], F32) for i in range(2)]
        o = [_sb(es, nc, "ap_o%d" % i, [128, 512], F32) for i in range(2)]
        sq = [_sb(es, nc, "ap_sq%d" % i, [128, 512], F32) for i in range(2)]
        ks = [_sb(es, nc, "ap_ks%d" % i, [128, 8], F32) for i in range(2)]
        ka = [_sb(es, nc, "ap_ka%d" % i, [128, 512], BF16) for i in range(2)]
        va = [_sb(es, nc, "ap_va%d" % i, [128, 4, 129], BF16) for i in range(2)]
        kT = [_sb(es, nc, "ap_kT%d" % i, [128, 4, 128], BF16) for i in range(2)]
        kmx = _sb(es, nc, "ap_kmx", [128, 8], F32)
        km2 = _sb(es, nc, "ap_km2", [8, 1], F32)
        dg = _sb(es, nc, "ap_dg", [8, 8], F32)
        kbc = _sb(es, nc, "ap_kbc", [128, 8], F32)
        pt = [_ps(es, nc, "ap_pt%d" % i, [128, 4, 128], BF16) for i in range(2)]
        pk = _ps(es, nc, "ap_pk", [128, 128], F32)
        K.dma("sp", identb.t[:], consts["identb"], [], [identb], identb)
        K.dma("sp", identf.t[:], consts["identf"], [], [identf], identf)
        K.dma("sp", ones.t[:], consts["ones"], [], [ones], ones)
        K.op("dve", "memset", [], [kmx], kmx.t[:], 0.0)
        for i in range(2):
            K.op("pool", "memset", [], [va[i]], va[i].t[:], 1.0)
        def loads(c):
            b = c % 2
            K.dma("sp", x[b].t[:], PTOK.t[c * 128:(c + 1) * 128, PT_DK:PT_DK + 512], [], [x[b]], x[b])
            K.dma("sp", v[b].t[:], PTOK.t[c * 128:(c + 1) * 128, PT_DV:PT_DV + 512], [], [v[b]], v[b])
            K.dma("sp", tab[b].t[:], consts["diff_tab"][c * 128:(c + 1) * 128, :], [], [tab[b]], tab[b])

        loads(0)
        for c in range(NCH):
            b = c % 2
            if c + 1 < NCH:
                loads(c + 1)
            rope_axial(K, x[b], tab[b], t1[b], o[b], 8)
            K.op("act", "activation", [o[b]], [ka[b]], out=ka[b].t[:], in_=o[b].t[:], func=AF.Copy)
            K.op("pool", "tensor_tensor", [o[b]], [sq[b]], out=sq[b].t[:], in0=o[b].t[:], in1=o[b].t[:], op=ALU.mult)
            K.op("dve", "reduce_sum", [sq[b]], [ks[b]], out=ks[b].t[:], in_=sq[b].t[:].rearrange("p (m d) -> p m d", d=64), axis=AX.X)
            K.op("dve", "tensor_tensor", [ks[b], kmx], [kmx], out=kmx.t[:], in0=kmx.t[:], in1=ks[b].t[:], op=ALU.max)
            for m in range(4):
                K.op("pe", "transpose", [ka[b], identb], [pt[b]], pt[b].t[:, m, :], ka[b].t[:, m * 128:(m + 1) * 128], identb.t[:])
            K.op("act", "activation", [pt[b]], [kT[b]], out=kT[b].t[:], in_=pt[b].t[:], func=AF.Copy)
            K.dma("sp", KT.t[:, :, c * 128:(c + 1) * 128], kT[b].t[:], [kT[b]], [], kT[b])
            K.op("dve", "tensor_copy", [v[b]], [va[b]], out=va[b].t[:, :, 0:128], in_=v[b].t[:].rearrange("p (h d) -> p h d", d=128))
            K.dma("sp", VA.t[:, :, c, :].rearrange("h p w -> p h w"), va[b].t[:, :, 0:128], [va[b]], [], va[b])
        K.op("pe", "transpose", [kmx, identf], [pk], pk.t[0:8, :], kmx.t[:], identf.t[:])
        K.op("dve", "reduce_max", [pk], [km2], out=km2.t[:], in_=pk.t[0:8, :], axis=AX.X)
        K.op("act", "activation", [km2], [km2], out=km2.t[:], in_=km2.t[:], func=AF.Sqrt)
        K.op("dve", "tensor_scalar", [km2, identf], [dg], out=dg.t[:], in0=identf.t[0:8, 0:8], scalar1=km2.t[:, 0:1], scalar2=None, op0=ALU.mult)
        K.op("pe", "matmul", [ones, dg], [pk], pk.t[:, 0:8], ones.t[0:8, :], dg.t[:], start=True, stop=True)
        K.op("dve", "tensor_copy", [pk], [kbc], out=kbc.t[:], in_=pk.t[:, 0:8])
        K.dma("sp", KMAX.t[:], kbc.t[:], [kbc], [], kbc)
        K.dma("sp", KM8.t[:], km2.t[:], [km2], [], km2)
        K.end_phase("attnprep1_%d" % li)
    S = K.new_phase()
    with ExitStack() as es:
        identb = _sb(es, nc, "aq_id", [128, 128], BF16)
        kbc = _sb(es, nc, "aq_kbc", [128, 8], F32)
        x = [_sb(es, nc, "aq_x%d" % i, [128, 512], F32) for i in range(2)]
        tab = [_sb(es, nc, "aq_tab%d" % i, [128, 128], F32) for i in range(2)]
        t1 = [_sb(es, nc, "aq_t1%d" % i, [128, 512], F32) for i in range(2)]
        o = [_sb(es, nc, "aq_o%d" % i, [128, 512], F32) for i in range(2)]
        sq = [_sb(es, nc, "aq_sq%d" % i, [128, 512], F32) for i in range(2)]
        qs = [_sb(es, nc, "aq_qs%d" % i, [128, 8], F32) for i in range(2)]
        qa = [_sb(es, nc, "aq_qa%d" % i, [128, 512], BF16) for i in range(2)]
        qT = [_sb(es, nc, "aq_qT%d" % i, [128, 4, 128], BF16) for i in range(2)]
        pt = [_ps(es, nc, "aq_pt%d" % i, [128, 4, 128], BF16) for i in range(2)]
        K.dma("sp", identb.t[:], consts["identb"], [], [identb], identb)
        identf = _sb(es, nc, "aq_idf", [128, 128], F32)
        ones = _sb(es, nc, "aq_ones", [128, 128], F32)
        qmx = _sb(es, nc, "aq_qmx", [128, 8], F32)
        km8 = _sb(es, nc, "aq_km8", [8, 1], F32)
        qm8 = _sb(es, nc, "aq_qm8", [8, 1], F32)
        dg = _sb(es, nc, "aq_dg", [8, 8], F32)
        nbt = _sb(es, nc, "aq_nb", [128, 4], F32)
        pk = _ps(es, nc, "aq_pk", [128, 128], F32)
        K.dma("sp", identf.t[:], consts["identf"], [], [identf], identf)
        K.dma("sp", ones.t[:], consts["ones"], [], [ones], ones)
        K.dma("sp", km8.t[:], KM8.t[:], [], [km8], km8)
        K.op("dve", "memset", [], [qmx], qmx.t[:], 0.0)
        def loads(c):
            b = c % 2
            K.dma("sp", x[b].t[:], PTOK.t[c * 128:(c + 1) * 128, PT_DQ:PT_DQ + 512], [], [x[b]], x[b])
            K.dma("sp", tab[b].t[:], consts["diff_tab"][c * 128:(c + 1) * 128, :], [], [tab[b]], tab[b])

        loads(0)
        for c in range(NCH):
            b = c % 2
            if c + 1 < NCH:
                loads(c + 1)
            rope_axial(K, x[b], tab[b], t1[b], o[b], 8)
            K.op("act", "activation", [o[b]], [qa[b]], out=qa[b].t[:], in_=o[b].t[:], func=AF.Copy)
            K.op("pool", "tensor_tensor", [o[b]], [sq[b]], out=sq[b].t[:], in0=o[b].t[:], in1=o[b].t[:], op=ALU.mult)
            K.op("dve", "reduce_sum", [sq[b]], [qs[b]], out=qs[b].t[:], in_=sq[b].t[:].rearrange("p (m d) -> p m d", d=64), axis=AX.X)
            K.op("dve", "tensor_tensor", [qs[b], qmx], [qmx], out=qmx.t[:], in0=qmx.t[:], in1=qs[b].t[:], op=ALU.max)
            for m in range(4):
                K.op("pe", "transpose", [qa[b], identb], [pt[b]], pt[b].t[:, m, :], qa[b].t[:, m * 128:(m + 1) * 128], identb.t[:])
            K.op("act", "activation", [pt[b]], [qT[b]], out=qT[b].t[:], in_=pt[b].t[:], func=AF.Copy)
            K.dma("sp", QT.t[:, :, c * 128:(c + 1) * 128], qT[b].t[:], [qT[b]], [], qT[b])
        K.op("pe", "transpose", [qmx, identf], [pk], pk.t[0:8, :], qmx.t[:], identf.t[:])
        K.op("dve", "reduce_max", [pk], [qm8], out=qm8.t[:], in_=pk.t[0:8, :], axis=AX.X)
        K.op("act", "activation", [qm8], [qm8], out=qm8.t[:], in_=qm8.t[:], func=AF.Sqrt)
        K.op("dve", "tensor_tensor", [qm8, km8], [qm8], out=qm8.t[:], in0=qm8.t[:], in1=km8.t[:], op=ALU.mult)
        K.op("dve", "tensor_scalar", [qm8, identf], [dg], out=dg.t[:], in0=identf.t[0:8, 0:8], scalar1=qm8.t[:, 0:1], scalar2=None, op0=ALU.mult)
        K.op("pe", "matmul", [ones, dg], [pk], pk.t[:, 0:8], ones.t[0:8, :], dg.t[:], start=True, stop=True)
        K.op("dve", "tensor_copy", [pk], [kbc], out=kbc.t[:], in_=pk.t[:, 0:8])
        pkv = kbc.t[:].rearrange("p (h m) -> p h m", m=2)
        K.op("dve", "tensor_tensor", [kbc], [nbt], out=nbt.t[:], in0=pkv[:, :, 0], in1=pkv[:, :, 1], op=ALU.max)
        K.op("dve", "tensor_scalar", [nbt], [nbt], out=nbt.t[:], in0=nbt.t[:], scalar1=-0.125, scalar2=None, op0=ALU.mult)
        K.dma("sp", NB.t[:], nbt.t[:], [nbt], [], nbt)
        K.end_phase("attnprep2_%d" % li)


def phase_attn(K, li, KT, VA, QT, NB, P, OD, consts):
    nc = K.nc
    NCH = K.NCH
    T = K.T
    S = K.new_phase()
    lam_init = 0.8 - 0.6 * math.exp(-0.3 * li)
    with ExitStack() as es:
        dl = _sb(es, nc, "at_dl", [128, 256], F32)
        dp = _sb(es, nc, "at_dp", [128, 128], F32)
        ds = _sb(es, nc, "at_ds", [128, 2], F32)
        nlam = _sb(es, nc, "at_nlam", [128, 1], F32)
        identf = _sb(es, nc, "at_idf", [128, 128], F32)
        ones = _sb(es, nc, "at_ones", [128, 128], F32)
        kt = [_sb(es, nc, "at_kt%d" % i, [128, T], BF16) for i in range(2)]
        nb = _sb(es, nc, "at_nb", [128, 4], F32)
        vs = [_sb(es, nc, "at_vs%d" % i, [128, NCH, 128], BF16) for i in range(2)]
        qt = [_sb(es, nc, "at_qt%d" % i, [128, 512], BF16) for i in range(2)]
        pT = [_sb(es, nc, "at_pT%d" % i, [128, 2, 512], BF16) for i in range(3)]
        lacc = [_sb(es, nc, "at_la%d" % i, [128, 2, 512], F32) for i in range(2)]
        rl = [_sb(es, nc, "at_rl%d" % i, [1, 2, 512], F32) for i in range(2)]
        bcs = [_sb(es, nc, "at_bc%d" % i, [128, 2, 512], F32) for i in range(2)]
        t0 = [_sb(es, nc, "at_t0%d" % i, [128, 512], F32) for i in range(2)]
        t1 = [_sb(es, nc, "at_t1%d" % i, [128, 512], F32) for i in range(2)]
        od = [_sb(es, nc, "at_od%d" % i, [128, 4, 128], F32) for i in range(2)]
        p_s = [_ps(es, nc, "at_ps%d" % i, [128, 2, 512], F32) for i in range(2)]
        p_o = [[_ps(es, nc, "at_po%d%d" % (i, m), [128, 512], F32) for m in range(2)] for i in range(2)]
        K.dma("sp", dl.t[:], P["diff_lam"][li], [], [dl], dl)
        K.dma("sp", nb.t[:], NB.t[:], [], [nb], nb)
        K.dma("sp", identf.t[:], consts["identf"], [], [identf], identf)
        K.dma("sp", ones.t[:], consts["ones"], [], [ones], ones)
        dlv = dl.t[:].rearrange("p (a two d) -> p a two d", a=2, two=2)
        K.op("dve", "tensor_tensor", [dl], [dp], out=dp.t[:].rearrange("p (a d) -> p a d", a=2), in0=dlv[:, :, 0, :], in1=dlv[:, :, 1, :], op=ALU.mult)
        K.op("dve", "reduce_sum", [dp], [ds], out=ds.t[:], in_=dp.t[:].rearrange("p (a d) -> p a d", a=2), axis=AX.X)
        K.op("act", "activation", [ds], [ds], out=ds.t[:], in_=ds.t[:], func=AF.Exp)
        K.op("dve", "scalar_tensor_tensor", [ds], [nlam], out=nlam.t[:], in0=ds.t[:, 1:2], scalar=-lam_init, in1=ds.t[:, 0:1],
             op0=ALU.add, op1=ALU.subtract)
        qtiles = [(0, NCTX, 0, NCTX // 128)] + [(q0, 512, 0, NCH) for q0 in range(NCTX, T, 512)]
        si = 0
        pi = 0
        ti = 0
        pending = []
        for h in range(4):
            hb = h % 2
            K.dma("sp", kt[hb].t[:], KT.t[:, h, :], [], [kt[hb]], kt[hb])
            K.dma("sp", vs[hb].t[:], VA.t[h], [], [vs[hb]], vs[hb])
            for (q0, QW, kc0, kc1) in qtiles:
                tb = ti % 2
                ti += 1
                nqb = QW // 128
                K.dma("sp", qt[tb].t[:, 0:QW], QT.t[:, h, q0:q0 + QW], [], [qt[tb]], qt[tb])
                po = p_o[tb]
                la_ = lacc[tb]
                kcs = list(range(kc0, kc1))
                bufs = {}

                def qk(kc):
                    nonlocal si, pi
                    ps = p_s[si % 2]
                    pt_ = pT[pi % 3]
                    si += 1
                    pi += 1
                    bufs[kc] = (ps, pt_)
                    for m in range(2):
                        K.op("pe", "matmul", [kt[hb], qt[tb]], [ps], ps.t[:, m, 0:QW], kt[hb].t[64 * m:64 * m + 64, kc * 128:(kc + 1) * 128],
                             qt[tb].t[64 * m:64 * m + 64, 0:QW], start=True, stop=True)
                    K.op("act", "activation", [ps, nb], [pt_], out=pt_.t[:, :, 0:QW], in_=ps.t[:, :, 0:QW], func=AF.Exp, scale=0.125,
                         bias=nb.t[:, h:h + 1])

                def av(kc):
                    ps, pt_ = bufs.pop(kc)
                    for m in range(2):
                        K.op("pe", "matmul", [pt_, vs[hb]], [po[m]], po[m].t[:, 0:QW], vs[hb].t[:, kc, :], pt_.t[:, m, 0:QW],
                             start=(kc == kc0), stop=(kc == kc1 - 1))
                    if kc == kc0:
                        K.op("dve", "tensor_copy", [pt_], [la_], out=la_.t[:, :, 0:QW], in_=pt_.t[:, :, 0:QW])
                    else:
                        K.op("dve", "tensor_tensor", [pt_, la_], [la_], out=la_.t[:, :, 0:QW], in0=la_.t[:, :, 0:QW], in1=pt_.t[:, :, 0:QW], op=ALU.add)

                qk(kcs[0])
                for i_, kc in enumerate(kcs):
                    if i_ + 1 < len(kcs):
                        qk(kcs[i_ + 1])
                    av(kc)
                    if i_ == 2 and pending:
                        pending.pop(0)()
                    if i_ == 5 and pending:
                        pending.pop(0)()
                while pending:
                    pending.pop(0)()

                def fin_a(tb=tb, QW=QW, la_=la_):
                    nonlocal si
                    ps = p_s[si % 2]
                    si += 1
                    for m in range(2):
                        K.op("pe", "matmul", [ones, la_], [ps], ps.t[0:1, m, 0:QW], ones.t[:, 0:1], la_.t[:, m, 0:QW], start=True, stop=True)
                    K.op("dve", "reciprocal", [ps], [rl[tb]], out=rl[tb].t[:, :, 0:QW], in_=ps.t[0:1, :, 0:QW])
                    K.op("dve", "tensor_scalar", [rl[tb], nlam], [rl[tb]], out=rl[tb].t[:, 1, 0:QW], in0=rl[tb].t[:, 1, 0:QW], scalar1=nlam.t[0:1, 0:1], scalar2=None, op0=ALU.mult)
                    ps2 = p_s[si % 2]
                    si += 1
                    for m in range(2):
                        K.op("pe", "matmul", [ones, rl[tb]], [ps2], ps2.t[:, m, 0:QW], ones.t[0:1, :], rl[tb].t[0:1, m, 0:QW], start=True, stop=True)
                    K.op("act", "activation", [ps2], [bcs[tb]], out=bcs[tb].t[:, :, 0:QW], in_=ps2.t[:, :, 0:QW], func=AF.Copy)

                def fin_b(tb=tb, QW=QW, po=po, nqb=nqb, q0=q0, h=h):
                    nonlocal si
                    K.op("dve", "tensor_tensor", [po[0], bcs[tb]], [t0[tb]], out=t0[tb].t[:, 0:QW], in0=po[0].t[:, 0:QW], in1=bcs[tb].t[:, 0, 0:QW], op=ALU.mult)
                    K.op("dve", "tensor_tensor", [po[1], bcs[tb]], [t1[tb]], out=t1[tb].t[:, 0:QW], in0=po[1].t[:, 0:QW], in1=bcs[tb].t[:, 1, 0:QW], op=ALU.mult)
                    K.op("pool", "tensor_tensor", [t0[tb], t1[tb]], [t0[tb]], out=t0[tb].t[:, 0:QW], in0=t0[tb].t[:, 0:QW], in1=t1[tb].t[:, 0:QW], op=ALU.add)
                    ps3 = p_s[si % 2]
                    si += 1
                    for qb in range(nqb):
                        K.op("pe", "transpose", [t0[tb], identf], [ps3], ps3.t[:, 0, qb * 128:(qb + 1) * 128], t0[tb].t[:, qb * 128:(qb + 1) * 128], identf.t[:])
                    K.op("act", "activation", [ps3], [od[tb]], out=od[tb].t[:, 0:nqb, :], in_=ps3.t[:, 0, 0:QW].rearrange("p (a b) -> p a b", b=128), func=AF.Copy)
                    K.dma("sp", OD.t[q0:q0 + QW, h * 128:(h + 1) * 128].rearrange("(qb p) d -> p qb d", p=128), od[tb].t[:, 0:nqb, :], [od[tb]], [], od[tb])

                pending.extend([fin_a, fin_b])
        while pending:
            pending.pop(0)()
        K.end_phase("attn%d" % li)


ALPHA = (2.0 * 2) ** 0.25
EPS = 1e-5


def resid_ln(K, ps_halves, xres, gate, lnw, lnb, r, tmp, st, out_buf):
    for hf in range(2):
        K.op("dve", "tensor_tensor", [ps_halves[hf], gate], [r], out=r.t[:, hf * 512:(hf + 1) * 512], in0=ps_halves[hf].t[:],
             in1=gate.t[:, hf * 512:(hf + 1) * 512], op=ALU.mult)
    K.op("dve", "scalar_tensor_tensor", [xres, r], [r], out=r.t[:], in0=xres.t[:], scalar=ALPHA, in1=r.t[:], op0=ALU.mult, op1=ALU.add)
    K.op("dve", "reduce_sum", [r], [st], out=st.t[:, 0:1], in_=r.t[:], axis=AX.X)
    K.op("dve", "tensor_scalar", [st], [st], out=st.t[:, 0:1], in0=st.t[:, 0:1], scalar1=-1.0 / 1024, scalar2=None, op0=ALU.mult)
    K.op("act", "activation", [r, st], [r], out=r.t[:], in_=r.t[:], func=AF.Identity, bias=st.t[:, 0:1])
    K.op("act", "activation", [r], [tmp], out=tmp.t[:], in_=r.t[:], func=AF.Square)
    K.op("dve", "reduce_sum", [tmp], [st], out=st.t[:, 1:2], in_=tmp.t[:], axis=AX.X)
    K.op("act", "activation", [st], [st], out=st.t[:, 1:2], in_=st.t[:, 1:2], func=AF.Sqrt, scale=1.0 / 1024, bias=EPS)
    K.op("dve", "reciprocal", [st], [st], out=st.t[:, 1:2], in_=st.t[:, 1:2])
    K.op("dve", "scalar_tensor_tensor", [r, st, lnw], [tmp], out=tmp.t[:], in0=r.t[:], scalar=st.t[:, 1:2], in1=lnw.t[:], op0=ALU.mult, op1=ALU.mult)
    K.op("pool", "tensor_tensor", [tmp, lnb], [out_buf], out=out_buf.t[:], in0=tmp.t[:], in1=lnb.t[:], op=ALU.add)


def phase_post_outproj(K, li, X, MOD, PTOK, XSB, YSS, YSR, OD, P, w_out, X1, consts):
    nc = K.nc
    NCH = K.NCH
    S = K.new_phase()
    lam_init = 0.8 - 0.6 * math.exp(-0.3 * li)
    with ExitStack() as es:
        W = _sb(es, nc, "po_w", [128, 16, 1024], BF16)
        identb = _sb(es, nc, "po_id", [128, 128], BF16)
        gate = [_sb(es, nc, "po_g%d" % i, [128, 1024], F32) for i in range(2)]
        lnw = _sb(es, nc, "po_lnw", [128, 1024], F32)
        lnb = _sb(es, nc, "po_lnb", [128, 1024], F32)
        snw = _sb(es, nc, "po_snw", [128, 1024], F32)
        dsk = _sb(es, nc, "po_dsk", [128, 16], F32)
        rnw = _sb(es, nc, "po_rnw", [128, 128], F32)
        dnw = _sb(es, nc, "po_dnw", [128, 128], F32)
        ys = [_sb(es, nc, "po_ys%d" % i, [128, 1024], F32) for i in range(2)]
        xs = [_sb(es, nc, "po_xs%d" % i, [128, 1024], BF16) for i in range(2)]
        z = [_sb(es, nc, "po_z%d" % i, [128, 1024], F32) for i in range(2)]
        t = [_sb(es, nc, "po_t%d" % i, [128, 1024], F32) for i in range(2)]
        yr = [_sb(es, nc, "po_yr%d" % i, [128, 512], F32) for i in range(2)]
        rg = [_sb(es, nc, "po_rg%d" % i, [128, 512], F32) for i in range(2)]
        od = [_sb(es, nc, "po_od%d" % i, [128, 512], F32) for i in range(2)]
        t5 = [_sb(es, nc, "po_t5%d" % i, [128, 512], F32) for i in range(2)]
        t6 = [_sb(es, nc, "po_t6%d" % i, [128, 512], F32) for i in range(2)]
        st = [_sb(es, nc, "po_st%d" % i, [128, 16], F32) for i in range(2)]
        std = [_sb(es, nc, "po_std%d" % i, [128, 16], F32) for i in range(2)]
        str_ = [_sb(es, nc, "po_str%d" % i, [128, 16], F32) for i in range(2)]
        ycat = [_sb(es, nc, "po_yc%d" % i, [128, 2048], BF16) for i in range(2)]
        ycT = [_sb(es, nc, "po_ycT%d" % i, [128, 16, 128], BF16) for i in range(2)]
        xr = [_sb(es, nc, "po_xr%d" % i, [128, 1024], F32) for i in range(2)]
        r = [_sb(es, nc, "po_r%d" % i, [128, 1024], F32) for i in range(2)]
        ob = [_sb(es, nc, "po_ob%d" % i, [128, 1024], F32) for i in range(2)]
        pt = [_ps(es, nc, "po_pt%d" % i, [128, 8, 128], BF16) for i in range(2)]
        py = [_ps(es, nc, "po_py%d" % i, [128, 512], F32) for i in range(4)]
        wres = load_weight_bf16(S, W, w_out[li], 16)
        K.dma("sp", identb.t[:], consts["identb"], [], [identb], identb)
        for s_ in range(2):
            K.dma("sp", gate[s_].t[:], MOD.t[s_, :, 2048:3072], [], [gate[s_]], gate[s_])
        K.dma("sp", lnw.t[:], P["ln1_w"][li], [], [lnw], lnw)
        K.dma("sp", lnb.t[:], P["ln1_b"][li], [], [lnb], lnb)
        K.dma("sp", snw.t[:], P["ssd_nw"][li], [], [snw], snw)
        K.dma("sp", dsk.t[:], P["ssd_d"][li], [], [dsk], dsk)
        K.dma("sp", rnw.t[:], P["ret_nw"][li], [], [rnw], rnw)
        K.dma("sp", dnw.t[:], P["diff_nw"][li], [], [dnw], dnw)
        def loads(c):
            b = c % 2
            rows = slice(c * 128, (c + 1) * 128)
            K.dma("sp", ys[b].t[:], YSS.t[rows, :], [], [ys[b]], ys[b])
            K.dma("sp", xs[b].t[:], XSB.t[rows, 0:1024], [], [xs[b]], xs[b])
            K.dma("sp", z[b].t[:], PTOK.t[rows, PT_Z:PT_Z + 1024], [], [z[b]], z[b])
            K.dma("sp", yr[b].t[:], YSR.t[rows, :], [], [yr[b]], yr[b])
            K.dma("sp", rg[b].t[:], PTOK.t[rows, PT_RG:PT_RG + 512], [], [rg[b]], rg[b])
            K.dma("sp", od[b].t[:], OD.t[rows, :], [], [od[b]], od[b])
            K.dma("sp", xr[b].t[:], X.t[rows, :], [], [xr[b]], xr[b])

        def stA(c):
            b = c % 2
            K.begin_chain()
            K.op("pool", "tensor_tensor", [xs[b], dsk], [t[b]], out=t[b].t[:].rearrange("p (h q) -> p h q", q=64),
                 in0=xs[b].t[:].rearrange("p (h q) -> p h q", q=64), in1=bc(dsk.t[:].unsqueeze(2), [128, 16, 64]), op=ALU.mult)
            K.op("dve", "tensor_tensor", [ys[b], t[b]], [ys[b]], out=ys[b].t[:], in0=ys[b].t[:], in1=t[b].t[:], op=ALU.add)
            K.op("act", "activation", [z[b]], [z[b]], out=z[b].t[:], in_=z[b].t[:], func=AF.Silu)
            K.op("dve", "tensor_tensor", [ys[b], z[b]], [ys[b]], out=ys[b].t[:], in0=ys[b].t[:], in1=z[b].t[:], op=ALU.mult)
            K.op("pool", "tensor_tensor", [ys[b]], [t[b]], out=t[b].t[:], in0=ys[b].t[:], in1=ys[b].t[:], op=ALU.mult)
            K.op("dve", "reduce_sum", [t[b]], [st[b]], out=st[b].t[:, 0:1], in_=t[b].t[:], axis=AX.X)
            K.op("act", "activation", [st[b]], [st[b]], out=st[b].t[:, 0:1], in_=st[b].t[:, 0:1], func=AF.Sqrt, scale=1.0 / 1024, bias=EPS)
            K.op("dve", "reciprocal", [st[b]], [st[b]], out=st[b].t[:, 0:1], in_=st[b].t[:, 0:1])
            K.op("dve", "scalar_tensor_tensor", [ys[b], st[b], snw], [ycat[b]], out=ycat[b].t[:, 0:1024], in0=ys[b].t[:], scalar=st[b].t[:, 0:1],
                 in1=snw.t[:], op0=ALU.mult, op1=ALU.mult)
            K.begin_chain()
            odv = od[b].t[:].rearrange("p (h d) -> p h d", d=128)
            t5v = t5[b].t[:].rearrange("p (h d) -> p h d", d=128)
            K.op("pool", "tensor_tensor", [od[b]], [t5[b]], out=t5[b].t[:], in0=od[b].t[:], in1=od[b].t[:], op=ALU.mult)
            K.op("dve", "reduce_sum", [t5[b]], [std[b]], out=std[b].t[:, 4:8], in_=t5v, axis=AX.X)
            K.op("act", "activation", [std[b]], [std[b]], out=std[b].t[:, 4:8], in_=std[b].t[:, 4:8], func=AF.Sqrt, scale=1.0 / 128, bias=EPS)
            K.op("dve", "reciprocal", [std[b]], [std[b]], out=std[b].t[:, 4:8], in_=std[b].t[:, 4:8])
            K.op("dve", "tensor_tensor", [od[b], std[b]], [od[b]], out=odv, in0=odv, in1=bc(std[b].t[:, 4:8].unsqueeze(2), [128, 4, 128]), op=ALU.mult)
            K.op("dve", "scalar_tensor_tensor", [od[b], dnw], [ycat[b]], out=ycat[b].t[:, 1024:1536].rearrange("p (h d) -> p h d", d=128), in0=odv,
                 scalar=1.0 - lam_init, in1=bc(dnw.t[:].unsqueeze(1), [128, 4, 128]), op0=ALU.mult, op1=ALU.mult)
            K.begin_chain()
            yrv = yr[b].t[:].rearrange("p (h d) -> p h d", d=128)
            t6v = t6[b].t[:].rearrange("p (h d) -> p h d", d=128)
            K.op("dve", "reduce_sum", [yr[b]], [str_[b]], out=str_[b].t[:, 8:12], in_=yrv, axis=AX.X)
            K.op("dve", "tensor_scalar", [str_[b]], [str_[b]], out=str_[b].t[:, 8:12], in0=str_[b].t[:, 8:12], scalar1=1.0 / 128, scalar2=None, op0=ALU.mult)
            K.op("dve", "tensor_tensor", [yr[b], str_[b]], [yr[b]], out=yrv, in0=yrv, in1=bc(str_[b].t[:, 8:12].unsqueeze(2), [128, 4, 128]), op=ALU.subtract)
            K.op("pool", "tensor_tensor", [yr[b]], [t6[b]], out=t6[b].t[:], in0=yr[b].t[:], in1=yr[b].t[:], op=ALU.mult)
            K.op("dve", "reduce_sum", [t6[b]], [str_[b]], out=str_[b].t[:, 12:16], in_=t6v, axis=AX.X)
            K.op("act", "activation", [str_[b]], [str_[b]], out=str_[b].t[:, 12:16], in_=str_[b].t[:, 12:16], func=AF.Sqrt, scale=1.0 / 128, bias=EPS)
            K.op("dve", "reciprocal", [str_[b]], [str_[b]], out=str_[b].t[:, 12:16], in_=str_[b].t[:, 12:16])
            K.op("dve", "tensor_tensor", [yr[b], str_[b]], [yr[b]], out=yrv, in0=yrv, in1=bc(str_[b].t[:, 12:16].unsqueeze(2), [128, 4, 128]), op=ALU.mult)
            K.op("pool", "tensor_tensor", [yr[b], rnw], [yr[b]], out=yrv, in0=yrv, in1=bc(rnw.t[:].unsqueeze(1), [128, 4, 128]), op=ALU.mult)
            K.op("act", "activation", [rg[b]], [rg[b]], out=rg[b].t[:], in_=rg[b].t[:], func=AF.Silu)
            K.op("dve", "tensor_tensor", [yr[b], rg[b]], [ycat[b]], out=ycat[b].t[:, 1536:2048], in0=yr[b].t[:], in1=rg[b].t[:], op=ALU.mult)
            K.flush_chains()

        def stB(c):
            b = c % 2
            for half in range(2):
                for k8 in range(8):
                    kc = half * 8 + k8
                    K.op("pe", "transpose", [ycat[b], identb], [pt[half]], pt[half].t[:, k8, :], ycat[b].t[:, kc * 128:(kc + 1) * 128], identb.t[:])
                if half == 0:
                    K.op("act", "activation", [pt[half]], [ycT[b]], out=ycT[b].t[:, 0:8, :], in_=pt[half].t[:], func=AF.Copy)
                else:
                    K.op("dve", "tensor_copy", [pt[half]], [ycT[b]], out=ycT[b].t[:, 8:16, :], in_=pt[half].t[:])
            pys = [py[(c % 2) * 2], py[(c % 2) * 2 + 1]]
            for hf in range(2):
                for kc in range(16):
                    K.op("pe", "matmul", [ycT[b]] + wres, [pys[hf]], pys[hf].t[:], ycT[b].t[:, kc, :], W.t[:, kc, hf * 512:(hf + 1) * 512],
                         start=(kc == 0), stop=(kc == 15))

        def stC(c):
            b = c % 2
            src = 1 if c < NCTX // 128 else 0
            rows = slice(c * 128, (c + 1) * 128)
            pys = [py[(c % 2) * 2], py[(c % 2) * 2 + 1]]
            resid_ln(K, pys, xr[b], gate[src], lnw, lnb, r[b], t[b], st[b], ob[b])
            K.dma("sp", X1.t[rows, :], ob[b].t[:], [ob[b]], [], ob[b])

        loads(0)
        if NCH > 1:
            loads(1)
        stA(0)
        for c in range(NCH):
            stB(c)
            if c + 1 < NCH:
                stA(c + 1)
            stC(c)
            if c + 2 < NCH:
                loads(c + 2)
        K.end_phase("post%d" % li)


def phase_ffn_down(K, li, X1, MOD, UVT, P, w_down, XN, out_ap, consts):
    nc = K.nc
    NCH = K.NCH
    S = K.new_phase()
    NT = DFF // 128
    with ExitStack() as es:
        W = _sb(es, nc, "fd_w", [128, NT, 1024], BF16)
        gate = [_sb(es, nc, "fd_g%d" % i, [128, 1024], F32) for i in range(2)]
        lnw = _sb(es, nc, "fd_lnw", [128, 1024], F32)
        lnb = _sb(es, nc, "fd_lnb", [128, 1024], F32)
        cw = _sb(es, nc, "fd_cw", [128, NT, 3], F32)
        cb = _sb(es, nc, "fd_cb", [128, NT], F32)
        uh = [_sb(es, nc, "fd_uh%d" % i, [128, NT, 130], F32) for i in range(3)]
        vv = [_sb(es, nc, "fd_v%d" % i, [128, NT, 128], F32) for i in range(2)]
        acc = [_sb(es, nc, "fd_acc%d" % i, [128, NT, 128], F32) for i in range(2)]
        accsl = [[Buf(acc[i].t, "accsl") for ct in range(NT)] for i in range(2)]
        gl = [_sb(es, nc, "fd_gl", [128, NT, 128], F32)] * 2
        gT = [_sb(es, nc, "fd_gT%d" % i, [128, NT, 128], BF16) for i in range(2)]
        xr = [_sb(es, nc, "fd_xr%d" % i, [128, 1024], F32) for i in range(2)]
        r = [_sb(es, nc, "fd_r", [128, 1024], F32)] * 2
        t = [_sb(es, nc, "fd_t", [128, 1024], F32)] * 2
        st = [_sb(es, nc, "fd_st%d" % i, [128, 2], F32) for i in range(2)]
        ob = [_sb(es, nc, "fd_ob%d" % i, [128, 1024], F32) for i in range(2)]
        py = [_ps(es, nc, "fd_py%d" % i, [128, 512], F32) for i in range(4)]
        wres = load_weight_bf16(S, W, w_down[li], NT)
        for s_ in range(2):
            K.dma("sp", gate[s_].t[:], MOD.t[s_, :, 5120:6144], [], [gate[s_]], gate[s_])
        K.dma("sp", lnw.t[:], P["ln2_w"][li], [], [lnw], lnw)
        K.dma("sp", lnb.t[:], P["ln2_b"][li], [], [lnb], lnb)
        K.dma("sp", cw.t[:], P["ffn_cw"][li], [], [cw], cw)
        K.dma("sp", cb.t[:], P["ffn_cb"][li], [], [cb], cb)
        def loadsA(c):
            K.dma("sp", uh[c % 3].t[:, :, 1:129], UVT.t[c, :, 0:NT, :], [], [uh[c % 3]], uh[c % 3])
            K.dma("sp", vv[c % 2].t[:], UVT.t[c, :, NT:2 * NT, :], [], [vv[c % 2]], vv[c % 2])

        def loadsC(c):
            K.dma("sp", xr[c % 2].t[:], X1.t[c * 128:(c + 1) * 128, :], [], [xr[c % 2]], xr[c % 2])

        def stA(c):
            b = c % 2
            ub = uh[c % 3]
            lv = c not in (0, 2)
            rv = c not in (1, NCH - 1)
            if lv:
                K.op("pool", "tensor_copy", [uh[(c - 1) % 3]], [ub], out=ub.t[:, :, 0:1], in_=uh[(c - 1) % 3].t[:, :, 128:129])
            else:
                K.op("pool", "memset", [], [ub], ub.t[:, :, 0:1], 0.0)
            if rv:
                K.op("pool", "tensor_copy", [uh[(c + 1) % 3]], [ub], out=ub.t[:, :, 129:130], in_=uh[(c + 1) % 3].t[:, :, 1:2])
            else:
                K.op("pool", "memset", [], [ub], ub.t[:, :, 129:130], 0.0)
            asl = accsl[b]
            for ct in range(NT):
                K.op("act", "activation", [ub, cw, cb], [asl[ct]], out=acc[b].t[:, ct, :], in_=ub.t[:, ct, 0:128], func=AF.Identity,
                     scale=cw.t[:, ct, 0:1], bias=cb.t[:, ct:ct + 1])
            for j in range(1, 3):
                for ct in range(NT):
                    K.op("dve", "scalar_tensor_tensor", [ub, cw, asl[ct]], [asl[ct]], out=acc[b].t[:, ct, :], in0=ub.t[:, ct, j:j + 128],
                         scalar=cw.t[:, ct, j:j + 1], in1=acc[b].t[:, ct, :], op0=ALU.mult, op1=ALU.add)
            K.op("act", "activation", asl, [gl[b]], out=gl[b].t[:], in_=acc[b].t[:], func=AF.Gelu)
            K.op("pool", "tensor_tensor", [gl[b], vv[b]], [gT[b]], out=gT[b].t[:], in0=gl[b].t[:], in1=vv[b].t[:], op=ALU.mult)

        def stB(c):
            b = c % 2
            pys = [py[(c % 2) * 2], py[(c % 2) * 2 + 1]]
            for hf in range(2):
                for kc in range(NT):
                    K.op("pe", "matmul", [gT[b]] + wres, [pys[hf]], pys[hf].t[:], gT[b].t[:, kc, :], W.t[:, kc, hf * 512:(hf + 1) * 512],
                         start=(kc == 0), stop=(kc == NT - 1))

        def stC(c):
            b = c % 2
            src = 1 if c < NCTX // 128 else 0
            rows = slice(c * 128, (c + 1) * 128)
            pys = [py[(c % 2) * 2], py[(c % 2) * 2 + 1]]
            resid_ln(K, pys, xr[b], gate[src], lnw, lnb, r[b], t[b], st[b], ob[b])
            if out_ap is None:
                K.dma("sp", XN.t[rows, :], ob[b].t[:], [ob[b]], [], ob[b])
            elif c >= NCTX // 128:
                K.dma("sp", out_ap[c * 128 - NCTX:(c + 1) * 128 - NCTX, :], ob[b].t[:], [ob[b]], [], ob[b])

        loadsA(0)
        loadsC(0)
        if NCH > 1:
            loadsA(1)
            loadsC(1)
        stA(0)
        for c in range(NCH):
            stB(c)
            if c + 2 < NCH:
                loadsA(c + 2)
            if c + 1 < NCH:
                stA(c + 1)
            stC(c)
            if c + 2 < NCH:
                loadsC(c + 2)
        K.end_phase("ffndown%d" % li)

PARAM_SHAPES = {
    "ssd_cw": [128, 12, 5], "ssd_cb": [128, 12], "ssd_dtb": [128, 32], "ssd_alog": [128, 32],
    "ret_dec": [128, 8], "diff_lam": [128, 256], "ln1_w": [128, 1024], "ln1_b": [128, 1024],
    "ln2_w": [128, 1024], "ln2_b": [128, 1024], "ssd_nw": [128, 1024], "ssd_d": [128, 16],
    "ret_nw": [128, 128], "diff_nw": [128, 128], "ffn_cw": [128, 22, 3], "ffn_cb": [128, 22],
}
PHASES = ["mod", "inproj", "ssdprep", "scans", "retprep", "scanr", "attnprep", "attn", "post", "ffnup", "ffndown"]


def build_program(NL, depth, dbg=False, stop_after=None):
    nc = bass.Bass("TRN2", target_bir_lowering=False)
    K = Ctx(nc, NL, depth, dbg)
    T = K.T
    inp = {}

    def din(name, shape, dt=F32):
        inp[name] = nc.dram_tensor(name, list(shape), dt, kind="ExternalInput").ap()
        return inp[name]

    x_in = din("x", [NL, D])
    ctx_in = din("ctx", [NCTX, D])
    c_fm = din("c_fm", [128, 8])
    cc_fm = din("cc_fm", [128, 8])
    w_ada = din("w_ada", [depth, D, 6 * D])
    b_ada = din("b_ada", [depth, 6 * D])
    w_in = din("w_in", [depth, D, IN_COLS])
    w_out = din("w_out", [depth, 2 * D, D])
    w_up = din("ffn_w_up", [depth, D, 2 * DFF])
    w_down = din("ffn_w_down", [depth, DFF, D])
    P = {k: din("p_" + k, [depth] + v) for k, v in PARAM_SHAPES.items()}
    consts = {
        "identb": din("identb", [128, 128], BF16), "identf": din("identf", [128, 128]),
        "ones": din("ones", [128, 128]), "tri": din("tri", [2, 128, 128]), "stri": din("stri", [2, 128, 128]),
        "mask": din("mask", [2, 128, 128]), "ret_tab": din("ret_tab", [T, 128]), "diff_tab": din("diff_tab", [T, 128]),
    }
    out = nc.dram_tensor("out", [NL, D], F32, kind="ExternalOutput").ap()

    X = K.dram_t("X", [T, D], F32)
    X1 = K.dram_t("X1", [T, D], F32)
    MOD = K.dram_t("MOD", [2, 128, 6 * D], F32)
    PTOK = K.dram_t("PTOK", [T, PT_W], F32)
    XBCT = K.dram_t("XBCT", [K.NCH, 128, 12, 128], F32)
    XSB = K.dram_t("XSB", [T, 1280], BF16)
    BCT = K.dram_t("BCT", [512, T], BF16)
    DTLA = K.dram_t("DTLA", [T, 64], F32)
    RQKT = K.dram_t("RQKT", [64, 8, T], BF16)
    RTOK = K.dram_t("RTOK", [T, 768], BF16)
    YSS = K.dram_t("YSS", [T, 1024], F32)
    YSR = K.dram_t("YSR", [T, 512], F32)
    KT = K.dram_t("KT", [128, 4, T], BF16)
    VA = K.dram_t("VA", [4, 128, K.NCH, 128], BF16)
    QT = K.dram_t("QT", [128, 4, T], BF16)
    KMAX = K.dram_t("KMAX", [128, 8], F32)
    KM8 = K.dram_t("KM8", [8, 1], F32)
    NB = K.dram_t("NB", [128, 4], F32)
    OD = K.dram_t("OD", [T, 512], F32)
    UVT = K.dram_t("UVT", [K.NCH, 128, 44, 128], F32)

    with ExitStack() as es:
        K.eng_sem = {e: es.enter_context(nc.semaphore("s_" + e)) for e in ENGS}
        K.dma_sems = [es.enter_context(nc.semaphore("d%d" % i)) for i in range(64)]
        K.eng_base = {e: 0 for e in ENGS}
        K.dma_base = [0] * 64
        phase_init(K, x_in, ctx_in, X)
        done = False
        for li in range(depth):
            last = li == depth - 1
            steps = [
                ("mod", lambda: phase_mod(K, li, c_fm, cc_fm, w_ada, b_ada, MOD, consts)),
                ("inproj", lambda: phase_proj(K, "inproj%d" % li, X, MOD, 1024, 0, w_in[li], IN_COLS, TOK_GROUPS, PTOK, 1024, 12, XBCT, consts)),
                ("ssdprep", lambda: phase_ssd_prep(K, li, PTOK, XBCT, P, XSB, BCT, DTLA, consts)),
                ("scans", lambda: [phase_scan(K, li, fam_ssd(), d, {"BCT": BCT, "XSB": XSB, "DTLA": DTLA}, P, YSS, consts) for d in range(2)]),
                ("retprep", lambda: phase_ret_prep(K, li, PTOK, P, RQKT, RTOK, consts)),
                ("scanr", lambda: [phase_scan(K, li, fam_ret(), d, {"RQKT": RQKT, "RTOK": RTOK}, P, YSR, consts) for d in range(2)]),
                ("attnprep", lambda: phase_attn_prep(K, li, PTOK, KT, VA, QT, KMAX, KM8, NB, consts)),
                ("attn", lambda: phase_attn(K, li, KT, VA, QT, NB, P, OD, consts)),
                ("post", lambda: phase_post_outproj(K, li, X, MOD, PTOK, XSB, YSS, YSR, OD, P, w_out, X1, consts)),
                ("ffnup", lambda: phase_proj(K, "ffnup%d" % li, X1, MOD, 4096, 3072, w_up[li], 2 * DFF, [], None, 0, 44, UVT, consts)),
                ("ffndown", lambda: phase_ffn_down(K, li, X1, MOD, UVT, P, w_down, X, out if last else None, consts)),
            ]
            for nm, fn in steps:
                fn()
                if stop_after == nm:
                    done = True
                    break
            if done:
                break
    return nc, K


def _tables(T):
    f32 = np.float32
    n_lat = T - NCTX
    inv_ax = (f32(1.0) / (f32(10000.0) ** (np.arange(16, dtype=f32) / f32(16)))).astype(f32)
    i = np.arange(n_lat)
    row = (i // GRID_W).astype(f32)
    col = (i % GRID_W).astype(f32)
    ar = (row[:, None] * inv_ax[None, :]).astype(f32).astype(np.float64)
    ac = (col[:, None] * inv_ax[None, :]).astype(f32).astype(np.float64)
    dt = np.zeros((T, 128), f32)
    dt[:NCTX, 0:64] = 1.0
    dt[NCTX:, 0:64] = np.concatenate([np.cos(ar), np.cos(ar), np.cos(ac), np.cos(ac)], 1)
    dt[NCTX:, 64:128] = np.concatenate([-np.sin(ar), np.sin(ar), -np.sin(ac), np.sin(ac)], 1)
    inv_ret = (f32(1.0) / (f32(10000.0) ** np.linspace(0.0, 1.0, 32, dtype=f32))).astype(f32)
    pos = np.arange(T).astype(f32)
    a = (pos[:, None] * inv_ret[None, :]).astype(f32).astype(np.float64)
    rt = np.concatenate([np.cos(a), np.cos(a), -np.sin(a), np.sin(a)], 1).astype(f32)
    return dt, rt


def make_consts(T):
    f32 = np.float32
    t = np.arange(128)
    le = (t[:, None] <= t[None, :]).astype(f32)
    ge = (t[:, None] >= t[None, :]).astype(f32)
    gt = (t[:, None] > t[None, :]).astype(f32)
    lt = (t[:, None] < t[None, :]).astype(f32)
    dtab, rtab = _tables(T)
    return {
        "identb": np.eye(128, dtype=f32).astype(ml_dtypes.bfloat16), "identf": np.eye(128, dtype=f32),
        "ones": np.ones((128, 128), f32), "tri": np.stack([le, ge]), "stri": np.stack([gt, lt]),
        "mask": np.stack([le, ge]), "ret_tab": rtab, "diff_tab": dtab,
    }


def _rep(a, depth):
    a = np.asarray(a[:depth], np.float32).reshape(depth, 1, -1)
    return np.ascontiguousarray(np.broadcast_to(a, (depth, 128, a.shape[2])))


def make_in_maps(inputs, NL, depth, ncores):
    T = NL + NCTX
    cs = make_consts(T)
    L = depth
    shared = {
        "w_ada": np.ascontiguousarray(inputs["w_ada"][:L]), "b_ada": np.ascontiguousarray(inputs["b_ada"][:L]),
        "w_in": np.ascontiguousarray(inputs["w_in"][:L]), "w_out": np.ascontiguousarray(inputs["w_out"][:L]),
        "ffn_w_up": np.ascontiguousarray(inputs["ffn_w_up"][:L]), "ffn_w_down": np.ascontiguousarray(inputs["ffn_w_down"][:L]),
        "cc_fm": np.ascontiguousarray(inputs["c_ctx"].reshape(8, 128).T),
        "p_ssd_cw": np.ascontiguousarray(inputs["ssd_conv_w"][:L].reshape(L, 5, 12, 128).transpose(0, 3, 2, 1)),
        "p_ssd_cb": np.ascontiguousarray(inputs["ssd_conv_b"][:L].reshape(L, 12, 128).transpose(0, 2, 1)),
        "p_ssd_dtb": _rep(inputs["ssd_dt_bias"].reshape(-1, 32), L), "p_ssd_alog": _rep(inputs["ssd_a_log"].reshape(-1, 32), L),
        "p_ret_dec": _rep(inputs["ret_decay"].reshape(-1, 8), L), "p_diff_lam": _rep(inputs["diff_lambda"].reshape(-1, 256), L),
        "p_ln1_w": _rep(inputs["ln1_w"], L), "p_ln1_b": _rep(inputs["ln1_b"], L),
        "p_ln2_w": _rep(inputs["ln2_w"], L), "p_ln2_b": _rep(inputs["ln2_b"], L),
        "p_ssd_nw": _rep(inputs["ssd_norm_w"], L), "p_ssd_d": _rep(inputs["ssd_d"], L),
        "p_ret_nw": _rep(inputs["ret_norm_w"], L), "p_diff_nw": _rep(inputs["diff_norm_w"], L),
        "p_ffn_cw": np.ascontiguousarray(inputs["ffn_conv_w"][:L].reshape(L, 3, 22, 128).transpose(0, 3, 2, 1)),
        "p_ffn_cb": np.ascontiguousarray(inputs["ffn_conv_b"][:L].reshape(L, 22, 128).transpose(0, 2, 1)),
    }
    shared.update(cs)
    maps = []
    for b in range(ncores):
        m = dict(shared)
        m["x"] = np.ascontiguousarray(inputs["x"][b, :NL])
        m["ctx"] = np.ascontiguousarray(inputs["ctx"][b])
        m["c_fm"] = np.ascontiguousarray(inputs["c"][b].reshape(8, 128).T)
        maps.append(m)
    return maps


def kernel(**inputs):
    inputs = {k: np.asarray(v) for k, v in inputs.items()}
    NL = inputs["x"].shape[1]
    depth = inputs["w_ada"].shape[0]
    nb = inputs["x"].shape[0]
    nc, K = build_program(NL, depth)
    maps = make_in_maps(inputs, NL, depth, nb)
    res = run_bass_kernel_spmd(nc, maps, core_ids=list(range(nb)))
    return np.stack([np.asarray(r["out"], np.float32) for r in res.results], axis=0)
```

```python
import math
from contextlib import ExitStack

import numpy as np
import ml_dtypes
import concourse.bass as bass
import concourse.mybir as mybir
from concourse.bass_utils import run_bass_kernel_spmd

F32 = mybir.dt.float32
BF16 = mybir.dt.bfloat16
AF = mybir.ActivationFunctionType
ALU = mybir.AluOpType
AX = mybir.AxisListType

D = 1024
NCTX = 256
GRID_W = 64
IN_COLS = 5664
DFF = 2816
ENGS = ("pe", "act", "dve", "pool", "sp")


class Res:
    __slots__ = ("name", "last_w", "readers", "sem", "sem_cnt", "base", "sw")

    def __init__(self, name=""):
        self.name = name
        self.last_w = None
        self.readers = []
        self.sem = None
        self.sem_cnt = 0
        self.base = 0
        self.sw = False


class Ins:
    __slots__ = ("eng", "idx", "fn", "deps", "dma", "sem_res", "count", "signal")

    def __init__(self, eng, idx, fn, dma, sem_res):
        self.eng = eng
        self.idx = idx
        self.fn = fn
        self.deps = []
        self.dma = dma
        self.sem_res = sem_res
        self.count = None
        self.signal = False


class Sched:
    def __init__(self, nc, eng_sem, dma_sems, eng_base, dma_base):
        self.nc = nc
        self.lists = {e: [] for e in ENGS}
        self.dma_res = []
        self.eng_sem = eng_sem
        self.dma_sems = dma_sems
        self.eng_base = eng_base
        self.dma_base = dma_base

    def add(self, eng, fn, reads=(), writes=(), dma=False, sem_res=None):
        lst = self.lists[eng]
        ins = Ins(eng, len(lst), fn, dma, sem_res)
        deps = {}
        for r in reads:
            d = r.last_w
            if d is not None:
                deps[id(d)] = d
        for w in writes:
            d = w.last_w
            if d is not None:
                deps[id(d)] = d
            for rd in w.readers:
                deps[id(rd)] = rd
        for r in reads:
            r.readers.append(ins)
        for w in writes:
            w.last_w = ins
            w.readers = []
        best = {}
        out = []
        for d in deps.values():
            if d is ins:
                continue
            if d.dma:
                out.append(d)
            else:
                if d.eng == eng and not dma:
                    if eng == "pe":
                        continue
                    if d.idx < ins.idx - 3:
                        continue
                b = best.get(d.eng)
                if b is None or d.idx > b.idx:
                    best[d.eng] = d
        out.extend(best.values())
        ins.deps = out
        if dma:
            if sem_res.sem is None:
                sem_res.sem = len(self.dma_res)
                sem_res.sw = (eng == "pool")
                self.dma_res.append(sem_res)
            assert sem_res.sw == (eng == "pool")
            sem_res.sem_cnt += 1
            ins.count = sem_res.sem_cnt
        lst.append(ins)
        return ins

    def emit(self):
        nc = self.nc
        for e in ENGS:
            for ins in self.lists[e]:
                for d in ins.deps:
                    d.signal = True
            for ins in reversed(self.lists[e]):
                if not ins.dma:
                    ins.signal = True
                    break
        final = {}
        for e in ENGS:
            c = self.eng_base[e]
            for ins in self.lists[e]:
                if (not ins.dma) and ins.signal:
                    c += 1
                    ins.count = c
            final[e] = c
        nsw = 0
        nhw = 0
        idxs = []
        for r in self.dma_res:
            if r.sw:
                nsw += 1
                idxs.append(len(self.dma_sems) - nsw)
            else:
                idxs.append(nhw)
                nhw += 1
        assert nhw + 24 <= len(self.dma_sems) and nsw <= 24, (nhw, nsw)
        self._idxs = idxs
        for i, r in zip(idxs, self.dma_res):
            r.sem = self.dma_sems[i]
            r.base = self.dma_base[i]
        stats = {}

        def run_engine(e, eo):
            seen = {}
            nw = 0
            for ins in self.lists[e]:
                for d in ins.deps:
                    if d.dma:
                        key = ("d", id(d.sem_res))
                        val = d.sem_res.base + 16 * d.count
                        sem = d.sem_res.sem
                    else:
                        key = ("e", d.eng)
                        val = d.count
                        sem = self.eng_sem[d.eng]
                    if seen.get(key, 0) >= val:
                        continue
                    seen[key] = val
                    eo.wait_ge(sem, val)
                    nw += 1
                bi = ins.fn(eo)
                if ins.dma:
                    bi.then_inc(ins.sem_res.sem, 16)
                elif ins.signal:
                    bi.then_inc(self.eng_sem[e], 1)
            for r in self.dma_res:
                eo.wait_ge(r.sem, r.base + 16 * r.sem_cnt)
            for e2 in ENGS:
                if final[e2] > self.eng_base[e2]:
                    eo.wait_ge(self.eng_sem[e2], final[e2])
            stats[e] = (len(self.lists[e]), nw)

        with nc.Block() as block:
            @block.tensor
            def _(eo):
                run_engine("pe", eo)

            @block.scalar
            def _(eo):
                run_engine("act", eo)

            @block.vector
            def _(eo):
                run_engine("dve", eo)

            @block.gpsimd
            def _(eo):
                run_engine("pool", eo)

            @block.sync
            def _(eo):
                run_engine("sp", eo)
        for e in ENGS:
            self.eng_base[e] = final[e]
        for i, r in zip(self._idxs, self.dma_res):
            self.dma_base[i] = r.base + 16 * r.sem_cnt
        return stats


class Buf:
    __slots__ = ("t", "r")

    def __init__(self, t, name=""):
        self.t = t
        self.r = Res(name)


class Ctx:
    def __init__(self, nc, NL, depth, dbg):
        self.nc = nc
        self.NL = NL
        self.T = NL + NCTX
        self.NCH = self.T // 128
        self.depth = depth
        self.dbg = dbg
        self.dram = {}
        self.S = None
        self.stats = []

    def dram_t(self, name, shape, dt, out=False):
        kind = "ExternalOutput" if (out or self.dbg) else "Internal"
        t = self.nc.dram_tensor(name, list(shape), dt, kind=kind).ap()
        b = Buf(t, name)
        self.dram[name] = b
        return b

    _chains = None

    def op(self, eng, meth, reads, writes, *args, **kw):
        if self._chains is not None:
            self._chains[-1].append((eng, meth, list(reads), list(writes), args, kw))
            return None
        return self.S.add(eng, lambda e: getattr(e, meth)(*args, **kw),
                          reads=[b.r for b in reads], writes=[b.r for b in writes])

    def begin_chain(self):
        if self._chains is None:
            self._chains = []
        self._chains.append([])

    def flush_chains(self):
        chains, self._chains = self._chains, None
        n = max(len(c) for c in chains)
        for i in range(n):
            for c in chains:
                if i < len(c):
                    eng, meth, reads, writes, args, kw = c[i]
                    self.op(eng, meth, reads, writes, *args, **kw)

    def dma(self, eng, out, in_, reads, writes, semb):
        return self.S.add(eng, lambda e: e.dma_start(out=out, in_=in_),
                          reads=[b.r for b in reads], writes=[b.r for b in writes], dma=True, sem_res=semb.r)

    def new_phase(self):
        self.S = Sched(self.nc, self.eng_sem, self.dma_sems, self.eng_base, self.dma_base)
        return self.S

    def end_phase(self, name):
        st = self.S.emit()
        self.stats.append((name, st))
        self.S = None


_UID = [0]


def _sb(es, nc, name, shape, dt):
    _UID[0] += 1
    name = "%s_%d" % (name, _UID[0])
    return Buf(es.enter_context(nc.sbuf_tensor(name, list(shape), dt)), name)


def _ps(es, nc, name, shape, dt):
    _UID[0] += 1
    name = "%s_%d" % (name, _UID[0])
    return Buf(es.enter_context(nc.psum_tensor(name, list(shape), dt)), name)


def phase_init(K, x_in, ctx_in, X):
    S = K.new_phase()
    S.add("sp", lambda e: e.dma_start(out=X.t[0:NCTX, :], in_=ctx_in), writes=[X.r], dma=True, sem_res=X.r)
    r2 = Res("x2")
    nparts = 4
    step = K.NL // nparts
    for i in range(nparts):
        rr = Res("xi%d" % i)
        S.add("sp" if i % 2 == 0 else "act",
              lambda e, i=i: e.dma_start(out=X.t[NCTX + i * step:NCTX + (i + 1) * step, :],
                                         in_=x_in[i * step:(i + 1) * step, :]),
              writes=[rr], dma=True, sem_res=rr)
    K.end_phase("init")


def phase_mod(K, li, c_fm, cc_fm, w_ada, b_ada, MOD, consts):
    nc = K.nc
    S = K.new_phase()
    with ExitStack() as es:
        cin = _sb(es, nc, "m_cin", [128, 16], F32)
        sil = _sb(es, nc, "m_sil", [128, 16], F32)
        silb = _sb(es, nc, "m_silb", [128, 16, 128], F32)
        brow = _sb(es, nc, "m_brow", [1, 6144], F32)
        ones = _sb(es, nc, "m_ones", [1, 128], F32)
        wblk = [_sb(es, nc, "m_w%d" % i, [128, 8, 512], F32) for i in range(2)]
        stage = [_sb(es, nc, "m_st%d" % i, [128, 512], F32) for i in range(2)]
        ps = [_ps(es, nc, "m_ps%d" % i, [128, 512], F32) for i in range(2)]
        S.add("sp", lambda e: e.dma_start(out=cin.t[:, 0:8], in_=c_fm), writes=[cin.r], dma=True, sem_res=cin.r)
        S.add("sp", lambda e: e.dma_start(out=cin.t[:, 8:16], in_=cc_fm), writes=[cin.r], dma=True, sem_res=cin.r)
        S.add("sp", lambda e: e.dma_start(out=brow.t[:], in_=b_ada[li:li + 1, :]), writes=[brow.r], dma=True, sem_res=brow.r)
        S.add("dve", lambda e: e.memset(ones.t[:], 1.0), writes=[ones.r])
        S.add("act", lambda e: e.activation(out=sil.t[:], in_=cin.t[:], func=AF.Silu), reads=[cin.r], writes=[sil.r])
        S.add("dve", lambda e: e.tensor_copy(out=silb.t[:], in_=sil.t[:].unsqueeze(2).broadcast_to([128, 16, 128])),
              reads=[sil.r], writes=[silb.r])
        k = 0
        for j in range(12):
            wb = wblk[j % 2]
            S.add("sp", lambda e, wb=wb, j=j: e.dma_start(
                out=wb.t[:], in_=w_ada[li, :, j * 512:(j + 1) * 512].rearrange("(kc p) n -> p kc n", p=128)),
                writes=[wb.r], dma=True, sem_res=wb.r)
            for src in range(2):
                p = ps[k % 2]
                st = stage[k % 2]
                for kc in range(8):
                    S.add("pe", lambda e, p=p, wb=wb, kc=kc, src=src: e.matmul(
                        p.t[:], silb.t[:, src * 8 + kc, :], wb.t[:, kc, :], start=(kc == 0), stop=False),
                        reads=[silb.r, wb.r], writes=[p.r])
                S.add("pe", lambda e, p=p, j=j: e.matmul(
                    p.t[:], ones.t[0:1, :], brow.t[0:1, j * 512:(j + 1) * 512], start=False, stop=True),
                    reads=[ones.r, brow.r], writes=[p.r])
                addone = 1.0 if j in (2, 3, 8, 9) else 0.0
                S.add("dve", lambda e, p=p, st=st, addone=addone: e.tensor_scalar(
                    out=st.t[:], in0=p.t[:], scalar1=addone, scalar2=None, op0=ALU.add),
                    reads=[p.r], writes=[st.r])
                S.add("sp", lambda e, st=st, src=src, j=j: e.dma_start(
                    out=MOD.t[src, :, j * 512:(j + 1) * 512], in_=st.t[:]),
                    reads=[st.r], writes=[MOD.r], dma=True, sem_res=st.r)
                k += 1
        K.end_phase("mod%d" % li)


def load_weight_bf16(S, wdst, w_src_kcn, nkc, split=4):
    res = []
    for kc in range(nkc):
        wb = Buf(None, "w")
        S.add("pool", lambda e, kc=kc: e.dma_start(out=wdst.t[:, kc, :], in_=w_src_kcn[kc * 128:(kc + 1) * 128, :]),
              writes=[wb.r], dma=True, sem_res=wb.r)
        res.append(wb)
    return res


PT_Z, PT_DT, PT_DQ, PT_DK, PT_DV, PT_RQ, PT_RK, PT_RV, PT_RG, PT_W = 0, 1024, 1056, 1568, 2080, 2592, 2848, 3104, 3616, 4128
TOK_GROUPS = [
    (PT_Z, 0, 512), (PT_Z + 512, 512, 512), (PT_DT, 2560, 32),
    (PT_DQ, 2592, 512), (PT_DK, 3104, 512), (PT_DV, 3616, 512),
    (PT_RQ, 4128, 512), (PT_RV, 4640, 512), (PT_RG, 5152, 512),
]


def phase_proj(K, name, X, MOD, sc_col, sh_col, w_src, ncols, tok_groups, PTOK, fm_col0, n_fm, FMB, consts):
    nc = K.nc
    S = K.new_phase()
    NCH = K.NCH
    with ExitStack() as es:
        W = _sb(es, nc, "ip_w", [128, 8, ncols], BF16)
        identb = _sb(es, nc, "ip_id", [128, 128], BF16)
        modt = [[_sb(es, nc, "ip_mod%d%d" % (s_, q), [128, 1024], F32) for q in range(2)] for s_ in range(2)]
        xt = [_sb(es, nc, "ip_x%d" % i, [128, 1024], F32) for i in range(2)]
        xmb = [_sb(es, nc, "ip_xb%d" % i, [128, 1024], BF16) for i in range(2)]
        xT4 = [_sb(es, nc, "ip_xT%d" % i, [128, 8, 512], BF16) for i in range(2)]
        xslot = [[Buf(xT4[i].t, "slot") for j in range(4)] for i in range(2)]
        stg = [_sb(es, nc, "ip_st%d" % i, [128, PT_W if tok_groups else 8], F32) for i in range(2)]
        stf = [_sb(es, nc, "ip_sf%d" % i, [128, 4, 512], F32) for i in range(3)]
        pst = [_ps(es, nc, "ip_pt%d" % i, [128, 8, 128], BF16) for i in range(2)]
        psg = [_ps(es, nc, "ip_pg%d" % i, [128, 512], F32) for i in range(4 if tok_groups else 1)]
        psf = [_ps(es, nc, "ip_pf%d" % i, [128, 512], F32) for i in range(2 if tok_groups else 4)]
        wres = load_weight_bf16(S, W, w_src, 8)
        K.dma("sp", identb.t[:], consts["identb"], [], [identb], identb)
        for s_ in range(2):
            for q in range(2):
                c0 = sc_col if q == 0 else sh_col
                K.dma("sp", modt[s_][q].t[:], MOD.t[s_, :, c0:c0 + 1024], [], [modt[s_][q]], modt[s_][q])

        def loads(c):
            K.dma("sp", xt[c % 2].t[:], X.t[c * 128:(c + 1) * 128, :], [], [xt[c % 2]], xt[c % 2])

        gi = 0
        fi = 0
        ei = 0
        loads(0)
        blocks = [list(range(i, min(i + 4, NCH))) for i in range(0, NCH, 4)]
        for bi, blk in enumerate(blocks):
            xb = xT4[bi % 2]
            slots = xslot[bi % 2]
            for j, c in enumerate(blk):
                if c + 1 < NCH:
                    loads(c + 1)
                b = c % 2
                src = 1 if c < NCTX // 128 else 0
                sl = slots[j]
                K.op("dve", "tensor_tensor", [xt[b], modt[src][0]], [xt[b]], out=xt[b].t[:], in0=xt[b].t[:], in1=modt[src][0].t[:], op=ALU.mult)
                K.op("pool", "tensor_tensor", [xt[b], modt[src][1]], [xmb[b]], out=xmb[b].t[:], in0=xt[b].t[:], in1=modt[src][1].t[:], op=ALU.add)
                for kc in range(8):
                    K.op("pe", "transpose", [xmb[b], identb], [pst[b]], pst[b].t[:, kc, :], xmb[b].t[:, kc * 128:(kc + 1) * 128], identb.t[:])
                K.op("act", "activation", [pst[b]], [sl], out=xb.t[:, :, j * 128:(j + 1) * 128], in_=pst[b].t[:], func=AF.Copy)
                for (dc, sc, wd) in tok_groups:
                    p = psg[gi % len(psg)]
                    for kc in range(8):
                        K.op("pe", "matmul", [sl] + wres, [p], p.t[:, 0:wd], xb.t[:, kc, j * 128:(j + 1) * 128], W.t[:, kc, sc:sc + wd],
                             start=(kc == 0), stop=(kc == 7))
                    if gi % 2 == 0:
                        K.op("act", "activation", [p], [stg[b]], out=stg[b].t[:, dc:dc + wd], in_=p.t[:, 0:wd], func=AF.Copy)
                    else:
                        K.op("dve", "tensor_copy", [p], [stg[b]], out=stg[b].t[:, dc:dc + wd], in_=p.t[:, 0:wd])
                    gi += 1
                if tok_groups:
                    K.dma("sp", PTOK.t[c * 128:(c + 1) * 128, :], stg[b].t[:], [stg[b]], [], stg[b])
            BW = len(blk) * 128
            for ct in range(n_fm):
                p = psf[fi % len(psf)]
                fi += 1
                col = fm_col0 + ct * 128
                for kc in range(8):
                    K.op("pe", "matmul", slots[0:len(blk)] + wres, [p], p.t[:, 0:BW], W.t[:, kc, col:col + 128], xb.t[:, kc, 0:BW],
                         start=(kc == 0), stop=(kc == 7))
                sb = stf[(ct // 4 + bi * ((n_fm + 3) // 4)) % 3]
                if ei % 2 == 0:
                    K.op("dve", "tensor_copy", [p], [sb], out=sb.t[:, ct % 4, 0:BW], in_=p.t[:, 0:BW])
                else:
                    K.op("act", "activation", [p], [sb], out=sb.t[:, ct % 4, 0:BW], in_=p.t[:, 0:BW], func=AF.Copy)
                ei += 1
                if ct % 4 == 3:
                    for j, c in enumerate(blk):
                        K.dma("sp", FMB.t[c, :, ct - 3:ct + 1, :], sb.t[:, :, j * 128:(j + 1) * 128], [sb], [], sb)
        K.end_phase(name)


def bc(ap, shape):
    return ap.broadcast_to(list(shape))


def phase_ssd_prep(K, li, PTOK, XBCT, P, XSB, BCT, DTLA, consts):
    nc = K.nc
    S = K.new_phase()
    NCH = K.NCH
    with ExitStack() as es:
        identb = _sb(es, nc, "sp_id", [128, 128], BF16)
        cw = _sb(es, nc, "sp_cw", [128, 12, 5], F32)
        cb = _sb(es, nc, "sp_cb", [128, 12], F32)
        dtb = _sb(es, nc, "sp_dtb", [128, 32], F32)
        alog = _sb(es, nc, "sp_alog", [128, 32], F32)
        aneg = _sb(es, nc, "sp_aneg", [128, 32], F32)
        xh = [_sb(es, nc, "sp_xh%d" % i, [128, 12, 132], F32) for i in range(3)]
        acc = [_sb(es, nc, "sp_acc%d" % i, [128, 12, 128], F32) for i in range(2)]
        accsl = [[Buf(acc[i].t, "accsl") for ct in range(12)] for i in range(2)]
        xc = [_sb(es, nc, "sp_xc%d" % i, [128, 12, 128], BF16) for i in range(2)]
        tok = [_sb(es, nc, "sp_tok%d" % i, [128, 1280], BF16) for i in range(2)]
        dtr = [_sb(es, nc, "sp_dtr%d" % i, [128, 32], F32) for i in range(2)]
        dl = [_sb(es, nc, "sp_dl%d" % i, [128, 64], F32) for i in range(2)]
        pt = [_ps(es, nc, "sp_pt%d" % i, [128, 8, 128], BF16) for i in range(2)]
        pb = [_ps(es, nc, "sp_pb%d" % i, [128, 2, 128], BF16) for i in range(2)]
        K.dma("sp", identb.t[:], consts["identb"], [], [identb], identb)
        K.dma("sp", cw.t[:], P["ssd_cw"][li], [], [cw], cw)
        K.dma("sp", cb.t[:], P["ssd_cb"][li], [], [cb], cb)
        K.dma("sp", dtb.t[:], P["ssd_dtb"][li], [], [dtb], dtb)
        K.dma("sp", alog.t[:], P["ssd_alog"][li], [], [alog], alog)
        K.op("act", "activation", [alog], [aneg], out=aneg.t[:], in_=alog.t[:], func=AF.Exp)
        K.op("dve", "tensor_scalar", [aneg], [aneg], out=aneg.t[:], in0=aneg.t[:], scalar1=-1.0, scalar2=None, op0=ALU.mult)
        def loads(c):
            K.dma("sp", xh[c % 3].t[:, :, 2:130], XBCT.t[c], [], [xh[c % 3]], xh[c % 3])
            K.dma("sp", dtr[c % 2].t[:], PTOK.t[c * 128:(c + 1) * 128, PT_DT:PT_DT + 32], [], [dtr[c % 2]], dtr[c % 2])

        def stA(c):
            b = c % 2
            xb_ = xh[c % 3]
            lv = c not in (0, 2)
            rv = c not in (1, NCH - 1)
            if lv:
                K.op("pool", "tensor_copy", [xh[(c - 1) % 3]], [xb_], out=xb_.t[:, :, 0:2], in_=xh[(c - 1) % 3].t[:, :, 128:130])
            else:
                K.op("pool", "memset", [], [xb_], xb_.t[:, :, 0:2], 0.0)
            if rv:
                K.op("pool", "tensor_copy", [xh[(c + 1) % 3]], [xb_], out=xb_.t[:, :, 130:132], in_=xh[(c + 1) % 3].t[:, :, 2:4])
            else:
                K.op("pool", "memset", [], [xb_], xb_.t[:, :, 130:132], 0.0)
            asl = accsl[b]
            for ct in range(12):
                K.op("dve", "tensor_scalar", [xb_, cw, cb], [asl[ct]], out=acc[b].t[:, ct, :], in0=xb_.t[:, ct, 0:128],
                     scalar1=cw.t[:, ct, 0:1], scalar2=cb.t[:, ct:ct + 1], op0=ALU.mult, op1=ALU.add)
            for j in range(1, 5):
                for ct in range(12):
                    K.op("dve", "scalar_tensor_tensor", [xb_, cw, asl[ct]], [asl[ct]], out=acc[b].t[:, ct, :], in0=xb_.t[:, ct, j:j + 128],
                         scalar=cw.t[:, ct, j:j + 1], in1=acc[b].t[:, ct, :], op0=ALU.mult, op1=ALU.add)
            K.op("act", "activation", asl, [xc[b]], out=xc[b].t[:], in_=acc[b].t[:], func=AF.Silu)
            K.op("dve", "tensor_tensor", [dtr[b], dtb], [dtr[b]], out=dtr[b].t[:], in0=dtr[b].t[:], in1=dtb.t[:], op=ALU.add)
            K.op("act", "activation", [dtr[b]], [dtr[b]], out=dtr[b].t[:], in_=dtr[b].t[:], func=AF.Exp)
            K.op("act", "activation", [dtr[b]], [dl[b]], out=dl[b].t[:, 0:32], in_=dtr[b].t[:], func=AF.Ln, bias=1.0)
            K.op("dve", "tensor_tensor", [dl[b], aneg], [dl[b]], out=dl[b].t[:, 32:64], in0=dl[b].t[:, 0:32], in1=aneg.t[:], op=ALU.mult)
            K.dma("sp", DTLA.t[c * 128:(c + 1) * 128, :], dl[b].t[:], [dl[b]], [], dl[b])

        def stB(c):
            b = c % 2
            for ct in range(8):
                K.op("pe", "transpose", [xc[b], identb], [pt[b]], pt[b].t[:, ct, :], xc[b].t[:, ct, :], identb.t[:])
            for ct in range(2):
                K.op("pe", "transpose", [xc[b], identb], [pb[b]], pb[b].t[:, ct, :], xc[b].t[:, 8 + ct, :], identb.t[:])
            K.op("act", "activation", [pt[b]], [tok[b]], out=tok[b].t[:, 0:1024], in_=pt[b].t[:].rearrange("p a b -> p (a b)"), func=AF.Copy)
            K.op("dve", "tensor_copy", [pb[b]], [tok[b]], out=tok[b].t[:, 1024:1280], in_=pb[b].t[:].rearrange("p a b -> p (a b)"))
            K.dma("sp", XSB.t[c * 128:(c + 1) * 128, :], tok[b].t[:], [tok[b]], [], tok[b])
            K.dma("sp", BCT.t[:, c * 128:(c + 1) * 128].rearrange("(ct p) t -> p ct t", p=128), xc[b].t[:, 8:12, :], [xc[b]], [], xc[b])

        loads(0)
        if NCH > 1:
            loads(1)
        stA(0)
        for c in range(NCH):
            stB(c)
            if c + 2 < NCH:
                loads(c + 2)
            if c + 1 < NCH:
                stA(c + 1)
        K.end_phase("ssdprep%d" % li)


def rope_tok(K, x, tabs, o, t1, nmap, half, cs_off, sn_off):
    raise NotImplementedError


def phase_ret_prep(K, li, PTOK, P, RQKT, RTOK, consts):
    nc = K.nc
    S = K.new_phase()
    NCH = K.NCH
    with ExitStack() as es:
        identb = _sb(es, nc, "rp_id", [128, 128], BF16)
        qk = [_sb(es, nc, "rp_qk%d" % i, [128, 512], F32) for i in range(2)]
        v = [_sb(es, nc, "rp_v%d" % i, [128, 512], F32) for i in range(2)]
        tab = [_sb(es, nc, "rp_tab%d" % i, [128, 128], F32) for i in range(2)]
        t1 = [_sb(es, nc, "rp_t1%d" % i, [128, 512], F32) for i in range(2)]
        o = [_sb(es, nc, "rp_o%d" % i, [128, 512], F32) for i in range(2)]
        ob = [_sb(es, nc, "rp_ob%d" % i, [128, 512], BF16) for i in range(2)]
        tk = [_sb(es, nc, "rp_tk%d" % i, [128, 768], BF16) for i in range(2)]
        qT = [_sb(es, nc, "rp_qT%d" % i, [64, 8, 128], BF16) for i in range(2)]
        pt = [_ps(es, nc, "rp_pt%d" % i, [64, 8, 128], BF16) for i in range(2)]
        K.dma("sp", identb.t[:], consts["identb"], [], [identb], identb)
        def loads(c):
            b = c % 2
            K.dma("sp", qk[b].t[:], PTOK.t[c * 128:(c + 1) * 128, PT_RQ:PT_RQ + 512], [], [qk[b]], qk[b])
            K.dma("sp", v[b].t[:], PTOK.t[c * 128:(c + 1) * 128, PT_RV:PT_RV + 512], [], [v[b]], v[b])
            K.dma("sp", tab[b].t[:], consts["ret_tab"][c * 128:(c + 1) * 128, :], [], [tab[b]], tab[b])

        loads(0)
        for c in range(NCH):
            b = c % 2
            if c + 1 < NCH:
                loads(c + 1)
            xv = qk[b].t[:].rearrange("p (m two h) -> p m two h", two=2, h=32)
            t1v = t1[b].t[:].rearrange("p (m two h) -> p m two h", two=2, h=32)
            sn = tab[b].t[:, 64:128].rearrange("p (two h) -> p two h", two=2)
            K.op("pool", "tensor_tensor", [qk[b], tab[b]], [t1[b]], out=t1v[:, :, 0, :], in0=xv[:, :, 1, :],
                 in1=bc(sn[:, 0:1, :], [128, 8, 32]), op=ALU.mult)
            K.op("pool", "tensor_tensor", [qk[b], tab[b]], [t1[b]], out=t1v[:, :, 1, :], in0=xv[:, :, 0, :],
                 in1=bc(sn[:, 1:2, :], [128, 8, 32]), op=ALU.mult)
            K.op("dve", "tensor_tensor", [qk[b], tab[b]], [o[b]], out=o[b].t[:].rearrange("p (m d) -> p m d", d=64),
                 in0=qk[b].t[:].rearrange("p (m d) -> p m d", d=64), in1=bc(tab[b].t[:, 0:64].unsqueeze(1), [128, 8, 64]), op=ALU.mult)
            K.op("dve", "tensor_tensor", [o[b], t1[b]], [ob[b]], out=ob[b].t[:, 0:256], in0=o[b].t[:, 0:256], in1=t1[b].t[:, 0:256], op=ALU.add)
            K.op("dve", "tensor_tensor", [o[b], t1[b]], [o[b]], out=o[b].t[:, 256:512], in0=o[b].t[:, 256:512], in1=t1[b].t[:, 256:512], op=ALU.add)
            K.op("dve", "tensor_scalar", [o[b]], [ob[b]], out=ob[b].t[:, 256:512], in0=o[b].t[:, 256:512], scalar1=0.125, scalar2=None, op0=ALU.mult)
            for h in range(8):
                K.op("pe", "transpose", [ob[b], identb], [pt[b]], pt[b].t[:, h, :], ob[b].t[:, h * 64:(h + 1) * 64], identb.t[:])
            K.op("act", "activation", [pt[b]], [qT[b]], out=qT[b].t[:], in_=pt[b].t[:], func=AF.Copy)
            K.dma("sp", RQKT.t[:, :, c * 128:(c + 1) * 128], qT[b].t[:], [qT[b]], [], qT[b])
            K.op("act", "activation", [v[b]], [tk[b]], out=tk[b].t[:, 0:512], in_=v[b].t[:], func=AF.Copy)
            K.op("pool", "tensor_copy", [ob[b]], [tk[b]], out=tk[b].t[:, 512:768], in_=ob[b].t[:, 256:512])
            K.dma("sp", RTOK.t[c * 128:(c + 1) * 128, :], tk[b].t[:], [tk[b]], [], tk[b])
        K.end_phase("retprep%d" % li)


class Fam:
    pass


def fam_ssd():
    f = Fam()
    f.name = "ssd"; f.H = 16; f.G = 2; f.N = 128; f.P = 64; f.J = 8; f.units = [[0], [1]]; f.YW = 1024
    return f


def fam_ret():
    f = Fam()
    f.name = "ret"; f.H = 4; f.G = 4; f.N = 64; f.P = 128; f.J = 4; f.units = [[0, 1, 2, 3]]; f.YW = 512
    return f


def phase_scan(K, li, fam, d, srcs, P, YS, consts):
    nc = K.nc
    S = K.new_phase()
    NCH = K.NCH
    ssd = fam.name == "ssd"
    H, G, N, PP, J = fam.H, fam.G, fam.N, fam.P, fam.J
    NU = len(fam.units)
    GU = len(fam.units[0])
    order = list(range(NCH)) if d == 0 else [1, 0] + list(range(NCH - 1, 1, -1))
    endcol = 127 if d == 0 else 0
    with ExitStack() as es:
        tri = _sb(es, nc, "sc_tri", [128, 128], F32)
        stri = _sb(es, nc, "sc_stri", [128, 128], F32)
        mask = _sb(es, nc, "sc_mask", [128, 128], F32)
        ones = _sb(es, nc, "sc_ones", [128, 128], F32)
        K.dma("sp", tri.t[:], consts["tri"][d], [], [tri], tri)
        K.dma("sp", stri.t[:], consts["stri"][d], [], [stri], stri)
        K.dma("sp", mask.t[:], consts["mask"][d], [], [mask], mask)
        K.dma("sp", ones.t[:], consts["ones"], [], [ones], ones)
        qkT = [_sb(es, nc, "sc_qkT%d" % i, [N, 2 * G, 128], BF16) for i in range(2)]
        tokw = 1280 if ssd else 768
        tok = [_sb(es, nc, "sc_tok%d" % i, [128, tokw], BF16) for i in range(2)]
        la = [_sb(es, nc, "sc_la%d" % i, [128, 64], F32) for i in range(2)]
        ecum = [_sb(es, nc, "sc_ecum%d" % i, [128, 2 * H], F32) for i in range(2)]
        sm = [_sb(es, nc, "sc_sm%d" % i, [128, GU, 128], F32) for i in range(2)]
        rc = [_sb(es, nc, "sc_rc%d" % i, [128, J, 128], F32) for i in range(2)]
        E = [_sb(es, nc, "sc_E%d" % i, [128, J, 128], F32) for i in range(2)]
        M = [_sb(es, nc, "sc_M%d" % i, [128, J, 128], BF16) for i in range(2)]
        vd = [_sb(es, nc, "sc_vd%d" % i, [128, 512], BF16) for i in range(2)]
        vs = [_sb(es, nc, "sc_vs%d" % i, [128, 512], BF16) for i in range(2)]
        Y = [_sb(es, nc, "sc_Y%d" % i, [128, fam.YW], F32) for i in range(2)]
        Yp = [_sb(es, nc, "sc_Yp%d" % i, [128, fam.YW], F32) for i in range(2)]
        St = [_sb(es, nc, "sc_S%d" % u, [N, 512], F32) for u in range(NU)]
        Sb = [_sb(es, nc, "sc_Sb%d" % u, [N, 512], BF16) for u in range(NU)]
        lac = _sb(es, nc, "sc_lac", [128, 8], F32)
        p_sc = _ps(es, nc, "sc_psc", [128, GU, 128], F32)
        p_seg = _ps(es, nc, "sc_pseg", [128, J, 128], F32)
        p_cum = _ps(es, nc, "sc_pcum", [128, 2 * H], F32)
        p_yd = _ps(es, nc, "sc_pyd", [128, 512], F32)
        p_yo = _ps(es, nc, "sc_pyo", [128, 512], F32)
        p_st = _ps(es, nc, "sc_pst", [N, 512], F32)
        for u in range(NU):
            K.op("dve", "memset", [], [St[u]], St[u].t[:], 0.0)
            K.op("pool", "memset", [], [Sb[u]], Sb[u].t[:], 0.0)
        if not ssd:
            K.dma("sp", lac.t[:], P["ret_dec"][li], [], [lac], lac)
            K.op("act", "activation", [lac], [lac], out=lac.t[:], in_=lac.t[:], func=AF.Exp)
            K.op("dve", "tensor_scalar", [lac], [lac], out=lac.t[:], in0=lac.t[:], scalar1=-1.0, scalar2=None, op0=ALU.mult)
        def loads(ci):
            c = order[ci]
            b = ci % 2
            if ssd:
                K.dma("sp", qkT[b].t[:], srcs["BCT"].t[:, c * 128:(c + 1) * 128].rearrange("(ct p) t -> p ct t", p=128), [], [qkT[b]], qkT[b])
                K.dma("sp", tok[b].t[:], srcs["XSB"].t[c * 128:(c + 1) * 128, :], [], [tok[b]], tok[b])
                K.dma("sp", la[b].t[:], srcs["DTLA"].t[c * 128:(c + 1) * 128, :], [], [la[b]], la[b])
            else:
                K.dma("sp", qkT[b].t[:, 0:4, :], srcs["RQKT"].t[:, 4:8, c * 128:(c + 1) * 128], [], [qkT[b]], qkT[b])
                K.dma("sp", qkT[b].t[:, 4:8, :], srcs["RQKT"].t[:, 0:4, c * 128:(c + 1) * 128], [], [qkT[b]], qkT[b])
                K.dma("sp", tok[b].t[:], srcs["RTOK"].t[c * 128:(c + 1) * 128, :], [], [tok[b]], tok[b])
            if d == 1:
                K.dma("sp", Yp[b].t[:], YS.t[c * 128:(c + 1) * 128, :], [], [Yp[b]], Yp[b])

        loads(0)
        for ci, c in enumerate(order):
            b = ci % 2
            if ci + 1 < len(order):
                loads(ci + 1)
            if ssd:
                la_ap = la[b].t[:, 32 + d * 16:32 + d * 16 + 16]
                dt_ap = la[b].t[:, d * 16:d * 16 + 16]
                la_res = la[b]
                kT = lambda g: qkT[b].t[:, g, :]
                qT = lambda g: qkT[b].t[:, 2 + g, :]
                ktok = lambda g: tok[b].t[:, 1024 + g * 128:1024 + (g + 1) * 128]
            else:
                la_ap = lac.t[:, d * 4:d * 4 + 4]
                la_res = lac
                kT = lambda g: qkT[b].t[:, g, :]
                qT = lambda g: qkT[b].t[:, 4 + g, :]
                ktok = lambda g: tok[b].t[:, 512 + g * 64:512 + (g + 1) * 64]
            K.op("pe", "matmul", [tri, la_res], [p_cum], p_cum.t[:, 0:H], tri.t[:], la_ap, start=True, stop=True)
            K.op("pe", "matmul", [ones, la_res], [p_cum], p_cum.t[:, H:2 * H], ones.t[:], la_ap, start=True, stop=True)
            K.op("act", "activation", [p_cum], [ecum[b]], out=ecum[b].t[:], in_=p_cum.t[:], func=AF.Exp)
            for u, groups in enumerate(fam.units):
                h0 = u * J
                for gi, g in enumerate(groups):
                    K.op("pe", "matmul", [qkT[b]], [p_sc], p_sc.t[:, gi, :], kT(g), qT(g), start=True, stop=True)
                K.op("dve", "tensor_tensor", [p_sc, mask], [sm[b]], out=sm[b].t[:], in0=p_sc.t[:],
                     in1=bc(mask.t[:].unsqueeze(1), [128, GU, 128]), op=ALU.mult)
                Eb = E[b] if ssd else E[0]
                if ssd or ci == 0:
                    K.op("pool", "tensor_tensor", [la_res, tri], [rc[b]], out=rc[b].t[:],
                         in0=bc(la_ap[:, h0:h0 + J].unsqueeze(2), [128, J, 128]),
                         in1=bc(tri.t[:].unsqueeze(1), [128, J, 128]), op=ALU.mult)
                    for q4 in range(J // 4):
                        K.op("pe", "matmul", [stri, rc[b]], [p_seg], p_seg.t[:, q4 * 4:(q4 + 1) * 4, :], stri.t[:], rc[b].t[:, q4 * 4:(q4 + 1) * 4, :],
                             start=True, stop=True)
                    K.op("act", "activation", [p_seg], [Eb], out=Eb.t[:], in_=p_seg.t[:], func=AF.Exp)
                smv = bc(sm[b].t[:], [128, J, 128]) if GU == 1 else sm[b].t[:]
                K.op("dve", "tensor_tensor", [Eb, sm[b]], [M[b]], out=M[b].t[:], in0=Eb.t[:], in1=smv, op=ALU.mult)
                if ssd:
                    K.op("pool", "tensor_tensor", [tok[b], la[b]], [vd[b]], out=vd[b].t[:].rearrange("p (j q) -> p j q", q=PP),
                         in0=tok[b].t[:, u * 512:(u + 1) * 512].rearrange("p (j q) -> p j q", q=PP),
                         in1=bc(dt_ap[:, h0:h0 + J].unsqueeze(2), [128, J, PP]), op=ALU.mult)
                    vd_ap = vd[b].t[:]
                    vd_res = vd[b]
                else:
                    vd_ap = tok[b].t[:, 0:512]
                    vd_res = tok[b]
                K.op("dve", "tensor_tensor", [vd_res, Eb], [vs[b]], out=vs[b].t[:].rearrange("p (j q) -> p j q", q=PP),
                     in0=vd_ap.rearrange("p (j q) -> p j q", q=PP),
                     in1=bc(Eb.t[:, :, endcol:endcol + 1], [128, J, PP]), op=ALU.mult)
                for j in range(J):
                    K.op("pe", "matmul", [M[b], vd_res], [p_yd], p_yd.t[:, j * PP:(j + 1) * PP], M[b].t[:, j, :], vd_ap[:, j * PP:(j + 1) * PP],
                         start=True, stop=True)
                gw = 512 // GU
                for gi, g in enumerate(groups):
                    K.op("pe", "matmul", [qkT[b], Sb[u]], [p_yo], p_yo.t[:, gi * gw:(gi + 1) * gw], qT(g), Sb[u].t[:, gi * gw:(gi + 1) * gw],
                         start=True, stop=True)
                ysl = Y[b].t[:, u * 512:(u + 1) * 512]
                K.op("dve", "tensor_tensor", [p_yo, ecum[b]], [Y[b]], out=ysl.rearrange("p (j q) -> p j q", q=PP),
                     in0=p_yo.t[:].rearrange("p (j q) -> p j q", q=PP),
                     in1=bc(ecum[b].t[:, h0:h0 + J].unsqueeze(2), [128, J, PP]), op=ALU.mult)
                K.op("dve", "tensor_tensor", [p_yd, Y[b]], [Y[b]], out=ysl, in0=ysl, in1=p_yd.t[:], op=ALU.add)
                if d == 1:
                    K.op("pool", "tensor_tensor", [Yp[b], Y[b]], [Y[b]], out=ysl, in0=ysl, in1=Yp[b].t[:, u * 512:(u + 1) * 512], op=ALU.add)
                for gi, g in enumerate(groups):
                    K.op("pe", "matmul", [tok[b], vs[b]], [p_st], p_st.t[:, gi * gw:(gi + 1) * gw], ktok(g), vs[b].t[:, gi * gw:(gi + 1) * gw],
                         start=True, stop=True)
                K.op("pool", "tensor_tensor", [St[u], ecum[b]], [St[u]], out=St[u].t[:].rearrange("p (j q) -> p j q", q=PP),
                     in0=St[u].t[:].rearrange("p (j q) -> p j q", q=PP),
                     in1=bc(ecum[b].t[0:N, H + h0:H + h0 + J].unsqueeze(2), [N, J, PP]), op=ALU.mult)
                K.op("dve", "tensor_tensor", [St[u], p_st], [St[u]], out=St[u].t[:], in0=St[u].t[:], in1=p_st.t[:], op=ALU.add)
                K.op("act", "activation", [St[u]], [Sb[u]], out=Sb[u].t[:], in_=St[u].t[:], func=AF.Copy)
            K.dma("sp", YS.t[c * 128:(c + 1) * 128, :], Y[b].t[:], [Y[b]], [], Y[b])
        K.end_phase("scan_%s%d_%d" % (fam.name, li, d))


def rope_axial(K, x, tab, t1, o, nm):
    xv = x.t[:].rearrange("p (m hf two e) -> p m hf two e", hf=2, two=2, e=16)
    tv = t1.t[:].rearrange("p (m hf two e) -> p m hf two e", hf=2, two=2, e=16)
    sn = tab.t[:, 64:128].rearrange("p (hf two e) -> p hf two e", hf=2, two=2)
    K.op("pool", "tensor_tensor", [x, tab], [t1], out=tv[:, :, :, 0, :], in0=xv[:, :, :, 1, :],
         in1=bc(sn[:, :, 0, :].unsqueeze(1), [128, nm, 2, 16]), op=ALU.mult)
    K.op("pool", "tensor_tensor", [x, tab], [t1], out=tv[:, :, :, 1, :], in0=xv[:, :, :, 0, :],
         in1=bc(sn[:, :, 1, :].unsqueeze(1), [128, nm, 2, 16]), op=ALU.mult)
    K.op("dve", "tensor_tensor", [x, tab], [o], out=o.t[:].rearrange("p (m d) -> p m d", d=64),
         in0=x.t[:].rearrange("p (m d) -> p m d", d=64), in1=bc(tab.t[:, 0:64].unsqueeze(1), [128, nm, 64]), op=ALU.mult)
    K.op("dve", "tensor_tensor", [o, t1], [o], out=o.t[:], in0=o.t[:], in1=t1.t[:], op=ALU.add)


def phase_attn_prep(K, li, PTOK, KT, VA, QT, KMAX, KM8, NB, consts):
    nc = K.nc
    NCH = K.NCH
    S = K.new_phase()
    with ExitStack() as es:
        identb = _sb(es, nc, "ap_id", [128, 128], BF16)
        identf = _sb(es, nc, "ap_idf", [128, 128], F32)
        ones = _sb(es, nc, "ap_ones", [128, 128], F32)
        x = [_sb(es, nc, "ap_x%d" % i, [128, 512], F32) for i in range(2)]
        v = [_sb(es, nc, "ap_v%d" % i, [128, 512], F32) for i in range(2)]
        tab = [_sb(es, nc, "ap_tab%d" % i, [128, 128], F32) for i in range(2)]
        t1 = [_sb(es, nc, "ap_t1%d" % i, [128, 512], F32) for i in range(2)]
        o = [_sb(es, nc, "ap_o%d" % i, [128, 512], F32) for i in range(2)]
        sq = [_sb(es, nc, "ap_sq%d" % i, [128, 512], F32) for i in range(2)]
        ks = [_sb(es, nc, "ap_ks%d" % i, [128, 8], F32) for i in range(2)]
        ka = [_sb(es, nc, "ap_ka%d" % i, [128, 512], BF16) for i in range(2)]
        va = [_sb(es, nc, "ap_va%d" % i, [128, 4, 129], BF16) for i in range(2)]
        kT = [_sb(es, nc, "ap_kT%d" % i, [128, 4, 128], BF16) for i in range(2)]
        kmx = _sb(es, nc, "ap_kmx", [128, 8], F32)
        km2 = _sb(es, nc, "ap_km2", [8, 1], F32)
        dg = _sb(es, nc, "ap_dg", [8, 8], F32)
        kbc = _sb(es, nc, "ap_kbc", [128, 8], F32)
        pt = [_ps(es, nc, "ap_pt%d" % i, [128, 4, 128], BF16) for i in range(2)]
        pk = _ps(es, nc, "ap_pk", [128, 128], F32)
        K.dma("sp", identb.t[:], consts["identb"], [], [identb], identb)
        K.dma("sp", identf.t[:], consts["identf"], [], [identf], identf)
        K.dma("sp", ones.t[:], consts["ones"], [], [ones], ones)
        K.op("dve", "memset", [], [kmx], kmx.t[:], 0.0)
        for i in range(2):
            K.op("pool", "memset", [], [va[i]], va[i].t[:], 1.0)
        def loads(c):
            b = c % 2
            K.dma("sp", x[b].t[:], PTOK.t[c * 128:(c + 1) * 128, PT_DK:PT_DK + 512], [], [x[b]], x[b])
            K.dma("sp", v[b].t[:], PTOK.t[c * 128:(c + 1) * 128, PT_DV:PT_DV + 512], [], [v[b]], v[b])
            K.dma("sp", tab[b].t[:], consts["diff_tab"][c * 128:(c + 1) * 128, :], [], [tab[b]], tab[b])

        loads(0)
        for c in range(NCH):
            b = c % 2
            if c + 1 < NCH:
                loads(c + 1)
            rope_axial(K, x[b], tab[b], t1[b], o[b], 8)
            K.op("act", "activation", [o[b]], [ka[b]], out=ka[b].t[:], in_=o[b].t[:], func=AF.Copy)
            K.op("pool", "tensor_tensor", [o[b]], [sq[b]], out=sq[b].t[:], in0=o[b].t[:], in1=o[b].t[:], op=ALU.mult)
            K.op("dve", "reduce_sum", [sq[b]], [ks[b]], out=ks[b].t[:], in_=sq[b].t[:].rearrange("p (m d) -> p m d", d=64), axis=AX.X)
            K.op("dve", "tensor_tensor", [ks[b], kmx], [kmx], out=kmx.t[:], in0=kmx.t[:], in1=ks[b].t[:], op=ALU.max)
            for m in range(4):
                K.op("pe", "transpose", [ka[b], identb], [pt[b]], pt[b].t[:, m, :], ka[b].t[:, m * 128:(m + 1) * 128], identb.t[:])
            K.op("act", "activation", [pt[b]], [kT[b]], out=kT[b].t[:], in_=pt[b].t[:], func=AF.Copy)
            K.dma("sp", KT.t[:, :, c * 128:(c + 1) * 128], kT[b].t[:], [kT[b]], [], kT[b])
            K.op("dve", "tensor_copy", [v[b]], [va[b]], out=va[b].t[:, :, 0:128], in_=v[b].t[:].rearrange("p (h d) -> p h d", d=128))
            K.dma("sp", VA.t[:, :, c, :].rearrange("h p w -> p h w"), va[b].t[:, :, 0:128], [va[b]], [], va[b])
        K.op("pe", "transpose", [kmx, identf], [pk], pk.t[0:8, :], kmx.t[:], identf.t[:])
        K.op("dve", "reduce_max", [pk], [km2], out=km2.t[:], in_=pk.t[0:8, :], axis=AX.X)
        K.op("act", "activation", [km2], [km2], out=km2.t[:], in_=km2.t[:], func=AF.Sqrt)
        K.op("dve", "tensor_scalar", [km2, identf], [dg], out=dg.t[:], in0=identf.t[0:8, 0:8], scalar1=km2.t[:, 0:1], scalar2=None, op0=ALU.mult)
        K.op("pe", "matmul", [ones, dg], [pk], pk.t[:, 0:8], ones.t[0:8, :], dg.t[:], start=True, stop=True)
        K.op("dve", "tensor_copy", [pk], [kbc], out=kbc.t[:], in_=pk.t[:, 0:8])
        K.dma("sp", KMAX.t[:], kbc.t[:], [kbc], [], kbc)
        K.dma("sp", KM8.t[:], km2.t[:], [km2], [], km2)
        K.end_phase("attnprep1_%d" % li)
    S = K.new_phase()
    with ExitStack() as es:
        identb = _sb(es, nc, "aq_id", [128, 128], BF16)
        kbc = _sb(es, nc, "aq_kbc", [128, 8], F32)
        x = [_sb(es, nc, "aq_x%d" % i, [128, 512], F32) for i in range(2)]
        tab = [_sb(es, nc, "aq_tab%d" % i, [128, 128], F32) for i in range(2)]
        t1 = [_sb(es, nc, "aq_t1%d" % i, [128, 512], F32) for i in range(2)]
        o = [_sb(es, nc, "aq_o%d" % i, [128, 512], F32) for i in range(2)]
        sq = [_sb(es, nc, "aq_sq%d" % i, [128, 512], F32) for i in range(2)]
        qs = [_sb(es, nc, "aq_qs%d" % i, [128, 8], F32) for i in range(2)]
        qa = [_sb(es, nc, "aq_qa%d" % i, [128, 512], BF16) for i in range(2)]
        qT = [_sb(es, nc, "aq_qT%d" % i, [128, 4, 128], BF16) for i in range(2)]
        pt = [_ps(es, nc, "aq_pt%d" % i, [128, 4, 128], BF16) for i in range(2)]
        K.dma("sp", identb.t[:], consts["identb"], [], [identb], identb)
        identf = _sb(es, nc, "aq_idf", [128, 128], F32)
        ones = _sb(es, nc, "aq_ones", [128, 128], F32)
        qmx = _sb(es, nc, "aq_qmx", [128, 8], F32)
        km8 = _sb(es, nc, "aq_km8", [8, 1], F32)
        qm8 = _sb(es, nc, "aq_qm8", [8, 1], F32)
        dg = _sb(es, nc, "aq_dg", [8, 8], F32)
        nbt = _sb(es, nc, "aq_nb", [128, 4], F32)
        pk = _ps(es, nc, "aq_pk", [128, 128], F32)
        K.dma("sp", identf.t[:], consts["identf"], [], [identf], identf)
        K.dma("sp", ones.t[:], consts["ones"], [], [ones], ones)
        K.dma("sp", km8.t[:], KM8.t[:], [], [km8], km8)
        K.op("dve", "memset", [], [qmx], qmx.t[:], 0.0)
        def loads(c):
            b = c % 2
            K.dma("sp", x[b].t[:], PTOK.t[c * 128:(c + 1) * 128, PT_DQ:PT_DQ + 512], [], [x[b]], x[b])
            K.dma("sp", tab[b].t[:], consts["diff_tab"][c * 128:(c + 1) * 128, :], [], [tab[b]], tab[b])

        loads(0)
        for c in range(NCH):
            b = c % 2
            if c + 1 < NCH:
                loads(c + 1)
            rope_axial(K, x[b], tab[b], t1[b], o[b], 8)
            K.op("act", "activation", [o[b]], [qa[b]], out=qa[b].t[:], in_=o[b].t[:], func=AF.Copy)
            K.op("pool", "tensor_tensor", [o[b]], [sq[b]], out=sq[b].t[:], in0=o[b].t[:], in1=o[b].t[:], op=ALU.mult)
            K.op("dve", "reduce_sum", [sq[b]], [qs[b]], out=qs[b].t[:], in_=sq[b].t[:].rearrange("p (m d) -> p m d", d=64), axis=AX.X)
            K.op("dve", "tensor_tensor", [qs[b], qmx], [qmx], out=qmx.t[:], in0=qmx.t[:], in1=qs[b].t[:], op=ALU.max)
            for m in range(4):
                K.op("pe", "transpose", [qa[b], identb], [pt[b]], pt[b].t[:, m, :], qa[b].t[:, m * 128:(m + 1) * 128], identb.t[:])
            K.op("act", "activation", [pt[b]], [qT[b]], out=qT[b].t[:], in_=pt[b].t[:], func=AF.Copy)
            K.dma("sp", QT.t[:, :, c * 128:(c + 1) * 128], qT[b].t[:], [qT[b]], [], qT[b])
        K.op("pe", "transpose", [qmx, identf], [pk], pk.t[0:8, :], qmx.t[:], identf.t[:])
        K.op("dve", "reduce_max", [pk], [qm8], out=qm8.t[:], in_=pk.t[0:8, :], axis=AX.X)
        K.op("act", "activation", [qm8], [qm8], out=qm8.t[:], in_=qm8.t[:], func=AF.Sqrt)
        K.op("dve", "tensor_tensor", [qm8, km8], [qm8], out=qm8.t[:], in0=qm8.t[:], in1=km8.t[:], op=ALU.mult)
        K.op("dve", "tensor_scalar", [qm8, identf], [dg], out=dg.t[:], in0=identf.t[0:8, 0:8], scalar1=qm8.t[:, 0:1], scalar2=None, op0=ALU.mult)
        K.op("pe", "matmul", [ones, dg], [pk], pk.t[:, 0:8], ones.t[0:8, :], dg.t[:], start=True, stop=True)
        K.op("dve", "tensor_copy", [pk], [kbc], out=kbc.t[:], in_=pk.t[:, 0:8])
        pkv = kbc.t[:].rearrange("p (h m) -> p h m", m=2)
        K.op("dve", "tensor_tensor", [kbc], [nbt], out=nbt.t[:], in0=pkv[:, :, 0], in1=pkv[:, :, 1], op=ALU.max)
        K.op("dve", "tensor_scalar", [nbt], [nbt], out=nbt.t[:], in0=nbt.t[:], scalar1=-0.125, scalar2=None, op0=ALU.mult)
        K.dma("sp", NB.t[:], nbt.t[:], [nbt], [], nbt)
        K.end_phase("attnprep2_%d" % li)


def phase_attn(K, li, KT, VA, QT, NB, P, OD, consts):
    nc = K.nc
    NCH = K.NCH
    T = K.T
    S = K.new_phase()
    lam_init = 0.8 - 0.6 * math.exp(-0.3 * li)
    with ExitStack() as es:
        dl = _sb(es, nc, "at_dl", [128, 256], F32)
        dp = _sb(es, nc, "at_dp", [128, 128], F32)
        ds = _sb(es, nc, "at_ds", [128, 2], F32)
        nlam = _sb(es, nc, "at_nlam", [128, 1], F32)
        identf = _sb(es, nc, "at_idf", [128, 128], F32)
        ones = _sb(es, nc, "at_ones", [128, 128], F32)
        kt = [_sb(es, nc, "at_kt%d" % i, [128, T], BF16) for i in range(2)]
        nb = _sb(es, nc, "at_nb", [128, 4], F32)
        vs = [_sb(es, nc, "at_vs%d" % i, [128, NCH, 128], BF16) for i in range(2)]
        qt = [_sb(es, nc, "at_qt%d" % i, [128, 512], BF16) for i in range(2)]
        pT = [_sb(es, nc, "at_pT%d" % i, [128, 2, 512], BF16) for i in range(3)]
        lacc = [_sb(es, nc, "at_la%d" % i, [128, 2, 512], F32) for i in range(2)]
        rl = [_sb(es, nc, "at_rl%d" % i, [1, 2, 512], F32) for i in range(2)]
        bcs = [_sb(es, nc, "at_bc%d" % i, [128, 2, 512], F32) for i in range(2)]
        t0 = [_sb(es, nc, "at_t0%d" % i, [128, 512], F32) for i in range(2)]
        t1 = [_sb(es, nc, "at_t1%d" % i, [128, 512], F32) for i in range(2)]
        od = [_sb(es, nc, "at_od%d" % i, [128, 4, 128], F32) for i in range(2)]
        p_s = [_ps(es, nc, "at_ps%d" % i, [128, 2, 512], F32) for i in range(2)]
        p_o = [[_ps(es, nc, "at_po%d%d" % (i, m), [128, 512], F32) for m in range(2)] for i in range(2)]
        K.dma("sp", dl.t[:], P["diff_lam"][li], [], [dl], dl)
        K.dma("sp", nb.t[:], NB.t[:], [], [nb], nb)
        K.dma("sp", identf.t[:], consts["identf"], [], [identf], identf)
        K.dma("sp", ones.t[:], consts["ones"], [], [ones], ones)
        dlv = dl.t[:].rearrange("p (a two d) -> p a two d", a=2, two=2)
        K.op("dve", "tensor_tensor", [dl], [dp], out=dp.t[:].rearrange("p (a d) -> p a d", a=2), in0=dlv[:, :, 0, :], in1=dlv[:, :, 1, :], op=ALU.mult)
        K.op("dve", "reduce_sum", [dp], [ds], out=ds.t[:], in_=dp.t[:].rearrange("p (a d) -> p a d", a=2), axis=AX.X)
        K.op("act", "activation", [ds], [ds], out=ds.t[:], in_=ds.t[:], func=AF.Exp)
        K.op("dve", "scalar_tensor_tensor", [ds], [nlam], out=nlam.t[:], in0=ds.t[:, 1:2], scalar=-lam_init, in1=ds.t[:, 0:1],
             op0=ALU.add, op1=ALU.subtract)
        qtiles = [(0, NCTX, 0, NCTX // 128)] + [(q0, 512, 0, NCH) for q0 in range(NCTX, T, 512)]
        si = 0
        pi = 0
        ti = 0
        pending = []
        for h in range(4):
            hb = h % 2
            K.dma("sp", kt[hb].t[:], KT.t[:, h, :], [], [kt[hb]], kt[hb])
            K.dma("sp", vs[hb].t[:], VA.t[h], [], [vs[hb]], vs[hb])
            for (q0, QW, kc0, kc1) in qtiles:
                tb = ti % 2
                ti += 1
                nqb = QW // 128
                K.dma("sp", qt[tb].t[:, 0:QW], QT.t[:, h, q0:q0 + QW], [], [qt[tb]], qt[tb])
                po = p_o[tb]
                la_ = lacc[tb]
                kcs = list(range(kc0, kc1))
                bufs = {}

                def qk(kc):
                    nonlocal si, pi
                    ps = p_s[si % 2]
                    pt_ = pT[pi % 3]
                    si += 1
                    pi += 1
                    bufs[kc] = (ps, pt_)
                    for m in range(2):
                        K.op("pe", "matmul", [kt[hb], qt[tb]], [ps], ps.t[:, m, 0:QW], kt[hb].t[64 * m:64 * m + 64, kc * 128:(kc + 1) * 128],
                             qt[tb].t[64 * m:64 * m + 64, 0:QW], start=True, stop=True)
                    K.op("act", "activation", [ps, nb], [pt_], out=pt_.t[:, :, 0:QW], in_=ps.t[:, :, 0:QW], func=AF.Exp, scale=0.125,
                         bias=nb.t[:, h:h + 1])

                def av(kc):
                    ps, pt_ = bufs.pop(kc)
                    for m in range(2):
                        K.op("pe", "matmul", [pt_, vs[hb]], [po[m]], po[m].t[:, 0:QW], vs[hb].t[:, kc, :], pt_.t[:, m, 0:QW],
                             start=(kc == kc0), stop=(kc == kc1 - 1))
                    if kc == kc0:
                        K.op("dve", "tensor_copy", [pt_], [la_], out=la_.t[:, :, 0:QW], in_=pt_.t[:, :, 0:QW])
                    else:
                        K.op("dve", "tensor_tensor", [pt_, la_], [la_], out=la_.t[:, :, 0:QW], in0=la_.t[:, :, 0:QW], in1=pt_.t[:, :, 0:QW], op=ALU.add)

                qk(kcs[0])
                for i_, kc in enumerate(kcs):
                    if i_ + 1 < len(kcs):
                        qk(kcs[i_ + 1])
                    av(kc)
                    if i_ == 2 and pending:
                        pending.pop(0)()
                    if i_ == 5 and pending:
                        pending.pop(0)()
                while pending:
                    pending.pop(0)()

                def fin_a(tb=tb, QW=QW, la_=la_):
                    nonlocal si
                    ps = p_s[si % 2]
                    si += 1
                    for m in range(2):
                        K.op("pe", "matmul", [ones, la_], [ps], ps.t[0:1, m, 0:QW], ones.t[:, 0:1], la_.t[:, m, 0:QW], start=True, stop=True)
                    K.op("dve", "reciprocal", [ps], [rl[tb]], out=rl[tb].t[:, :, 0:QW], in_=ps.t[0:1, :, 0:QW])
                    K.op("dve", "tensor_scalar", [rl[tb], nlam], [rl[tb]], out=rl[tb].t[:, 1, 0:QW], in0=rl[tb].t[:, 1, 0:QW], scalar1=nlam.t[0:1, 0:1], scalar2=None, op0=ALU.mult)
                    ps2 = p_s[si % 2]
                    si += 1
                    for m in range(2):
                        K.op("pe", "matmul", [ones, rl[tb]], [ps2], ps2.t[:, m, 0:QW], ones.t[0:1, :], rl[tb].t[0:1, m, 0:QW], start=True, stop=True)
                    K.op("act", "activation", [ps2], [bcs[tb]], out=bcs[tb].t[:, :, 0:QW], in_=ps2.t[:, :, 0:QW], func=AF.Copy)

                def fin_b(tb=tb, QW=QW, po=po, nqb=nqb, q0=q0, h=h):
                    nonlocal si
                    K.op("dve", "tensor_tensor", [po[0], bcs[tb]], [t0[tb]], out=t0[tb].t[:, 0:QW], in0=po[0].t[:, 0:QW], in1=bcs[tb].t[:, 0, 0:QW], op=ALU.mult)
                    K.op("dve", "tensor_tensor", [po[1], bcs[tb]], [t1[tb]], out=t1[tb].t[:, 0:QW], in0=po[1].t[:, 0:QW], in1=bcs[tb].t[:, 1, 0:QW], op=ALU.mult)
                    K.op("pool", "tensor_tensor", [t0[tb], t1[tb]], [t0[tb]], out=t0[tb].t[:, 0:QW], in0=t0[tb].t[:, 0:QW], in1=t1[tb].t[:, 0:QW], op=ALU.add)
                    ps3 = p_s[si % 2]
                    si += 1
                    for qb in range(nqb):
                        K.op("pe", "transpose", [t0[tb], identf], [ps3], ps3.t[:, 0, qb * 128:(qb + 1) * 128], t0[tb].t[:, qb * 128:(qb + 1) * 128], identf.t[:])
                    K.op("act", "activation", [ps3], [od[tb]], out=od[tb].t[:, 0:nqb, :], in_=ps3.t[:, 0, 0:QW].rearrange("p (a b) -> p a b", b=128), func=AF.Copy)
                    K.dma("sp", OD.t[q0:q0 + QW, h * 128:(h + 1) * 128].rearrange("(qb p) d -> p qb d", p=128), od[tb].t[:, 0:nqb, :], [od[tb]], [], od[tb])

                pending.extend([fin_a, fin_b])
        while pending:
            pending.pop(0)()
        K.end_phase("attn%d" % li)


ALPHA = (2.0 * 2) ** 0.25
EPS = 1e-5


def resid_ln(K, ps_halves, xres, gate, lnw, lnb, r, tmp, st, out_buf):
    for hf in range(2):
        K.op("dve", "tensor_tensor", [ps_halves[hf], gate], [r], out=r.t[:, hf * 512:(hf + 1) * 512], in0=ps_halves[hf].t[:],
             in1=gate.t[:, hf * 512:(hf + 1) * 512], op=ALU.mult)
    K.op("dve", "scalar_tensor_tensor", [xres, r], [r], out=r.t[:], in0=xres.t[:], scalar=ALPHA, in1=r.t[:], op0=ALU.mult, op1=ALU.add)
    K.op("dve", "reduce_sum", [r], [st], out=st.t[:, 0:1], in_=r.t[:], axis=AX.X)
    K.op("dve", "tensor_scalar", [st], [st], out=st.t[:, 0:1], in0=st.t[:, 0:1], scalar1=-1.0 / 1024, scalar2=None, op0=ALU.mult)
    K.op("act", "activation", [r, st], [r], out=r.t[:], in_=r.t[:], func=AF.Identity, bias=st.t[:, 0:1])
    K.op("act", "activation", [r], [tmp], out=tmp.t[:], in_=r.t[:], func=AF.Square)
    K.op("dve", "reduce_sum", [tmp], [st], out=st.t[:, 1:2], in_=tmp.t[:], axis=AX.X)
    K.op("act", "activation", [st], [st], out=st.t[:, 1:2], in_=st.t[:, 1:2], func=AF.Sqrt, scale=1.0 / 1024, bias=EPS)
    K.op("dve", "reciprocal", [st], [st], out=st.t[:, 1:2], in_=st.t[:, 1:2])
    K.op("dve", "scalar_tensor_tensor", [r, st, lnw], [tmp], out=tmp.t[:], in0=r.t[:], scalar=st.t[:, 1:2], in1=lnw.t[:], op0=ALU.mult, op1=ALU.mult)
    K.op("pool", "tensor_tensor", [tmp, lnb], [out_buf], out=out_buf.t[:], in0=tmp.t[:], in1=lnb.t[:], op=ALU.add)


def phase_post_outproj(K, li, X, MOD, PTOK, XSB, YSS, YSR, OD, P, w_out, X1, consts):
    nc = K.nc
    NCH = K.NCH
    S = K.new_phase()
    lam_init = 0.8 - 0.6 * math.exp(-0.3 * li)
    with ExitStack() as es:
        W = _sb(es, nc, "po_w", [128, 16, 1024], BF16)
        identb = _sb(es, nc, "po_id", [128, 128], BF16)
        gate = [_sb(es, nc, "po_g%d" % i, [128, 1024], F32) for i in range(2)]
        lnw = _sb(es, nc, "po_lnw", [128, 1024], F32)
        lnb = _sb(es, nc, "po_lnb", [128, 1024], F32)
        snw = _sb(es, nc, "po_snw", [128, 1024], F32)
        dsk = _sb(es, nc, "po_dsk", [128, 16], F32)
        rnw = _sb(es, nc, "po_rnw", [128, 128], F32)
        dnw = _sb(es, nc, "po_dnw", [128, 128], F32)
        ys = [_sb(es, nc, "po_ys%d" % i, [128, 1024], F32) for i in range(2)]
        xs = [_sb(es, nc, "po_xs%d" % i, [128, 1024], BF16) for i in range(2)]
        z = [_sb(es, nc, "po_z%d" % i, [128, 1024], F32) for i in range(2)]
        t = [_sb(es, nc, "po_t%d" % i, [128, 1024], F32) for i in range(2)]
        yr = [_sb(es, nc, "po_yr%d" % i, [128, 512], F32) for i in range(2)]
        rg = [_sb(es, nc, "po_rg%d" % i, [128, 512], F32) for i in range(2)]
        od = [_sb(es, nc, "po_od%d" % i, [128, 512], F32) for i in range(2)]
        t5 = [_sb(es, nc, "po_t5%d" % i, [128, 512], F32) for i in range(2)]
        t6 = [_sb(es, nc, "po_t6%d" % i, [128, 512], F32) for i in range(2)]
        st = [_sb(es, nc, "po_st%d" % i, [128, 16], F32) for i in range(2)]
        std = [_sb(es, nc, "po_std%d" % i, [128, 16], F32) for i in range(2)]
        str_ = [_sb(es, nc, "po_str%d" % i, [128, 16], F32) for i in range(2)]
        ycat = [_sb(es, nc, "po_yc%d" % i, [128, 2048], BF16) for i in range(2)]
        ycT = [_sb(es, nc, "po_ycT%d" % i, [128, 16, 128], BF16) for i in range(2)]
        xr = [_sb(es, nc, "po_xr%d" % i, [128, 1024], F32) for i in range(2)]
        r = [_sb(es, nc, "po_r%d" % i, [128, 1024], F32) for i in range(2)]
        ob = [_sb(es, nc, "po_ob%d" % i, [128, 1024], F32) for i in range(2)]
        pt = [_ps(es, nc, "po_pt%d" % i, [128, 8, 128], BF16) for i in range(2)]
        py = [_ps(es, nc, "po_py%d" % i, [128, 512], F32) for i in range(4)]
        wres = load_weight_bf16(S, W, w_out[li], 16)
        K.dma("sp", identb.t[:], consts["identb"], [], [identb], identb)
        for s_ in range(2):
            K.dma("sp", gate[s_].t[:], MOD.t[s_, :, 2048:3072], [], [gate[s_]], gate[s_])
        K.dma("sp", lnw.t[:], P["ln1_w"][li], [], [lnw], lnw)
        K.dma("sp", lnb.t[:], P["ln1_b"][li], [], [lnb], lnb)
        K.dma("sp", snw.t[:], P["ssd_nw"][li], [], [snw], snw)
        K.dma("sp", dsk.t[:], P["ssd_d"][li], [], [dsk], dsk)
        K.dma("sp", rnw.t[:], P["ret_nw"][li], [], [rnw], rnw)
        K.dma("sp", dnw.t[:], P["diff_nw"][li], [], [dnw], dnw)
        def loads(c):
            b = c % 2
            rows = slice(c * 128, (c + 1) * 128)
            K.dma("sp", ys[b].t[:], YSS.t[rows, :], [], [ys[b]], ys[b])
            K.dma("sp", xs[b].t[:], XSB.t[rows, 0:1024], [], [xs[b]], xs[b])
            K.dma("sp", z[b].t[:], PTOK.t[rows, PT_Z:PT_Z + 1024], [], [z[b]], z[b])
            K.dma("sp", yr[b].t[:], YSR.t[rows, :], [], [yr[b]], yr[b])
            K.dma("sp", rg[b].t[:], PTOK.t[rows, PT_RG:PT_RG + 512], [], [rg[b]], rg[b])
            K.dma("sp", od[b].t[:], OD.t[rows, :], [], [od[b]], od[b])
            K.dma("sp", xr[b].t[:], X.t[rows, :], [], [xr[b]], xr[b])

        def stA(c):
            b = c % 2
            K.begin_chain()
            K.op("pool", "tensor_tensor", [xs[b], dsk], [t[b]], out=t[b].t[:].rearrange("p (h q) -> p h q", q=64),
                 in0=xs[b].t[:].rearrange("p (h q) -> p h q", q=64), in1=bc(dsk.t[:].unsqueeze(2), [128, 16, 64]), op=ALU.mult)
            K.op("dve", "tensor_tensor", [ys[b], t[b]], [ys[b]], out=ys[b].t[:], in0=ys[b].t[:], in1=t[b].t[:], op=ALU.add)
            K.op("act", "activation", [z[b]], [z[b]], out=z[b].t[:], in_=z[b].t[:], func=AF.Silu)
            K.op("dve", "tensor_tensor", [ys[b], z[b]], [ys[b]], out=ys[b].t[:], in0=ys[b].t[:], in1=z[b].t[:], op=ALU.mult)
            K.op("pool", "tensor_tensor", [ys[b]], [t[b]], out=t[b].t[:], in0=ys[b].t[:], in1=ys[b].t[:], op=ALU.mult)
            K.op("dve", "reduce_sum", [t[b]], [st[b]], out=st[b].t[:, 0:1], in_=t[b].t[:], axis=AX.X)
            K.op("act", "activation", [st[b]], [st[b]], out=st[b].t[:, 0:1], in_=st[b].t[:, 0:1], func=AF.Sqrt, scale=1.0 / 1024, bias=EPS)
            K.op("dve", "reciprocal", [st[b]], [st[b]], out=st[b].t[:, 0:1], in_=st[b].t[:, 0:1])
            K.op("dve", "scalar_tensor_tensor", [ys[b], st[b], snw], [ycat[b]], out=ycat[b].t[:, 0:1024], in0=ys[b].t[:], scalar=st[b].t[:, 0:1],
                 in1=snw.t[:], op0=ALU.mult, op1=ALU.mult)
            K.begin_chain()
            odv = od[b].t[:].rearrange("p (h d) -> p h d", d=128)
            t5v = t5[b].t[:].rearrange("p (h d) -> p h d", d=128)
            K.op("pool", "tensor_tensor", [od[b]], [t5[b]], out=t5[b].t[:], in0=od[b].t[:], in1=od[b].t[:], op=ALU.mult)
            K.op("dve", "reduce_sum", [t5[b]], [std[b]], out=std[b].t[:, 4:8], in_=t5v, axis=AX.X)
            K.op("act", "activation", [std[b]], [std[b]], out=std[b].t[:, 4:8], in_=std[b].t[:, 4:8], func=AF.Sqrt, scale=1.0 / 128, bias=EPS)
            K.op("dve", "reciprocal", [std[b]], [std[b]], out=std[b].t[:, 4:8], in_=std[b].t[:, 4:8])
            K.op("dve", "tensor_tensor", [od[b], std[b]], [od[b]], out=odv, in0=odv, in1=bc(std[b].t[:, 4:8].unsqueeze(2), [128, 4, 128]), op=ALU.mult)
            K.op("dve", "scalar_tensor_tensor", [od[b], dnw], [ycat[b]], out=ycat[b].t[:, 1024:1536].rearrange("p (h d) -> p h d", d=128), in0=odv,
                 scalar=1.0 - lam_init, in1=bc(dnw.t[:].unsqueeze(1), [128, 4, 128]), op0=ALU.mult, op1=ALU.mult)
            K.begin_chain()
            yrv = yr[b].t[:].rearrange("p (h d) -> p h d", d=128)
            t6v = t6[b].t[:].rearrange("p (h d) -> p h d", d=128)
            K.op("dve", "reduce_sum", [yr[b]], [str_[b]], out=str_[b].t[:, 8:12], in_=yrv, axis=AX.X)
            K.op("dve", "tensor_scalar", [str_[b]], [str_[b]], out=str_[b].t[:, 8:12], in0=str_[b].t[:, 8:12], scalar1=1.0 / 128, scalar2=None, op0=ALU.mult)
            K.op("dve", "tensor_tensor", [yr[b], str_[b]], [yr[b]], out=yrv, in0=yrv, in1=bc(str_[b].t[:, 8:12].unsqueeze(2), [128, 4, 128]), op=ALU.subtract)
            K.op("pool", "tensor_tensor", [yr[b]], [t6[b]], out=t6[b].t[:], in0=yr[b].t[:], in1=yr[b].t[:], op=ALU.mult)
            K.op("dve", "reduce_sum", [t6[b]], [str_[b]], out=str_[b].t[:, 12:16], in_=t6v, axis=AX.X)
            K.op("act", "activation", [str_[b]], [str_[b]], out=str_[b].t[:, 12:16], in_=str_[b].t[:, 12:16], func=AF.Sqrt, scale=1.0 / 128, bias=EPS)
            K.op("dve", "reciprocal", [str_[b]], [str_[b]], out=str_[b].t[:, 12:16], in_=str_[b].t[:, 12:16])
            K.op("dve", "tensor_tensor", [yr[b], str_[b]], [yr[b]], out=yrv, in0=yrv, in1=bc(str_[b].t[:, 12:16].unsqueeze(2), [128, 4, 128]), op=ALU.mult)
            K.op("pool", "tensor_tensor", [yr[b], rnw], [yr[b]], out=yrv, in0=yrv, in1=bc(rnw.t[:].unsqueeze(1), [128, 4, 128]), op=ALU.mult)
            K.op("act", "activation", [rg[b]], [rg[b]], out=rg[b].t[:], in_=rg[b].t[:], func=AF.Silu)
            K.op("dve", "tensor_tensor", [yr[b], rg[b]], [ycat[b]], out=ycat[b].t[:, 1536:2048], in0=yr[b].t[:], in1=rg[b].t[:], op=ALU.mult)
            K.flush_chains()

        def stB(c):
            b = c % 2
            for half in range(2):
                for k8 in range(8):
                    kc = half * 8 + k8
                    K.op("pe", "transpose", [ycat[b], identb], [pt[half]], pt[half].t[:, k8, :], ycat[b].t[:, kc * 128:(kc + 1) * 128], identb.t[:])
                if half == 0:
                    K.op("act", "activation", [pt[half]], [ycT[b]], out=ycT[b].t[:, 0:8, :], in_=pt[half].t[:], func=AF.Copy)
                else:
                    K.op("dve", "tensor_copy", [pt[half]], [ycT[b]], out=ycT[b].t[:, 8:16, :], in_=pt[half].t[:])
            pys = [py[(c % 2) * 2], py[(c % 2) * 2 + 1]]
            for hf in range(2):
                for kc in range(16):
                    K.op("pe", "matmul", [ycT[b]] + wres, [pys[hf]], pys[hf].t[:], ycT[b].t[:, kc, :], W.t[:, kc, hf * 512:(hf + 1) * 512],
                         start=(kc == 0), stop=(kc == 15))

        def stC(c):
            b = c % 2
            src = 1 if c < NCTX // 128 else 0
            rows = slice(c * 128, (c + 1) * 128)
            pys = [py[(c % 2) * 2], py[(c % 2) * 2 + 1]]
            resid_ln(K, pys, xr[b], gate[src], lnw, lnb, r[b], t[b], st[b], ob[b])
            K.dma("sp", X1.t[rows, :], ob[b].t[:], [ob[b]], [], ob[b])

        loads(0)
        if NCH > 1:
            loads(1)
        stA(0)
        for c in range(NCH):
            stB(c)
            if c + 1 < NCH:
                stA(c + 1)
            stC(c)
            if c + 2 < NCH:
                loads(c + 2)
        K.end_phase("post%d" % li)


def phase_ffn_down(K, li, X1, MOD, UVT, P, w_down, XN, out_ap, consts):
    nc = K.nc
    NCH = K.NCH
    S = K.new_phase()
    NT = DFF // 128
    with ExitStack() as es:
        W = _sb(es, nc, "fd_w", [128, NT, 1024], BF16)
        gate = [_sb(es, nc, "fd_g%d" % i, [128, 1024], F32) for i in range(2)]
        lnw = _sb(es, nc, "fd_lnw", [128, 1024], F32)
        lnb = _sb(es, nc, "fd_lnb", [128, 1024], F32)
        cw = _sb(es, nc, "fd_cw", [128, NT, 3], F32)
        cb = _sb(es, nc, "fd_cb", [128, NT], F32)
        uh = [_sb(es, nc, "fd_uh%d" % i, [128, NT, 130], F32) for i in range(3)]
        vv = [_sb(es, nc, "fd_v%d" % i, [128, NT, 128], F32) for i in range(2)]
        acc = [_sb(es, nc, "fd_acc%d" % i, [128, NT, 128], F32) for i in range(2)]
        accsl = [[Buf(acc[i].t, "accsl") for ct in range(NT)] for i in range(2)]
        gl = [_sb(es, nc, "fd_gl", [128, NT, 128], F32)] * 2
        gT = [_sb(es, nc, "fd_gT%d" % i, [128, NT, 128], BF16) for i in range(2)]
        xr = [_sb(es, nc, "fd_xr%d" % i, [128, 1024], F32) for i in range(2)]
        r = [_sb(es, nc, "fd_r", [128, 1024], F32)] * 2
        t = [_sb(es, nc, "fd_t", [128, 1024], F32)] * 2
        st = [_sb(es, nc, "fd_st%d" % i, [128, 2], F32) for i in range(2)]
        ob = [_sb(es, nc, "fd_ob%d" % i, [128, 1024], F32) for i in range(2)]
        py = [_ps(es, nc, "fd_py%d" % i, [128, 512], F32) for i in range(4)]
        wres = load_weight_bf16(S, W, w_down[li], NT)
        for s_ in range(2):
            K.dma("sp", gate[s_].t[:], MOD.t[s_, :, 5120:6144], [], [gate[s_]], gate[s_])
        K.dma("sp", lnw.t[:], P["ln2_w"][li], [], [lnw], lnw)
        K.dma("sp", lnb.t[:], P["ln2_b"][li], [], [lnb], lnb)
        K.dma("sp", cw.t[:], P["ffn_cw"][li], [], [cw], cw)
        K.dma("sp", cb.t[:], P["ffn_cb"][li], [], [cb], cb)
        def loadsA(c):
            K.dma("sp", uh[c % 3].t[:, :, 1:129], UVT.t[c, :, 0:NT, :], [], [uh[c % 3]], uh[c % 3])
            K.dma("sp", vv[c % 2].t[:], UVT.t[c, :, NT:2 * NT, :], [], [vv[c % 2]], vv[c % 2])

        def loadsC(c):
            K.dma("sp", xr[c % 2].t[:], X1.t[c * 128:(c + 1) * 128, :], [], [xr[c % 2]], xr[c % 2])

        def stA(c):
            b = c % 2
            ub = uh[c % 3]
            lv = c not in (0, 2)
            rv = c not in (1, NCH - 1)
            if lv:
                K.op("pool", "tensor_copy", [uh[(c - 1) % 3]], [ub], out=ub.t[:, :, 0:1], in_=uh[(c - 1) % 3].t[:, :, 128:129])
            else:
                K.op("pool", "memset", [], [ub], ub.t[:, :, 0:1], 0.0)
            if rv:
                K.op("pool", "tensor_copy", [uh[(c + 1) % 3]], [ub], out=ub.t[:, :, 129:130], in_=uh[(c + 1) % 3].t[:, :, 1:2])
            else:
                K.op("pool", "memset", [], [ub], ub.t[:, :, 129:130], 0.0)
            asl = accsl[b]
            for ct in range(NT):
                K.op("act", "activation", [ub, cw, cb], [asl[ct]], out=acc[b].t[:, ct, :], in_=ub.t[:, ct, 0:128], func=AF.Identity,
                     scale=cw.t[:, ct, 0:1], bias=cb.t[:, ct:ct + 1])
            for j in range(1, 3):
                for ct in range(NT):
                    K.op("dve", "scalar_tensor_tensor", [ub, cw, asl[ct]], [asl[ct]], out=acc[b].t[:, ct, :], in0=ub.t[:, ct, j:j + 128],
                         scalar=cw.t[:, ct, j:j + 1], in1=acc[b].t[:, ct, :], op0=ALU.mult, op1=ALU.add)
            K.op("act", "activation", asl, [gl[b]], out=gl[b].t[:], in_=acc[b].t[:], func=AF.Gelu)
            K.op("pool", "tensor_tensor", [gl[b], vv[b]], [gT[b]], out=gT[b].t[:], in0=gl[b].t[:], in1=vv[b].t[:], op=ALU.mult)

        def stB(c):
            b = c % 2
            pys = [py[(c % 2) * 2], py[(c % 2) * 2 + 1]]
            for hf in range(2):
                for kc in range(NT):
                    K.op("pe", "matmul", [gT[b]] + wres, [pys[hf]], pys[hf].t[:], gT[b].t[:, kc, :], W.t[:, kc, hf * 512:(hf + 1) * 512],
                         start=(kc == 0), stop=(kc == NT - 1))

        def stC(c):
            b = c % 2
            src = 1 if c < NCTX // 128 else 0
            rows = slice(c * 128, (c + 1) * 128)
            pys = [py[(c % 2) * 2], py[(c % 2) * 2 + 1]]
            resid_ln(K, pys, xr[b], gate[src], lnw, lnb, r[b], t[b], st[b], ob[b])
            if out_ap is None:
                K.dma("sp", XN.t[rows, :], ob[b].t[:], [ob[b]], [], ob[b])
            elif c >= NCTX // 128:
                K.dma("sp", out_ap[c * 128 - NCTX:(c + 1) * 128 - NCTX, :], ob[b].t[:], [ob[b]], [], ob[b])

        loadsA(0)
        loadsC(0)
        if NCH > 1:
            loadsA(1)
            loadsC(1)
        stA(0)
        for c in range(NCH):
            stB(c)
            if c + 2 < NCH:
                loadsA(c + 2)
            if c + 1 < NCH:
                stA(c + 1)
            stC(c)
            if c + 2 < NCH:
                loadsC(c + 2)
        K.end_phase("ffndown%d" % li)

PARAM_SHAPES = {
    "ssd_cw": [128, 12, 5], "ssd_cb": [128, 12], "ssd_dtb": [128, 32], "ssd_alog": [128, 32],
    "ret_dec": [128, 8], "diff_lam": [128, 256], "ln1_w": [128, 1024], "ln1_b": [128, 1024],
    "ln2_w": [128, 1024], "ln2_b": [128, 1024], "ssd_nw": [128, 1024], "ssd_d": [128, 16],
    "ret_nw": [128, 128], "diff_nw": [128, 128], "ffn_cw": [128, 22, 3], "ffn_cb": [128, 22],
}
PHASES = ["mod", "inproj", "ssdprep", "scans", "retprep", "scanr", "attnprep", "attn", "post", "ffnup", "ffndown"]


def build_program(NL, depth, dbg=False, stop_after=None):
    nc = bass.Bass("TRN2", target_bir_lowering=False)
    K = Ctx(nc, NL, depth, dbg)
    T = K.T
    inp = {}

    def din(name, shape, dt=F32):
        inp[name] = nc.dram_tensor(name, list(shape), dt, kind="ExternalInput").ap()
        return inp[name]

    x_in = din("x", [NL, D])
    ctx_in = din("ctx", [NCTX, D])
    c_fm = din("c_fm", [128, 8])
    cc_fm = din("cc_fm", [128, 8])
    w_ada = din("w_ada", [depth, D, 6 * D])
    b_ada = din("b_ada", [depth, 6 * D])
    w_in = din("w_in", [depth, D, IN_COLS])
    w_out = din("w_out", [depth, 2 * D, D])
    w_up = din("ffn_w_up", [depth, D, 2 * DFF])
    w_down = din("ffn_w_down", [depth, DFF, D])
    P = {k: din("p_" + k, [depth] + v) for k, v in PARAM_SHAPES.items()}
    consts = {
        "identb": din("identb", [128, 128], BF16), "identf": din("identf", [128, 128]),
        "ones": din("ones", [128, 128]), "tri": din("tri", [2, 128, 128]), "stri": din("stri", [2, 128, 128]),
        "mask": din("mask", [2, 128, 128]), "ret_tab": din("ret_tab", [T, 128]), "diff_tab": din("diff_tab", [T, 128]),
    }
    out = nc.dram_tensor("out", [NL, D], F32, kind="ExternalOutput").ap()

    X = K.dram_t("X", [T, D], F32)
    X1 = K.dram_t("X1", [T, D], F32)
    MOD = K.dram_t("MOD", [2, 128, 6 * D], F32)
    PTOK = K.dram_t("PTOK", [T, PT_W], F32)
    XBCT = K.dram_t("XBCT", [K.NCH, 128, 12, 128], F32)
    XSB = K.dram_t("XSB", [T, 1280], BF16)
    BCT = K.dram_t("BCT", [512, T], BF16)
    DTLA = K.dram_t("DTLA", [T, 64], F32)
    RQKT = K.dram_t("RQKT", [64, 8, T], BF16)
    RTOK = K.dram_t("RTOK", [T, 768], BF16)
    YSS = K.dram_t("YSS", [T, 1024], F32)
    YSR = K.dram_t("YSR", [T, 512], F32)
    KT = K.dram_t("KT", [128, 4, T], BF16)
    VA = K.dram_t("VA", [4, 128, K.NCH, 128], BF16)
    QT = K.dram_t("QT", [128, 4, T], BF16)
    KMAX = K.dram_t("KMAX", [128, 8], F32)
    KM8 = K.dram_t("KM8", [8, 1], F32)
    NB = K.dram_t("NB", [128, 4], F32)
    OD = K.dram_t("OD", [T, 512], F32)
    UVT = K.dram_t("UVT", [K.NCH, 128, 44, 128], F32)

    with ExitStack() as es:
        K.eng_sem = {e: es.enter_context(nc.semaphore("s_" + e)) for e in ENGS}
        K.dma_sems = [es.enter_context(nc.semaphore("d%d" % i)) for i in range(64)]
        K.eng_base = {e: 0 for e in ENGS}
        K.dma_base = [0] * 64
        phase_init(K, x_in, ctx_in, X)
        done = False
        for li in range(depth):
            last = li == depth - 1
            steps = [
                ("mod", lambda: phase_mod(K, li, c_fm, cc_fm, w_ada, b_ada, MOD, consts)),
                ("inproj", lambda: phase_proj(K, "inproj%d" % li, X, MOD, 1024, 0, w_in[li], IN_COLS, TOK_GROUPS, PTOK, 1024, 12, XBCT, consts)),
                ("ssdprep", lambda: phase_ssd_prep(K, li, PTOK, XBCT, P, XSB, BCT, DTLA, consts)),
                ("scans", lambda: [phase_scan(K, li, fam_ssd(), d, {"BCT": BCT, "XSB": XSB, "DTLA": DTLA}, P, YSS, consts) for d in range(2)]),
                ("retprep", lambda: phase_ret_prep(K, li, PTOK, P, RQKT, RTOK, consts)),
                ("scanr", lambda: [phase_scan(K, li, fam_ret(), d, {"RQKT": RQKT, "RTOK": RTOK}, P, YSR, consts) for d in range(2)]),
                ("attnprep", lambda: phase_attn_prep(K, li, PTOK, KT, VA, QT, KMAX, KM8, NB, consts)),
                ("attn", lambda: phase_attn(K, li, KT, VA, QT, NB, P, OD, consts)),
                ("post", lambda: phase_post_outproj(K, li, X, MOD, PTOK, XSB, YSS, YSR, OD, P, w_out, X1, consts)),
                ("ffnup", lambda: phase_proj(K, "ffnup%d" % li, X1, MOD, 4096, 3072, w_up[li], 2 * DFF, [], None, 0, 44, UVT, consts)),
                ("ffndown", lambda: phase_ffn_down(K, li, X1, MOD, UVT, P, w_down, X, out if last else None, consts)),
            ]
            for nm, fn in steps:
                fn()
                if stop_after == nm:
                    done = True
                    break
            if done:
                break
    return nc, K


def _tables(T):
    f32 = np.float32
    n_lat = T - NCTX
    inv_ax = (f32(1.0) / (f32(10000.0) ** (np.arange(16, dtype=f32) / f32(16)))).astype(f32)
    i = np.arange(n_lat)
    row = (i // GRID_W).astype(f32)
    col = (i % GRID_W).astype(f32)
    ar = (row[:, None] * inv_ax[None, :]).astype(f32).astype(np.float64)
    ac = (col[:, None] * inv_ax[None, :]).astype(f32).astype(np.float64)
    dt = np.zeros((T, 128), f32)
    dt[:NCTX, 0:64] = 1.0
    dt[NCTX:, 0:64] = np.concatenate([np.cos(ar), np.cos(ar), np.cos(ac), np.cos(ac)], 1)
    dt[NCTX:, 64:128] = np.concatenate([-np.sin(ar), np.sin(ar), -np.sin(ac), np.sin(ac)], 1)
    inv_ret = (f32(1.0) / (f32(10000.0) ** np.linspace(0.0, 1.0, 32, dtype=f32))).astype(f32)
    pos = np.arange(T).astype(f32)
    a = (pos[:, None] * inv_ret[None, :]).astype(f32).astype(np.float64)
    rt = np.concatenate([np.cos(a), np.cos(a), -np.sin(a), np.sin(a)], 1).astype(f32)
    return dt, rt


def make_consts(T):
    f32 = np.float32
    t = np.arange(128)
    le = (t[:, None] <= t[None, :]).astype(f32)
    ge = (t[:, None] >= t[None, :]).astype(f32)
    gt = (t[:, None] > t[None, :]).astype(f32)
    lt = (t[:, None] < t[None, :]).astype(f32)
    dtab, rtab = _tables(T)
    return {
        "identb": np.eye(128, dtype=f32).astype(ml_dtypes.bfloat16), "identf": np.eye(128, dtype=f32),
        "ones": np.ones((128, 128), f32), "tri": np.stack([le, ge]), "stri": np.stack([gt, lt]),
        "mask": np.stack([le, ge]), "ret_tab": rtab, "diff_tab": dtab,
    }


def _rep(a, depth):
    a = np.asarray(a[:depth], np.float32).reshape(depth, 1, -1)
    return np.ascontiguousarray(np.broadcast_to(a, (depth, 128, a.shape[2])))


def make_in_maps(inputs, NL, depth, ncores):
    T = NL + NCTX
    cs = make_consts(T)
    L = depth
    shared = {
        "w_ada": np.ascontiguousarray(inputs["w_ada"][:L]), "b_ada": np.ascontiguousarray(inputs["b_ada"][:L]),
        "w_in": np.ascontiguousarray(inputs["w_in"][:L]), "w_out": np.ascontiguousarray(inputs["w_out"][:L]),
        "ffn_w_up": np.ascontiguousarray(inputs["ffn_w_up"][:L]), "ffn_w_down": np.ascontiguousarray(inputs["ffn_w_down"][:L]),
        "cc_fm": np.ascontiguousarray(inputs["c_ctx"].reshape(8, 128).T),
        "p_ssd_cw": np.ascontiguousarray(inputs["ssd_conv_w"][:L].reshape(L, 5, 12, 128).transpose(0, 3, 2, 1)),
        "p_ssd_cb": np.ascontiguousarray(inputs["ssd_conv_b"][:L].reshape(L, 12, 128).transpose(0, 2, 1)),
        "p_ssd_dtb": _rep(inputs["ssd_dt_bias"].reshape(-1, 32), L), "p_ssd_alog": _rep(inputs["ssd_a_log"].reshape(-1, 32), L),
        "p_ret_dec": _rep(inputs["ret_decay"].reshape(-1, 8), L), "p_diff_lam": _rep(inputs["diff_lambda"].reshape(-1, 256), L),
        "p_ln1_w": _rep(inputs["ln1_w"], L), "p_ln1_b": _rep(inputs["ln1_b"], L),
        "p_ln2_w": _rep(inputs["ln2_w"], L), "p_ln2_b": _rep(inputs["ln2_b"], L),
        "p_ssd_nw": _rep(inputs["ssd_norm_w"], L), "p_ssd_d": _rep(inputs["ssd_d"], L),
        "p_ret_nw": _rep(inputs["ret_norm_w"], L), "p_diff_nw": _rep(inputs["diff_norm_w"], L),
        "p_ffn_cw": np.ascontiguousarray(inputs["ffn_conv_w"][:L].reshape(L, 3, 22, 128).transpose(0, 3, 2, 1)),
        "p_ffn_cb": np.ascontiguousarray(inputs["ffn_conv_b"][:L].reshape(L, 22, 128).transpose(0, 2, 1)),
    }
    shared.update(cs)
    maps = []
    for b in range(ncores):
        m = dict(shared)
        m["x"] = np.ascontiguousarray(inputs["x"][b, :NL])
        m["ctx"] = np.ascontiguousarray(inputs["ctx"][b])
        m["c_fm"] = np.ascontiguousarray(inputs["c"][b].reshape(8, 128).T)
        maps.append(m)
    return maps


def kernel(**inputs):
    inputs = {k: np.asarray(v) for k, v in inputs.items()}
    NL = inputs["x"].shape[1]
    depth = inputs["w_ada"].shape[0]
    nb = inputs["x"].shape[0]
    nc, K = build_program(NL, depth)
    maps = make_in_maps(inputs, NL, depth, nb)
    res = run_bass_kernel_spmd(nc, maps, core_ids=list(range(nb)))
    return np.stack([np.asarray(r["out"], np.float32) for r in res.results], axis=0)
```

```python
import math
from contextlib import ExitStack

import numpy as np
import ml_dtypes
import concourse.bass as bass
import concourse.mybir as mybir
from concourse.bass_utils import run_bass_kernel_spmd

F32 = mybir.dt.float32
BF16 = mybir.dt.bfloat16
AF = mybir.ActivationFunctionType
ALU = mybir.AluOpType
AX = mybir.AxisListType

D = 1024
NCTX = 256
GRID_W = 64
IN_COLS = 5664
DFF = 2816
ENGS = ("pe", "act", "dve", "pool", "sp")


class Res:
    __slots__ = ("name", "last_w", "readers", "sem", "sem_cnt", "base", "sw")

    def __init__(self, name=""):
        self.name = name
        self.last_w = None
        self.readers = []
        self.sem = None
        self.sem_cnt = 0
        self.base = 0
        self.sw = False


class Ins:
    __slots__ = ("eng", "idx", "fn", "deps", "dma", "sem_res", "count", "signal")

    def __init__(self, eng, idx, fn, dma, sem_res):
        self.eng = eng
        self.idx = idx
        self.fn = fn
        self.deps = []
        self.dma = dma
        self.sem_res = sem_res
        self.count = None
        self.signal = False


class Sched:
    def __init__(self, nc, eng_sem, dma_sems, eng_base, dma_base):
        self.nc = nc
        self.lists = {e: [] for e in ENGS}
        self.dma_res = []
        self.eng_sem = eng_sem
        self.dma_sems = dma_sems
        self.eng_base = eng_base
        self.dma_base = dma_base

    def add(self, eng, fn, reads=(), writes=(), dma=False, sem_res=None):
        lst = self.lists[eng]
        ins = Ins(eng, len(lst), fn, dma, sem_res)
        deps = {}
        for r in reads:
            d = r.last_w
            if d is not None:
                deps[id(d)] = d
        for w in writes:
            d = w.last_w
            if d is not None:
                deps[id(d)] = d
            for rd in w.readers:
                deps[id(rd)] = rd
        for r in reads:
            r.readers.append(ins)
        for w in writes:
            w.last_w = ins
            w.readers = []
        best = {}
        out = []
        for d in deps.values():
            if d is ins:
                continue
            if d.dma:
                out.append(d)
            else:
                if d.eng == eng and not dma:
                    if eng == "pe":
                        continue
                    if d.idx < ins.idx - 3:
                        continue
                b = best.get(d.eng)
                if b is None or d.idx > b.idx:
                    best[d.eng] = d
        out.extend(best.values())
        ins.deps = out
        if dma:
            if sem_res.sem is None:
                sem_res.sem = len(self.dma_res)
                sem_res.sw = (eng == "pool")
                self.dma_res.append(sem_res)
            assert sem_res.sw == (eng == "pool")
            sem_res.sem_cnt += 1
            ins.count = sem_res.sem_cnt
        lst.append(ins)
        return ins

    def emit(self):
        nc = self.nc
        for e in ENGS:
            for ins in self.lists[e]:
                for d in ins.deps:
                    d.signal = True
            for ins in reversed(self.lists[e]):
                if not ins.dma:
                    ins.signal = True
                    break
        final = {}
        for e in ENGS:
            c = self.eng_base[e]
            for ins in self.lists[e]:
                if (not ins.dma) and ins.signal:
                    c += 1
                    ins.count = c
            final[e] = c
        nsw = 0
        nhw = 0
        idxs = []
        for r in self.dma_res:
            if r.sw:
                nsw += 1
                idxs.append(len(self.dma_sems) - nsw)
            else:
                idxs.append(nhw)
                nhw += 1
        assert nhw + 24 <= len(self.dma_sems) and nsw <= 24, (nhw, nsw)
        self._idxs = idxs
        for i, r in zip(idxs, self.dma_res):
            r.sem = self.dma_sems[i]
            r.base = self.dma_base[i]
        stats = {}

        def run_engine(e, eo):
            seen = {}
            nw = 0
            for ins in self.lists[e]:
                need = {}
                for d in ins.deps:
                    if d.dma:
                        key = ("d", id(d.sem_res))
                        val = d.sem_res.base + 16 * d.count
                        sem = d.sem_res.sem
                    else:
                        key = ("e", d.eng)
                        val = d.count
                        sem = self.eng_sem[d.eng]
                    if key not in need or need[key][1] < val:
                        need[key] = (sem, val)
                for key, (sem, val) in need.items():
                    if seen.get(key, 0) >= val:
                        continue
                    seen[key] = val
                    eo.wait_ge(sem, val)
                    nw += 1
                bi = ins.fn(eo)
                if ins.dma:
                    bi.then_inc(ins.sem_res.sem, 16)
                elif ins.signal:
                    bi.then_inc(self.eng_sem[e], 1)
            for r in self.dma_res:
                eo.wait_ge(r.sem, r.base + 16 * r.sem_cnt)
            for e2 in ENGS:
                if final[e2] > self.eng_base[e2]:
                    eo.wait_ge(self.eng_sem[e2], final[e2])
            stats[e] = (len(self.lists[e]), nw)

        with nc.Block() as block:
            @block.tensor
            def _(eo):
                run_engine("pe", eo)

            @block.scalar
            def _(eo):
                run_engine("act", eo)

            @block.vector
            def _(eo):
                run_engine("dve", eo)

            @block.gpsimd
            def _(eo):
                run_engine("pool", eo)

            @block.sync
            def _(eo):
                run_engine("sp", eo)
        for e in ENGS:
            self.eng_base[e] = final[e]
        for i, r in zip(self._idxs, self.dma_res):
            self.dma_base[i] = r.base + 16 * r.sem_cnt
        return stats


class Buf:
    __slots__ = ("t", "r")

    def __init__(self, t, name=""):
        self.t = t
        self.r = Res(name)


class Ctx:
    def __init__(self, nc, NL, depth, dbg):
        self.nc = nc
        self.NL = NL
        self.T = NL + NCTX
        self.NCH = self.T // 128
        self.depth = depth
        self.dbg = dbg
        self.dram = {}
        self.S = None
        self.stats = []

    def dram_t(self, name, shape, dt, out=False):
        kind = "ExternalOutput" if (out or self.dbg) else "Internal"
        t = self.nc.dram_tensor(name, list(shape), dt, kind=kind).ap()
        b = Buf(t, name)
        self.dram[name] = b
        return b

    _chains = None

    def op(self, eng, meth, reads, writes, *args, **kw):
        if self._chains is not None:
            self._chains[-1].append((eng, meth, list(reads), list(writes), args, kw))
            return None
        return self.S.add(eng, lambda e: getattr(e, meth)(*args, **kw),
                          reads=[b.r for b in reads], writes=[b.r for b in writes])

    def begin_chain(self):
        if self._chains is None:
            self._chains = []
        self._chains.append([])

    def flush_chains(self):
        chains, self._chains = self._chains, None
        n = max(len(c) for c in chains)
        for i in range(n):
            for c in chains:
                if i < len(c):
                    eng, meth, reads, writes, args, kw = c[i]
                    self.op(eng, meth, reads, writes, *args, **kw)

    def dma(self, eng, out, in_, reads, writes, semb):
        return self.S.add(eng, lambda e: e.dma_start(out=out, in_=in_),
                          reads=[b.r for b in reads], writes=[b.r for b in writes], dma=True, sem_res=semb.r)

    def new_phase(self):
        self.S = Sched(self.nc, self.eng_sem, self.dma_sems, self.eng_base, self.dma_base)
        return self.S

    def end_phase(self, name):
        st = self.S.emit()
        self.stats.append((name, st))
        self.S = None


_UID = [0]


def _sb(es, nc, name, shape, dt):
    _UID[0] += 1
    name = "%s_%d" % (name, _UID[0])
    return Buf(es.enter_context(nc.sbuf_tensor(name, list(shape), dt)), name)


def _ps(es, nc, name, shape, dt):
    _UID[0] += 1
    name = "%s_%d" % (name, _UID[0])
    return Buf(es.enter_context(nc.psum_tensor(name, list(shape), dt)), name)


def phase_init(K, x_in, ctx_in, X):
    S = K.new_phase()
    S.add("sp", lambda e: e.dma_start(out=X.t[0:NCTX, :], in_=ctx_in), writes=[X.r], dma=True, sem_res=X.r)
    r2 = Res("x2")
    nparts = 4
    step = K.NL // nparts
    for i in range(nparts):
        rr = Res("xi%d" % i)
        S.add("sp" if i % 2 == 0 else "act",
              lambda e, i=i: e.dma_start(out=X.t[NCTX + i * step:NCTX + (i + 1) * step, :],
                                         in_=x_in[i * step:(i + 1) * step, :]),
              writes=[rr], dma=True, sem_res=rr)
    K.end_phase("init")


def phase_mod(K, li, c_fm, cc_fm, w_ada, b_ada, MOD, consts):
    nc = K.nc
    S = K.new_phase()
    with ExitStack() as es:
        cin = _sb(es, nc, "m_cin", [128, 16], F32)
        sil = _sb(es, nc, "m_sil", [128, 16], F32)
        silb = _sb(es, nc, "m_silb", [128, 16, 128], F32)
        brow = _sb(es, nc, "m_brow", [1, 6144], F32)
        ones = _sb(es, nc, "m_ones", [1, 128], F32)
        wblk = [_sb(es, nc, "m_w%d" % i, [128, 8, 512], F32) for i in range(2)]
        stage = [_sb(es, nc, "m_st%d" % i, [128, 512], F32) for i in range(2)]
        ps = [_ps(es, nc, "m_ps%d" % i, [128, 512], F32) for i in range(2)]
        S.add("sp", lambda e: e.dma_start(out=cin.t[:, 0:8], in_=c_fm), writes=[cin.r], dma=True, sem_res=cin.r)
        S.add("sp", lambda e: e.dma_start(out=cin.t[:, 8:16], in_=cc_fm), writes=[cin.r], dma=True, sem_res=cin.r)
        S.add("sp", lambda e: e.dma_start(out=brow.t[:], in_=b_ada[li:li + 1, :]), writes=[brow.r], dma=True, sem_res=brow.r)
        S.add("dve", lambda e: e.memset(ones.t[:], 1.0), writes=[ones.r])
        S.add("act", lambda e: e.activation(out=sil.t[:], in_=cin.t[:], func=AF.Silu), reads=[cin.r], writes=[sil.r])
        S.add("dve", lambda e: e.tensor_copy(out=silb.t[:], in_=sil.t[:].unsqueeze(2).broadcast_to([128, 16, 128])),
              reads=[sil.r], writes=[silb.r])
        k = 0
        for j in range(12):
            wb = wblk[j % 2]
            S.add("sp", lambda e, wb=wb, j=j: e.dma_start(
                out=wb.t[:], in_=w_ada[li, :, j * 512:(j + 1) * 512].rearrange("(kc p) n -> p kc n", p=128)),
                writes=[wb.r], dma=True, sem_res=wb.r)
            for src in range(2):
                p = ps[k % 2]
                st = stage[k % 2]
                for kc in range(8):
                    S.add("pe", lambda e, p=p, wb=wb, kc=kc, src=src: e.matmul(
                        p.t[:], silb.t[:, src * 8 + kc, :], wb.t[:, kc, :], start=(kc == 0), stop=False),
                        reads=[silb.r, wb.r], writes=[p.r])
                S.add("pe", lambda e, p=p, j=j: e.matmul(
                    p.t[:], ones.t[0:1, :], brow.t[0:1, j * 512:(j + 1) * 512], start=False, stop=True),
                    reads=[ones.r, brow.r], writes=[p.r])
                addone = 1.0 if j in (2, 3, 8, 9) else 0.0
                S.add("dve", lambda e, p=p, st=st, addone=addone: e.tensor_scalar(
                    out=st.t[:], in0=p.t[:], scalar1=addone, scalar2=None, op0=ALU.add),
                    reads=[p.r], writes=[st.r])
                S.add("sp", lambda e, st=st, src=src, j=j: e.dma_start(
                    out=MOD.t[src, :, j * 512:(j + 1) * 512], in_=st.t[:]),
                    reads=[st.r], writes=[MOD.r], dma=True, sem_res=st.r)
                k += 1
        K.end_phase("mod%d" % li)


def load_weight_bf16(S, wdst, w_src_kcn, nkc, split=4):
    res = []
    for kc in range(nkc):
        wb = Buf(None, "w")
        S.add("pool", lambda e, kc=kc: e.dma_start(out=wdst.t[:, kc, :], in_=w_src_kcn[kc * 128:(kc + 1) * 128, :]),
              writes=[wb.r], dma=True, sem_res=wb.r)
        res.append(wb)
    return res


PT_Z, PT_DT, PT_DQ, PT_DK, PT_DV, PT_RQ, PT_RK, PT_RV, PT_RG, PT_W = 0, 1024, 1056, 1568, 2080, 2592, 2848, 3104, 3616, 4128
TOK_GROUPS = [
    (PT_Z, 0, 512), (PT_Z + 512, 512, 512), (PT_DT, 2560, 32),
    (PT_DQ, 2592, 512), (PT_DK, 3104, 512), (PT_DV, 3616, 512),
    (PT_RQ, 4128, 512), (PT_RV, 4640, 512), (PT_RG, 5152, 512),
]


def phase_proj(K, name, X, MOD, sc_col, sh_col, w_src, ncols, tok_groups, PTOK, fm_col0, n_fm, FMB, consts):
    nc = K.nc
    S = K.new_phase()
    NCH = K.NCH
    with ExitStack() as es:
        W = _sb(es, nc, "ip_w", [128, 8, ncols], BF16)
        identb = _sb(es, nc, "ip_id", [128, 128], BF16)
        modt = [[_sb(es, nc, "ip_mod%d%d" % (s_, q), [128, 1024], F32) for q in range(2)] for s_ in range(2)]
        xt = [_sb(es, nc, "ip_x%d" % i, [128, 1024], F32) for i in range(2)]
        xmb = [_sb(es, nc, "ip_xb%d" % i, [128, 1024], BF16) for i in range(2)]
        xT4 = [_sb(es, nc, "ip_xT%d" % i, [128, 8, 512], BF16) for i in range(2)]
        xslot = [[Buf(xT4[i].t, "slot") for j in range(4)] for i in range(2)]
        stg = [_sb(es, nc, "ip_st%d" % i, [128, PT_W if tok_groups else 8], F32) for i in range(2)]
        stf = [_sb(es, nc, "ip_sf%d" % i, [128, 4, 512], F32) for i in range(3)]
        pst = [_ps(es, nc, "ip_pt%d" % i, [128, 8, 128], BF16) for i in range(2)]
        psg = [_ps(es, nc, "ip_pg%d" % i, [128, 512], F32) for i in range(4 if tok_groups else 1)]
        psf = [_ps(es, nc, "ip_pf%d" % i, [128, 512], F32) for i in range(2 if tok_groups else 4)]
        wres = load_weight_bf16(S, W, w_src, 8)
        K.dma("sp", identb.t[:], consts["identb"], [], [identb], identb)
        for s_ in range(2):
            for q in range(2):
                c0 = sc_col if q == 0 else sh_col
                K.dma("sp", modt[s_][q].t[:], MOD.t[s_, :, c0:c0 + 1024], [], [modt[s_][q]], modt[s_][q])

        def loads(c):
            K.dma("sp", xt[c % 2].t[:], X.t[c * 128:(c + 1) * 128, :], [], [xt[c % 2]], xt[c % 2])

        gi = 0
        fi = 0
        ei = 0
        loads(0)
        blocks = [list(range(i, min(i + 4, NCH))) for i in range(0, NCH, 4)]
        for bi, blk in enumerate(blocks):
            xb = xT4[bi % 2]
            slots = xslot[bi % 2]
            for j, c in enumerate(blk):
                if c + 1 < NCH:
                    loads(c + 1)
                b = c % 2
                src = 1 if c < NCTX // 128 else 0
                sl = slots[j]
                K.op("dve", "tensor_tensor", [xt[b], modt[src][0]], [xt[b]], out=xt[b].t[:], in0=xt[b].t[:], in1=modt[src][0].t[:], op=ALU.mult)
                K.op("pool", "tensor_tensor", [xt[b], modt[src][1]], [xmb[b]], out=xmb[b].t[:], in0=xt[b].t[:], in1=modt[src][1].t[:], op=ALU.add)
                for kc in range(8):
                    K.op("pe", "transpose", [xmb[b], identb], [pst[b]], pst[b].t[:, kc, :], xmb[b].t[:, kc * 128:(kc + 1) * 128], identb.t[:])
                K.op("act", "activation", [pst[b]], [sl], out=xb.t[:, :, j * 128:(j + 1) * 128], in_=pst[b].t[:], func=AF.Copy)
                for (dc, sc, wd) in tok_groups:
                    p = psg[gi % len(psg)]
                    for kc in range(8):
                        K.op("pe", "matmul", [sl] + wres, [p], p.t[:, 0:wd], xb.t[:, kc, j * 128:(j + 1) * 128], W.t[:, kc, sc:sc + wd],
                             start=(kc == 0), stop=(kc == 7))
                    if gi % 2 == 0:
                        K.op("act", "activation", [p], [stg[b]], out=stg[b].t[:, dc:dc + wd], in_=p.t[:, 0:wd], func=AF.Copy)
                    else:
                        K.op("dve", "tensor_copy", [p], [stg[b]], out=stg[b].t[:, dc:dc + wd], in_=p.t[:, 0:wd])
                    gi += 1
                if tok_groups:
                    K.dma("sp", PTOK.t[c * 128:(c + 1) * 128, :], stg[b].t[:], [stg[b]], [], stg[b])
            BW = len(blk) * 128
            for ct in range(n_fm):
                p = psf[fi % len(psf)]
                fi += 1
                col = fm_col0 + ct * 128
                for kc in range(8):
                    K.op("pe", "matmul", slots[0:len(blk)] + wres, [p], p.t[:, 0:BW], W.t[:, kc, col:col + 128], xb.t[:, kc, 0:BW],
                         start=(kc == 0), stop=(kc == 7))
                sb = stf[(ct // 4 + bi * ((n_fm + 3) // 4)) % 3]
                if ei % 2 == 0:
                    K.op("dve", "tensor_copy", [p], [sb], out=sb.t[:, ct % 4, 0:BW], in_=p.t[:, 0:BW])
                else:
                    K.op("act", "activation", [p], [sb], out=sb.t[:, ct % 4, 0:BW], in_=p.t[:, 0:BW], func=AF.Copy)
                ei += 1
                if ct % 4 == 3:
                    for j, c in enumerate(blk):
                        K.dma("sp", FMB.t[c, :, ct - 3:ct + 1, :], sb.t[:, :, j * 128:(j + 1) * 128], [sb], [], sb)
        K.end_phase(name)


def bc(ap, shape):
    return ap.broadcast_to(list(shape))


def phase_ssd_prep(K, li, PTOK, XBCT, P, XSB, BCT, DTLA, consts):
    nc = K.nc
    S = K.new_phase()
    NCH = K.NCH
    with ExitStack() as es:
        identb = _sb(es, nc, "sp_id", [128, 128], BF16)
        cw = _sb(es, nc, "sp_cw", [128, 12, 5], F32)
        cb = _sb(es, nc, "sp_cb", [128, 12], F32)
        dtb = _sb(es, nc, "sp_dtb", [128, 32], F32)
        alog = _sb(es, nc, "sp_alog", [128, 32], F32)
        aneg = _sb(es, nc, "sp_aneg", [128, 32], F32)
        xh = [_sb(es, nc, "sp_xh%d" % i, [128, 12, 132], F32) for i in range(3)]
        acc = [_sb(es, nc, "sp_acc%d" % i, [128, 12, 128], F32) for i in range(2)]
        accsl = [[Buf(acc[i].t, "accsl") for ct in range(12)] for i in range(2)]
        xc = [_sb(es, nc, "sp_xc%d" % i, [128, 12, 128], BF16) for i in range(2)]
        tok = [_sb(es, nc, "sp_tok%d" % i, [128, 1280], BF16) for i in range(2)]
        dtr = [_sb(es, nc, "sp_dtr%d" % i, [128, 32], F32) for i in range(2)]
        dl = [_sb(es, nc, "sp_dl%d" % i, [128, 64], F32) for i in range(2)]
        pt = [_ps(es, nc, "sp_pt%d" % i, [128, 8, 128], BF16) for i in range(2)]
        pb = [_ps(es, nc, "sp_pb%d" % i, [128, 2, 128], BF16) for i in range(2)]
        K.dma("sp", identb.t[:], consts["identb"], [], [identb], identb)
        K.dma("sp", cw.t[:], P["ssd_cw"][li], [], [cw], cw)
        K.dma("sp", cb.t[:], P["ssd_cb"][li], [], [cb], cb)
        K.dma("sp", dtb.t[:], P["ssd_dtb"][li], [], [dtb], dtb)
        K.dma("sp", alog.t[:], P["ssd_alog"][li], [], [alog], alog)
        K.op("act", "activation", [alog], [aneg], out=aneg.t[:], in_=alog.t[:], func=AF.Exp)
        K.op("dve", "tensor_scalar", [aneg], [aneg], out=aneg.t[:], in0=aneg.t[:], scalar1=-1.0, scalar2=None, op0=ALU.mult)
        def loads(c):
            K.dma("sp", xh[c % 3].t[:, :, 2:130], XBCT.t[c], [], [xh[c % 3]], xh[c % 3])
            K.dma("sp", dtr[c % 2].t[:], PTOK.t[c * 128:(c + 1) * 128, PT_DT:PT_DT + 32], [], [dtr[c % 2]], dtr[c % 2])

        def stA(c):
            b = c % 2
            xb_ = xh[c % 3]
            lv = c not in (0, 2)
            rv = c not in (1, NCH - 1)
            if lv:
                K.op("pool", "tensor_copy", [xh[(c - 1) % 3]], [xb_], out=xb_.t[:, :, 0:2], in_=xh[(c - 1) % 3].t[:, :, 128:130])
            else:
                K.op("pool", "memset", [], [xb_], xb_.t[:, :, 0:2], 0.0)
            if rv:
                K.op("pool", "tensor_copy", [xh[(c + 1) % 3]], [xb_], out=xb_.t[:, :, 130:132], in_=xh[(c + 1) % 3].t[:, :, 2:4])
            else:
                K.op("pool", "memset", [], [xb_], xb_.t[:, :, 130:132], 0.0)
            asl = accsl[b]
            for ct in range(12):
                K.op("dve", "tensor_scalar", [xb_, cw, cb], [asl[ct]], out=acc[b].t[:, ct, :], in0=xb_.t[:, ct, 0:128],
                     scalar1=cw.t[:, ct, 0:1], scalar2=cb.t[:, ct:ct + 1], op0=ALU.mult, op1=ALU.add)
            for j in range(1, 5):
                for ct in range(12):
                    K.op("dve", "scalar_tensor_tensor", [xb_, cw, asl[ct]], [asl[ct]], out=acc[b].t[:, ct, :], in0=xb_.t[:, ct, j:j + 128],
                         scalar=cw.t[:, ct, j:j + 1], in1=acc[b].t[:, ct, :], op0=ALU.mult, op1=ALU.add)
            K.op("act", "activation", asl, [xc[b]], out=xc[b].t[:], in_=acc[b].t[:], func=AF.Silu)
            K.op("dve", "tensor_tensor", [dtr[b], dtb], [dtr[b]], out=dtr[b].t[:], in0=dtr[b].t[:], in1=dtb.t[:], op=ALU.add)
            K.op("act", "activation", [dtr[b]], [dtr[b]], out=dtr[b].t[:], in_=dtr[b].t[:], func=AF.Exp)
            K.op("act", "activation", [dtr[b]], [dl[b]], out=dl[b].t[:, 0:32], in_=dtr[b].t[:], func=AF.Ln, bias=1.0)
            K.op("dve", "tensor_tensor", [dl[b], aneg], [dl[b]], out=dl[b].t[:, 32:64], in0=dl[b].t[:, 0:32], in1=aneg.t[:], op=ALU.mult)
            K.dma("sp", DTLA.t[c * 128:(c + 1) * 128, :], dl[b].t[:], [dl[b]], [], dl[b])

        def stB(c):
            b = c % 2
            for ct in range(8):
                K.op("pe", "transpose", [xc[b], identb], [pt[b]], pt[b].t[:, ct, :], xc[b].t[:, ct, :], identb.t[:])
            for ct in range(2):
                K.op("pe", "transpose", [xc[b], identb], [pb[b]], pb[b].t[:, ct, :], xc[b].t[:, 8 + ct, :], identb.t[:])
            K.op("act", "activation", [pt[b]], [tok[b]], out=tok[b].t[:, 0:1024], in_=pt[b].t[:].rearrange("p a b -> p (a b)"), func=AF.Copy)
            K.op("dve", "tensor_copy", [pb[b]], [tok[b]], out=tok[b].t[:, 1024:1280], in_=pb[b].t[:].rearrange("p a b -> p (a b)"))
            K.dma("sp", XSB.t[c * 128:(c + 1) * 128, :], tok[b].t[:], [tok[b]], [], tok[b])
            K.dma("sp", BCT.t[:, c * 128:(c + 1) * 128].rearrange("(ct p) t -> p ct t", p=128), xc[b].t[:, 8:12, :], [xc[b]], [], xc[b])

        loads(0)
        if NCH > 1:
            loads(1)
        stA(0)
        for c in range(NCH):
            stB(c)
            if c + 2 < NCH:
                loads(c + 2)
            if c + 1 < NCH:
                stA(c + 1)
        K.end_phase("ssdprep%d" % li)


def rope_tok(K, x, tabs, o, t1, nmap, half, cs_off, sn_off):
    raise NotImplementedError


def phase_ret_prep(K, li, PTOK, P, RQKT, RTOK, consts):
    nc = K.nc
    S = K.new_phase()
    NCH = K.NCH
    with ExitStack() as es:
        identb = _sb(es, nc, "rp_id", [128, 128], BF16)
        qk = [_sb(es, nc, "rp_qk%d" % i, [128, 512], F32) for i in range(2)]
        v = [_sb(es, nc, "rp_v%d" % i, [128, 512], F32) for i in range(2)]
        tab = [_sb(es, nc, "rp_tab%d" % i, [128, 128], F32) for i in range(2)]
        t1 = [_sb(es, nc, "rp_t1%d" % i, [128, 512], F32) for i in range(2)]
        o = [_sb(es, nc, "rp_o%d" % i, [128, 512], F32) for i in range(2)]
        ob = [_sb(es, nc, "rp_ob%d" % i, [128, 512], BF16) for i in range(2)]
        tk = [_sb(es, nc, "rp_tk%d" % i, [128, 768], BF16) for i in range(2)]
        qT = [_sb(es, nc, "rp_qT%d" % i, [64, 8, 128], BF16) for i in range(2)]
        pt = [_ps(es, nc, "rp_pt%d" % i, [64, 8, 128], BF16) for i in range(2)]
        K.dma("sp", identb.t[:], consts["identb"], [], [identb], identb)
        def loads(c):
            b = c % 2
            K.dma("sp", qk[b].t[:], PTOK.t[c * 128:(c + 1) * 128, PT_RQ:PT_RQ + 512], [], [qk[b]], qk[b])
            K.dma("sp", v[b].t[:], PTOK.t[c * 128:(c + 1) * 128, PT_RV:PT_RV + 512], [], [v[b]], v[b])
            K.dma("sp", tab[b].t[:], consts["ret_tab"][c * 128:(c + 1) * 128, :], [], [tab[b]], tab[b])

        loads(0)
        for c in range(NCH):
            b = c % 2
            if c + 1 < NCH:
                loads(c + 1)
            xv = qk[b].t[:].rearrange("p (m two h) -> p m two h", two=2, h=32)
            t1v = t1[b].t[:].rearrange("p (m two h) -> p m two h", two=2, h=32)
            sn = tab[b].t[:, 64:128].rearrange("p (two h) -> p two h", two=2)
            K.op("pool", "tensor_tensor", [qk[b], tab[b]], [t1[b]], out=t1v[:, :, 0, :], in0=xv[:, :, 1, :],
                 in1=bc(sn[:, 0:1, :], [128, 8, 32]), op=ALU.mult)
            K.op("pool", "tensor_tensor", [qk[b], tab[b]], [t1[b]], out=t1v[:, :, 1, :], in0=xv[:, :, 0, :],
                 in1=bc(sn[:, 1:2, :], [128, 8, 32]), op=ALU.mult)
            K.op("dve", "tensor_tensor", [qk[b], tab[b]], [o[b]], out=o[b].t[:].rearrange("p (m d) -> p m d", d=64),
                 in0=qk[b].t[:].rearrange("p (m d) -> p m d", d=64), in1=bc(tab[b].t[:, 0:64].unsqueeze(1), [128, 8, 64]), op=ALU.mult)
            K.op("dve", "tensor_tensor", [o[b], t1[b]], [ob[b]], out=ob[b].t[:, 0:256], in0=o[b].t[:, 0:256], in1=t1[b].t[:, 0:256], op=ALU.add)
            K.op("dve", "tensor_tensor", [o[b], t1[b]], [o[b]], out=o[b].t[:, 256:512], in0=o[b].t[:, 256:512], in1=t1[b].t[:, 256:512], op=ALU.add)
            K.op("dve", "tensor_scalar", [o[b]], [ob[b]], out=ob[b].t[:, 256:512], in0=o[b].t[:, 256:512], scalar1=0.125, scalar2=None, op0=ALU.mult)
            for h in range(8):
                K.op("pe", "transpose", [ob[b], identb], [pt[b]], pt[b].t[:, h, :], ob[b].t[:, h * 64:(h + 1) * 64], identb.t[:])
            K.op("act", "activation", [pt[b]], [qT[b]], out=qT[b].t[:], in_=pt[b].t[:], func=AF.Copy)
            K.dma("sp", RQKT.t[:, :, c * 128:(c + 1) * 128], qT[b].t[:], [qT[b]], [], qT[b])
            K.op("act", "activation", [v[b]], [tk[b]], out=tk[b].t[:, 0:512], in_=v[b].t[:], func=AF.Copy)
            K.op("pool", "tensor_copy", [ob[b]], [tk[b]], out=tk[b].t[:, 512:768], in_=ob[b].t[:, 256:512])
            K.dma("sp", RTOK.t[c * 128:(c + 1) * 128, :], tk[b].t[:], [tk[b]], [], tk[b])
        K.end_phase("retprep%d" % li)


class Fam:
    pass


def fam_ssd():
    f = Fam()
    f.name = "ssd"; f.H = 16; f.G = 2; f.N = 128; f.P = 64; f.J = 8; f.units = [[0], [1]]; f.YW = 1024
    return f


def fam_ret():
    f = Fam()
    f.name = "ret"; f.H = 4; f.G = 4; f.N = 64; f.P = 128; f.J = 4; f.units = [[0, 1, 2, 3]]; f.YW = 512
    return f


def phase_scan(K, li, fam, d, srcs, P, YS, consts):
    nc = K.nc
    S = K.new_phase()
    NCH = K.NCH
    ssd = fam.name == "ssd"
    H, G, N, PP, J = fam.H, fam.G, fam.N, fam.P, fam.J
    NU = len(fam.units)
    GU = len(fam.units[0])
    order = list(range(NCH)) if d == 0 else [1, 0] + list(range(NCH - 1, 1, -1))
    endcol = 127 if d == 0 else 0
    with ExitStack() as es:
        tri = _sb(es, nc, "sc_tri", [128, 128], F32)
        stri = _sb(es, nc, "sc_stri", [128, 128], F32)
        mask = _sb(es, nc, "sc_mask", [128, 128], F32)
        ones = _sb(es, nc, "sc_ones", [128, 128], F32)
        K.dma("sp", tri.t[:], consts["tri"][d], [], [tri], tri)
        K.dma("sp", stri.t[:], consts["stri"][d], [], [stri], stri)
        K.dma("sp", mask.t[:], consts["mask"][d], [], [mask], mask)
        K.dma("sp", ones.t[:], consts["ones"], [], [ones], ones)
        qkT = [_sb(es, nc, "sc_qkT%d" % i, [N, 2 * G, 128], BF16) for i in range(2)]
        tokw = 1280 if ssd else 768
        tok = [_sb(es, nc, "sc_tok%d" % i, [128, tokw], BF16) for i in range(2)]
        la = [_sb(es, nc, "sc_la%d" % i, [128, 64], F32) for i in range(2)]
        ecum = [_sb(es, nc, "sc_ecum%d" % i, [128, 2 * H], F32) for i in range(2)]
        sm = [_sb(es, nc, "sc_sm%d" % i, [128, GU, 128], F32) for i in range(2)]
        rc = [_sb(es, nc, "sc_rc%d" % i, [128, J, 128], F32) for i in range(2)]
        E = [_sb(es, nc, "sc_E%d" % i, [128, J, 128], F32) for i in range(2)]
        M = [_sb(es, nc, "sc_M%d" % i, [128, J, 128], BF16) for i in range(2)]
        vd = [_sb(es, nc, "sc_vd%d" % i, [128, 512], BF16) for i in range(2)]
        vs = [_sb(es, nc, "sc_vs%d" % i, [128, 512], BF16) for i in range(2)]
        Y = [_sb(es, nc, "sc_Y%d" % i, [128, fam.YW], F32) for i in range(2)]
        Yp = [_sb(es, nc, "sc_Yp%d" % i, [128, fam.YW], F32) for i in range(2)]
        St = [_sb(es, nc, "sc_S%d" % u, [N, 512], F32) for u in range(NU)]
        Sb = [_sb(es, nc, "sc_Sb%d" % u, [N, 512], BF16) for u in range(NU)]
        lac = _sb(es, nc, "sc_lac", [128, 8], F32)
        p_sc = _ps(es, nc, "sc_psc", [128, GU, 128], F32)
        p_seg = _ps(es, nc, "sc_pseg", [128, J, 128], F32)
        p_cum = _ps(es, nc, "sc_pcum", [128, 2 * H], F32)
        p_yd = _ps(es, nc, "sc_pyd", [128, 512], F32)
        p_yo = _ps(es, nc, "sc_pyo", [128, 512], F32)
        p_st = _ps(es, nc, "sc_pst", [N, 512], F32)
        for u in range(NU):
            K.op("dve", "memset", [], [St[u]], St[u].t[:], 0.0)
            K.op("pool", "memset", [], [Sb[u]], Sb[u].t[:], 0.0)
        if not ssd:
            K.dma("sp", lac.t[:], P["ret_dec"][li], [], [lac], lac)
            K.op("act", "activation", [lac], [lac], out=lac.t[:], in_=lac.t[:], func=AF.Exp)
            K.op("dve", "tensor_scalar", [lac], [lac], out=lac.t[:], in0=lac.t[:], scalar1=-1.0, scalar2=None, op0=ALU.mult)
        def loads(ci):
            c = order[ci]
            b = ci % 2
            if ssd:
                K.dma("sp", qkT[b].t[:], srcs["BCT"].t[:, c * 128:(c + 1) * 128].rearrange("(ct p) t -> p ct t", p=128), [], [qkT[b]], qkT[b])
                K.dma("sp", tok[b].t[:], srcs["XSB"].t[c * 128:(c + 1) * 128, :], [], [tok[b]], tok[b])
                K.dma("sp", la[b].t[:], srcs["DTLA"].t[c * 128:(c + 1) * 128, :], [], [la[b]], la[b])
            else:
                K.dma("sp", qkT[b].t[:, 0:4, :], srcs["RQKT"].t[:, 4:8, c * 128:(c + 1) * 128], [], [qkT[b]], qkT[b])
                K.dma("sp", qkT[b].t[:, 4:8, :], srcs["RQKT"].t[:, 0:4, c * 128:(c + 1) * 128], [], [qkT[b]], qkT[b])
                K.dma("sp", tok[b].t[:], srcs["RTOK"].t[c * 128:(c + 1) * 128, :], [], [tok[b]], tok[b])
            if d == 1:
                K.dma("sp", Yp[b].t[:], YS.t[c * 128:(c + 1) * 128, :], [], [Yp[b]], Yp[b])

        loads(0)
        for ci, c in enumerate(order):
            b = ci % 2
            if ci + 1 < len(order):
                loads(ci + 1)
            if ssd:
                la_ap = la[b].t[:, 32 + d * 16:32 + d * 16 + 16]
                dt_ap = la[b].t[:, d * 16:d * 16 + 16]
                la_res = la[b]
                kT = lambda g: qkT[b].t[:, g, :]
                qT = lambda g: qkT[b].t[:, 2 + g, :]
                ktok = lambda g: tok[b].t[:, 1024 + g * 128:1024 + (g + 1) * 128]
            else:
                la_ap = lac.t[:, d * 4:d * 4 + 4]
                la_res = lac
                kT = lambda g: qkT[b].t[:, g, :]
                qT = lambda g: qkT[b].t[:, 4 + g, :]
                ktok = lambda g: tok[b].t[:, 512 + g * 64:512 + (g + 1) * 64]
            K.op("pe", "matmul", [tri, la_res], [p_cum], p_cum.t[:, 0:H], tri.t[:], la_ap, start=True, stop=True)
            K.op("pe", "matmul", [ones, la_res], [p_cum], p_cum.t[:, H:2 * H], ones.t[:], la_ap, start=True, stop=True)
            K.op("act", "activation", [p_cum], [ecum[b]], out=ecum[b].t[:], in_=p_cum.t[:], func=AF.Exp)
            for u, groups in enumerate(fam.units):
                h0 = u * J
                for gi, g in enumerate(groups):
                    K.op("pe", "matmul", [qkT[b]], [p_sc], p_sc.t[:, gi, :], kT(g), qT(g), start=True, stop=True)
                K.op("dve", "tensor_tensor", [p_sc, mask], [sm[b]], out=sm[b].t[:], in0=p_sc.t[:],
                     in1=bc(mask.t[:].unsqueeze(1), [128, GU, 128]), op=ALU.mult)
                Eb = E[b] if ssd else E[0]
                if ssd or ci == 0:
                    K.op("pool", "tensor_tensor", [la_res, tri], [rc[b]], out=rc[b].t[:],
                         in0=bc(la_ap[:, h0:h0 + J].unsqueeze(2), [128, J, 128]),
                         in1=bc(tri.t[:].unsqueeze(1), [128, J, 128]), op=ALU.mult)
                    for q4 in range(J // 4):
                        K.op("pe", "matmul", [stri, rc[b]], [p_seg], p_seg.t[:, q4 * 4:(q4 + 1) * 4, :], stri.t[:], rc[b].t[:, q4 * 4:(q4 + 1) * 4, :],
                             start=True, stop=True)
                    K.op("act", "activation", [p_seg], [Eb], out=Eb.t[:], in_=p_seg.t[:], func=AF.Exp)
                smv = bc(sm[b].t[:], [128, J, 128]) if GU == 1 else sm[b].t[:]
                K.op("dve", "tensor_tensor", [Eb, sm[b]], [M[b]], out=M[b].t[:], in0=Eb.t[:], in1=smv, op=ALU.mult)
                if ssd:
                    K.op("pool", "tensor_tensor", [tok[b], la[b]], [vd[b]], out=vd[b].t[:].rearrange("p (j q) -> p j q", q=PP),
                         in0=tok[b].t[:, u * 512:(u + 1) * 512].rearrange("p (j q) -> p j q", q=PP),
                         in1=bc(dt_ap[:, h0:h0 + J].unsqueeze(2), [128, J, PP]), op=ALU.mult)
                    vd_ap = vd[b].t[:]
                    vd_res = vd[b]
                else:
                    vd_ap = tok[b].t[:, 0:512]
                    vd_res = tok[b]
                K.op("dve", "tensor_tensor", [vd_res, Eb], [vs[b]], out=vs[b].t[:].rearrange("p (j q) -> p j q", q=PP),
                     in0=vd_ap.rearrange("p (j q) -> p j q", q=PP),
                     in1=bc(Eb.t[:, :, endcol:endcol + 1], [128, J, PP]), op=ALU.mult)
                for j in range(J):
                    K.op("pe", "matmul", [M[b], vd_res], [p_yd], p_yd.t[:, j * PP:(j + 1) * PP], M[b].t[:, j, :], vd_ap[:, j * PP:(j + 1) * PP],
                         start=True, stop=True)
                gw = 512 // GU
                for gi, g in enumerate(groups):
                    K.op("pe", "matmul", [qkT[b], Sb[u]], [p_yo], p_yo.t[:, gi * gw:(gi + 1) * gw], qT(g), Sb[u].t[:, gi * gw:(gi + 1) * gw],
                         start=True, stop=True)
                ysl = Y[b].t[:, u * 512:(u + 1) * 512]
                K.op("dve", "tensor_tensor", [p_yo, ecum[b]], [Y[b]], out=ysl.rearrange("p (j q) -> p j q", q=PP),
                     in0=p_yo.t[:].rearrange("p (j q) -> p j q", q=PP),
                     in1=bc(ecum[b].t[:, h0:h0 + J].unsqueeze(2), [128, J, PP]), op=ALU.mult)
                K.op("dve", "tensor_tensor", [p_yd, Y[b]], [Y[b]], out=ysl, in0=ysl, in1=p_yd.t[:], op=ALU.add)
                if d == 1:
                    K.op("pool", "tensor_tensor", [Yp[b], Y[b]], [Y[b]], out=ysl, in0=ysl, in1=Yp[b].t[:, u * 512:(u + 1) * 512], op=ALU.add)
                for gi, g in enumerate(groups):
                    K.op("pe", "matmul", [tok[b], vs[b]], [p_st], p_st.t[:, gi * gw:(gi + 1) * gw], ktok(g), vs[b].t[:, gi * gw:(gi + 1) * gw],
                         start=True, stop=True)
                K.op("pool", "tensor_tensor", [St[u], ecum[b]], [St[u]], out=St[u].t[:].rearrange("p (j q) -> p j q", q=PP),
                     in0=St[u].t[:].rearrange("p (j q) -> p j q", q=PP),
                     in1=bc(ecum[b].t[0:N, H + h0:H + h0 + J].unsqueeze(2), [N, J, PP]), op=ALU.mult)
                K.op("dve", "tensor_tensor", [St[u], p_st], [St[u]], out=St[u].t[:], in0=St[u].t[:], in1=p_st.t[:], op=ALU.add)
                K.op("act", "activation", [St[u]], [Sb[u]], out=Sb[u].t[:], in_=St[u].t[:], func=AF.Copy)
            K.dma("sp", YS.t[c * 128:(c + 1) * 128, :], Y[b].t[:], [Y[b]], [], Y[b])
        K.end_phase("scan_%s%d_%d" % (fam.name, li, d))


def rope_axial(K, x, tab, t1, o, nm):
    xv = x.t[:].rearrange("p (m hf two e) -> p m hf two e", hf=2, two=2, e=16)
    tv = t1.t[:].rearrange("p (m hf two e) -> p m hf two e", hf=2, two=2, e=16)
    sn = tab.t[:, 64:128].rearrange("p (hf two e) -> p hf two e", hf=2, two=2)
    K.op("pool", "tensor_tensor", [x, tab], [t1], out=tv[:, :, :, 0, :], in0=xv[:, :, :, 1, :],
         in1=bc(sn[:, :, 0, :].unsqueeze(1), [128, nm, 2, 16]), op=ALU.mult)
    K.op("pool", "tensor_tensor", [x, tab], [t1], out=tv[:, :, :, 1, :], in0=xv[:, :, :, 0, :],
         in1=bc(sn[:, :, 1, :].unsqueeze(1), [128, nm, 2, 16]), op=ALU.mult)
    K.op("dve", "tensor_tensor", [x, tab], [o], out=o.t[:].rearrange("p (m d) -> p m d", d=64),
         in0=x.t[:].rearrange("p (m d) -> p m d", d=64), in1=bc(tab.t[:, 0:64].unsqueeze(1), [128, nm, 64]), op=ALU.mult)
    K.op("dve", "tensor_tensor", [o, t1], [o], out=o.t[:], in0=o.t[:], in1=t1.t[:], op=ALU.add)


def phase_attn_prep(K, li, PTOK, KT, VA, QT, KMAX, KM8, NB, consts):
    nc = K.nc
    NCH = K.NCH
    S = K.new_phase()
    with ExitStack() as es:
        identb = _sb(es, nc, "ap_id", [128, 128], BF16)
        identf = _sb(es, nc, "ap_idf", [128, 128], F32)
        ones = _sb(es, nc, "ap_ones", [128, 128], F32)
        x = [_sb(es, nc, "ap_x%d" % i, [128, 512], F32) for i in range(2)]
        v = [_sb(es, nc, "ap_v%d" % i, [128, 512], F32) for i in range(2)]
        tab = [_sb(es, nc, "ap_tab%d" % i, [128, 128], F32) for i in range(2)]
        t1 = [_sb(es, nc, "ap_t1%d" % i, [128, 512], F32) for i in range(2)]
        o = [_sb(es, nc, "ap_o%d" % i, [128, 512], F32) for i in range(2)]
        sq = [_sb(es, nc, "ap_sq%d" % i, [128, 512], F32) for i in range(2)]
        ks = [_sb(es, nc, "ap_ks%d" % i, [128, 8], F32) for i in range(2)]
        ka = [_sb(es, nc, "ap_ka%d" % i, [128, 512], BF16) for i in range(2)]
        va = [_sb(es, nc, "ap_va%d" % i, [128, 4, 129], BF16) for i in range(2)]
        kT = [_sb(es, nc, "ap_kT%d" % i, [128, 4, 128], BF16) for i in range(2)]
        kmx = _sb(es, nc, "ap_kmx", [128, 8], F32)
        km2 = _sb(es, nc, "ap_km2", [8, 1], F32)
        dg = _sb(es, nc, "ap_dg", [8, 8], F32)
        kbc = _sb(es, nc, "ap_kbc", [128, 8], F32)
        pt = [_ps(es, nc, "ap_pt%d" % i, [128, 4, 128], BF16) for i in range(2)]
        pk = _ps(es, nc, "ap_pk", [128, 128], F32)
        K.dma("sp", identb.t[:], consts["identb"], [], [identb], identb)
        K.dma("sp", identf.t[:], consts["identf"], [], [identf], identf)
        K.dma("sp", ones.t[:], consts["ones"], [], [ones], ones)
        K.op("dve", "memset", [], [kmx], kmx.t[:], 0.0)
        for i in range(2):
            K.op("pool", "memset", [], [va[i]], va[i].t[:], 1.0)
        def loads(c):
            b = c % 2
            K.dma("sp", x[b].t[:], PTOK.t[c * 128:(c + 1) * 128, PT_DK:PT_DK + 512], [], [x[b]], x[b])
            K.dma("sp", v[b].t[:], PTOK.t[c * 128:(c + 1) * 128, PT_DV:PT_DV + 512], [], [v[b]], v[b])
            K.dma("sp", tab[b].t[:], consts["diff_tab"][c * 128:(c + 1) * 128, :], [], [tab[b]], tab[b])

        loads(0)
        for c in range(NCH):
            b = c % 2
            if c + 1 < NCH:
                loads(c + 1)
            rope_axial(K, x[b], tab[b], t1[b], o[b], 8)
            K.op("act", "activation", [o[b]], [ka[b]], out=ka[b].t[:], in_=o[b].t[:], func=AF.Copy)
            K.op("pool", "tensor_tensor", [o[b]], [sq[b]], out=sq[b].t[:], in0=o[b].t[:], in1=o[b].t[:], op=ALU.mult)
            K.op("dve", "reduce_sum", [sq[b]], [ks[b]], out=ks[b].t[:], in_=sq[b].t[:].rearrange("p (m d) -> p m d", d=64), axis=AX.X)
            K.op("dve", "tensor_tensor", [ks[b], kmx], [kmx], out=kmx.t[:], in0=kmx.t[:], in1=ks[b].t[:], op=ALU.max)
            for m in range(4):
                K.op("pe", "transpose", [ka[b], identb], [pt[b]], pt[b].t[:, m, :], ka[b].t[:, m * 128:(m + 1) * 128], identb.t[:])
            K.op("act", "activation", [pt[b]], [kT[b]], out=kT[b].t[:], in_=pt[b].t[:], func=AF.Copy)
            K.dma("sp", KT.t[:, :, c * 128:(c + 1) * 128], kT[b].t[:], [kT[b]], [], kT[b])
            K.op("dve", "tensor_copy", [v[b]], [va[b]], out=va[b].t[:, :, 0:128], in_=v[b].t[:].rearrange("p (h d) -> p h d", d=128))
            K.dma("sp", VA.t[:, :, c, :].rearrange("h p w -> p h w"), va[b].t[:, :, 0:128], [va[b]], [], va[b])
        K.op("pe", "transpose", [kmx, identf], [pk], pk.t[0:8, :], kmx.t[:], identf.t[:])
        K.op("dve", "reduce_max", [pk], [km2], out=km2.t[:], in_=pk.t[0:8, :], axis=AX.X)
        K.op("act", "activation", [km2], [km2], out=km2.t[:], in_=km2.t[:], func=AF.Sqrt)
        K.op("dve", "tensor_scalar", [km2, identf], [dg], out=dg.t[:], in0=identf.t[0:8, 0:8], scalar1=km2.t[:, 0:1], scalar2=None, op0=ALU.mult)
        K.op("pe", "matmul", [ones, dg], [pk], pk.t[:, 0:8], ones.t[0:8, :], dg.t[:], start=True, stop=True)
        K.op("dve", "tensor_copy", [pk], [kbc], out=kbc.t[:], in_=pk.t[:, 0:8])
        K.dma("sp", KMAX.t[:], kbc.t[:], [kbc], [], kbc)
        K.dma("sp", KM8.t[:], km2.t[:], [km2], [], km2)
        K.end_phase("attnprep1_%d" % li)
    S = K.new_phase()
    with ExitStack() as es:
        identb = _sb(es, nc, "aq_id", [128, 128], BF16)
        kbc = _sb(es, nc, "aq_kbc", [128, 8], F32)
        x = [_sb(es, nc, "aq_x%d" % i, [128, 512], F32) for i in range(2)]
        tab = [_sb(es, nc, "aq_tab%d" % i, [128, 128], F32) for i in range(2)]
        t1 = [_sb(es, nc, "aq_t1%d" % i, [128, 512], F32) for i in range(2)]
        o = [_sb(es, nc, "aq_o%d" % i, [128, 512], F32) for i in range(2)]
        sq = [_sb(es, nc, "aq_sq%d" % i, [128, 512], F32) for i in range(2)]
        qs = [_sb(es, nc, "aq_qs%d" % i, [128, 8], F32) for i in range(2)]
        qa = [_sb(es, nc, "aq_qa%d" % i, [128, 512], BF16) for i in range(2)]
        qT = [_sb(es, nc, "aq_qT%d" % i, [128, 4, 128], BF16) for i in range(2)]
        pt = [_ps(es, nc, "aq_pt%d" % i, [128, 4, 128], BF16) for i in range(2)]
        K.dma("sp", identb.t[:], consts["identb"], [], [identb], identb)
        identf = _sb(es, nc, "aq_idf", [128, 128], F32)
        ones = _sb(es, nc, "aq_ones", [128, 128], F32)
        qmx = _sb(es, nc, "aq_qmx", [128, 8], F32)
        km8 = _sb(es, nc, "aq_km8", [8, 1], F32)
        qm8 = _sb(es, nc, "aq_qm8", [8, 1], F32)
        dg = _sb(es, nc, "aq_dg", [8, 8], F32)
        nbt = _sb(es, nc, "aq_nb", [128, 4], F32)
        pk = _ps(es, nc, "aq_pk", [128, 128], F32)
        K.dma("sp", identf.t[:], consts["identf"], [], [identf], identf)
        K.dma("sp", ones.t[:], consts["ones"], [], [ones], ones)
        K.dma("sp", km8.t[:], KM8.t[:], [], [km8], km8)
        K.op("dve", "memset", [], [qmx], qmx.t[:], 0.0)
        def loads(c):
            b = c % 2
            K.dma("sp", x[b].t[:], PTOK.t[c * 128:(c + 1) * 128, PT_DQ:PT_DQ + 512], [], [x[b]], x[b])
            K.dma("sp", tab[b].t[:], consts["diff_tab"][c * 128:(c + 1) * 128, :], [], [tab[b]], tab[b])

        loads(0)
        for c in range(NCH):
            b = c % 2
            if c + 1 < NCH:
                loads(c + 1)
            rope_axial(K, x[b], tab[b], t1[b], o[b], 8)
            K.op("act", "activation", [o[b]], [qa[b]], out=qa[b].t[:], in_=o[b].t[:], func=AF.Copy)
            K.op("pool", "tensor_tensor", [o[b]], [sq[b]], out=sq[b].t[:], in0=o[b].t[:], in1=o[b].t[:], op=ALU.mult)
            K.op("dve", "reduce_sum", [sq[b]], [qs[b]], out=qs[b].t[:], in_=sq[b].t[:].rearrange("p (m d) -> p m d", d=64), axis=AX.X)
            K.op("dve", "tensor_tensor", [qs[b], qmx], [qmx], out=qmx.t[:], in0=qmx.t[:], in1=qs[b].t[:], op=ALU.max)
            for m in range(4):
                K.op("pe", "transpose", [qa[b], identb], [pt[b]], pt[b].t[:, m, :], qa[b].t[:, m * 128:(m + 1) * 128], identb.t[:])
            K.op("act", "activation", [pt[b]], [qT[b]], out=qT[b].t[:], in_=pt[b].t[:], func=AF.Copy)
            K.dma("sp", QT.t[:, :, c * 128:(c + 1) * 128], qT[b].t[:], [qT[b]], [], qT[b])
        K.op("pe", "transpose", [qmx, identf], [pk], pk.t[0:8, :], qmx.t[:], identf.t[:])
        K.op("dve", "reduce_max", [pk], [qm8], out=qm8.t[:], in_=pk.t[0:8, :], axis=AX.X)
        K.op("act", "activation", [qm8], [qm8], out=qm8.t[:], in_=qm8.t[:], func=AF.Sqrt)
        K.op("dve", "tensor_tensor", [qm8, km8], [qm8], out=qm8.t[:], in0=qm8.t[:], in1=km8.t[:], op=ALU.mult)
        K.op("dve", "tensor_scalar", [qm8, identf], [dg], out=dg.t[:], in0=identf.t[0:8, 0:8], scalar1=qm8.t[:, 0:1], scalar2=None, op0=ALU.mult)
        K.op("pe", "matmul", [ones, dg], [pk], pk.t[:, 0:8], ones.t[0:8, :], dg.t[:], start=True, stop=True)
        K.op("dve", "tensor_copy", [pk], [kbc], out=kbc.t[:], in_=pk.t[:, 0:8])
        pkv = kbc.t[:].rearrange("p (h m) -> p h m", m=2)
        K.op("dve", "tensor_tensor", [kbc], [nbt], out=nbt.t[:], in0=pkv[:, :, 0], in1=pkv[:, :, 1], op=ALU.max)
        K.op("dve", "tensor_scalar", [nbt], [nbt], out=nbt.t[:], in0=nbt.t[:], scalar1=-0.125, scalar2=None, op0=ALU.mult)
        K.dma("sp", NB.t[:], nbt.t[:], [nbt], [], nbt)
        K.end_phase("attnprep2_%d" % li)


def phase_attn(K, li, KT, VA, QT, NB, P, OD, consts):
    nc = K.nc
    NCH = K.NCH
    T = K.T
    S = K.new_phase()
    lam_init = 0.8 - 0.6 * math.exp(-0.3 * li)
    with ExitStack() as es:
        dl = _sb(es, nc, "at_dl", [128, 256], F32)
        dp = _sb(es, nc, "at_dp", [128, 128], F32)
        ds = _sb(es, nc, "at_ds", [128, 2], F32)
        nlam = _sb(es, nc, "at_nlam", [128, 1], F32)
        identf = _sb(es, nc, "at_idf", [128, 128], F32)
        ones = _sb(es, nc, "at_ones", [128, 128], F32)
        kt = [_sb(es, nc, "at_kt%d" % i, [128, T], BF16) for i in range(2)]
        nb = _sb(es, nc, "at_nb", [128, 4], F32)
        vs = [_sb(es, nc, "at_vs%d" % i, [128, NCH, 128], BF16) for i in range(2)]
        qt = [_sb(es, nc, "at_qt%d" % i, [128, 512], BF16) for i in range(2)]
        pT = [_sb(es, nc, "at_pT%d" % i, [128, 2, 512], BF16) for i in range(3)]
        lacc = [_sb(es, nc, "at_la%d" % i, [128, 2, 512], F32) for i in range(2)]
        rl = [_sb(es, nc, "at_rl%d" % i, [1, 2, 512], F32) for i in range(2)]
        bcs = [_sb(es, nc, "at_bc%d" % i, [128, 2, 512], F32) for i in range(2)]
        t0 = [_sb(es, nc, "at_t0%d" % i, [128, 512], F32) for i in range(2)]
        t1 = [_sb(es, nc, "at_t1%d" % i, [128, 512], F32) for i in range(2)]
        od = [_sb(es, nc, "at_od%d" % i, [128, 4, 128], F32) for i in range(2)]
        p_s = [_ps(es, nc, "at_ps%d" % i, [128, 2, 512], F32) for i in range(2)]
        p_o = [[_ps(es, nc, "at_po%d%d" % (i, m), [128, 512], F32) for m in range(2)] for i in range(2)]
        K.dma("sp", dl.t[:], P["diff_lam"][li], [], [dl], dl)
        K.dma("sp", nb.t[:], NB.t[:], [], [nb], nb)
        K.dma("sp", identf.t[:], consts["identf"], [], [identf], identf)
        K.dma("sp", ones.t[:], consts["ones"], [], [ones], ones)
        dlv = dl.t[:].rearrange("p (a two d) -> p a two d", a=2, two=2)
        K.op("dve", "tensor_tensor", [dl], [dp], out=dp.t[:].rearrange("p (a d) -> p a d", a=2), in0=dlv[:, :, 0, :], in1=dlv[:, :, 1, :], op=ALU.mult)
        K.op("dve", "reduce_sum", [dp], [ds], out=ds.t[:], in_=dp.t[:].rearrange("p (a d) -> p a d", a=2), axis=AX.X)
        K.op("act", "activation", [ds], [ds], out=ds.t[:], in_=ds.t[:], func=AF.Exp)
        K.op("dve", "scalar_tensor_tensor", [ds], [nlam], out=nlam.t[:], in0=ds.t[:, 1:2], scalar=-lam_init, in1=ds.t[:, 0:1],
             op0=ALU.add, op1=ALU.subtract)
        qtiles = [(0, NCTX, 0, NCTX // 128)] + [(q0, 512, 0, NCH) for q0 in range(NCTX, T, 512)]
        si = 0
        pi = 0
        ti = 0
        pending = []
        for h in range(4):
            hb = h % 2
            K.dma("sp", kt[hb].t[:], KT.t[:, h, :], [], [kt[hb]], kt[hb])
            K.dma("sp", vs[hb].t[:], VA.t[h], [], [vs[hb]], vs[hb])
            for (q0, QW, kc0, kc1) in qtiles:
                tb = ti % 2
                ti += 1
                nqb = QW // 128
                K.dma("sp", qt[tb].t[:, 0:QW], QT.t[:, h, q0:q0 + QW], [], [qt[tb]], qt[tb])
                po = p_o[tb]
                la_ = lacc[tb]
                kcs = list(range(kc0, kc1))
                bufs = {}

                def qk(kc):
                    nonlocal si, pi
                    ps = p_s[si % 2]
                    pt_ = pT[pi % 3]
                    si += 1
                    pi += 1
                    bufs[kc] = (ps, pt_)
                    for m in range(2):
                        K.op("pe", "matmul", [kt[hb], qt[tb]], [ps], ps.t[:, m, 0:QW], kt[hb].t[64 * m:64 * m + 64, kc * 128:(kc + 1) * 128],
                             qt[tb].t[64 * m:64 * m + 64, 0:QW], start=True, stop=True)
                    K.op("act", "activation", [ps, nb], [pt_], out=pt_.t[:, :, 0:QW], in_=ps.t[:, :, 0:QW], func=AF.Exp, scale=0.125,
                         bias=nb.t[:, h:h + 1])

                def av(kc):
                    ps, pt_ = bufs.pop(kc)
                    for m in range(2):
                        K.op("pe", "matmul", [pt_, vs[hb]], [po[m]], po[m].t[:, 0:QW], vs[hb].t[:, kc, :], pt_.t[:, m, 0:QW],
                             start=(kc == kc0), stop=(kc == kc1 - 1))
                    if kc == kc0:
                        K.op("dve", "tensor_copy", [pt_], [la_], out=la_.t[:, :, 0:QW], in_=pt_.t[:, :, 0:QW])
                    else:
                        K.op("dve", "tensor_tensor", [pt_, la_], [la_], out=la_.t[:, :, 0:QW], in0=la_.t[:, :, 0:QW], in1=pt_.t[:, :, 0:QW], op=ALU.add)

                qk(kcs[0])
                for i_, kc in enumerate(kcs):
                    if i_ + 1 < len(kcs):
                        qk(kcs[i_ + 1])
                    av(kc)
                    if i_ == 2 and pending:
                        pending.pop(0)()
                    if i_ == 5 and pending:
                        pending.pop(0)()
                while pending:
                    pending.pop(0)()

                def fin_a(tb=tb, QW=QW, la_=la_):
                    nonlocal si
                    ps = p_s[si % 2]
                    si += 1
                    for m in range(2):
                        K.op("pe", "matmul", [ones, la_], [ps], ps.t[0:1, m, 0:QW], ones.t[:, 0:1], la_.t[:, m, 0:QW], start=True, stop=True)
                    K.op("dve", "reciprocal", [ps], [rl[tb]], out=rl[tb].t[:, :, 0:QW], in_=ps.t[0:1, :, 0:QW])
                    K.op("dve", "tensor_scalar", [rl[tb], nlam], [rl[tb]], out=rl[tb].t[:, 1, 0:QW], in0=rl[tb].t[:, 1, 0:QW], scalar1=nlam.t[0:1, 0:1], scalar2=None, op0=ALU.mult)
                    ps2 = p_s[si % 2]
                    si += 1
                    for m in range(2):
                        K.op("pe", "matmul", [ones, rl[tb]], [ps2], ps2.t[:, m, 0:QW], ones.t[0:1, :], rl[tb].t[0:1, m, 0:QW], start=True, stop=True)
                    K.op("act", "activation", [ps2], [bcs[tb]], out=bcs[tb].t[:, :, 0:QW], in_=ps2.t[:, :, 0:QW], func=AF.Copy)

                def fin_b(tb=tb, QW=QW, po=po, nqb=nqb, q0=q0, h=h):
                    nonlocal si
                    K.op("dve", "tensor_tensor", [po[0], bcs[tb]], [t0[tb]], out=t0[tb].t[:, 0:QW], in0=po[0].t[:, 0:QW], in1=bcs[tb].t[:, 0, 0:QW], op=ALU.mult)
                    K.op("dve", "tensor_tensor", [po[1], bcs[tb]], [t1[tb]], out=t1[tb].t[:, 0:QW], in0=po[1].t[:, 0:QW], in1=bcs[tb].t[:, 1, 0:QW], op=ALU.mult)
                    K.op("pool", "tensor_tensor", [t0[tb], t1[tb]], [t0[tb]], out=t0[tb].t[:, 0:QW], in0=t0[tb].t[:, 0:QW], in1=t1[tb].t[:, 0:QW], op=ALU.add)
                    ps3 = p_s[si % 2]
                    si += 1
                    for qb in range(nqb):
                        K.op("pe", "transpose", [t0[tb], identf], [ps3], ps3.t[:, 0, qb * 128:(qb + 1) * 128], t0[tb].t[:, qb * 128:(qb + 1) * 128], identf.t[:])
                    K.op("act", "activation", [ps3], [od[tb]], out=od[tb].t[:, 0:nqb, :], in_=ps3.t[:, 0, 0:QW].rearrange("p (a b) -> p a b", b=128), func=AF.Copy)
                    K.dma("sp", OD.t[q0:q0 + QW, h * 128:(h + 1) * 128].rearrange("(qb p) d -> p qb d", p=128), od[tb].t[:, 0:nqb, :], [od[tb]], [], od[tb])

                pending.extend([fin_a, fin_b])
        while pending:
            pending.pop(0)()
        K.end_phase("attn%d" % li)


ALPHA = (2.0 * 2) ** 0.25
EPS = 1e-5


def resid_ln(K, ps_halves, xres, gate, lnw, lnb, r, tmp, st, out_buf):
    for hf in range(2):
        K.op("dve", "tensor_tensor", [ps_halves[hf], gate], [r], out=r.t[:, hf * 512:(hf + 1) * 512], in0=ps_halves[hf].t[:],
             in1=gate.t[:, hf * 512:(hf + 1) * 512], op=ALU.mult)
    K.op("dve", "scalar_tensor_tensor", [xres, r], [r], out=r.t[:], in0=xres.t[:], scalar=ALPHA, in1=r.t[:], op0=ALU.mult, op1=ALU.add)
    K.op("dve", "reduce_sum", [r], [st], out=st.t[:, 0:1], in_=r.t[:], axis=AX.X)
    K.op("dve", "tensor_scalar", [st], [st], out=st.t[:, 0:1], in0=st.t[:, 0:1], scalar1=-1.0 / 1024, scalar2=None, op0=ALU.mult)
    K.op("act", "activation", [r, st], [r], out=r.t[:], in_=r.t[:], func=AF.Identity, bias=st.t[:, 0:1])
    K.op("act", "activation", [r], [tmp], out=tmp.t[:], in_=r.t[:], func=AF.Square)
    K.op("dve", "reduce_sum", [tmp], [st], out=st.t[:, 1:2], in_=tmp.t[:], axis=AX.X)
    K.op("act", "activation", [st], [st], out=st.t[:, 1:2], in_=st.t[:, 1:2], func=AF.Sqrt, scale=1.0 / 1024, bias=EPS)
    K.op("dve", "reciprocal", [st], [st], out=st.t[:, 1:2], in_=st.t[:, 1:2])
    K.op("dve", "scalar_tensor_tensor", [r, st, lnw], [tmp], out=tmp.t[:], in0=r.t[:], scalar=st.t[:, 1:2], in1=lnw.t[:], op0=ALU.mult, op1=ALU.mult)
    K.op("pool", "tensor_tensor", [tmp, lnb], [out_buf], out=out_buf.t[:], in0=tmp.t[:], in1=lnb.t[:], op=ALU.add)


def phase_post_outproj(K, li, X, MOD, PTOK, XSB, YSS, YSR, OD, P, w_out, X1, consts):
    nc = K.nc
    NCH = K.NCH
    S = K.new_phase()
    lam_init = 0.8 - 0.6 * math.exp(-0.3 * li)
    with ExitStack() as es:
        W = _sb(es, nc, "po_w", [128, 16, 1024], BF16)
        identb = _sb(es, nc, "po_id", [128, 128], BF16)
        gate = [_sb(es, nc, "po_g%d" % i, [128, 1024], F32) for i in range(2)]
        lnw = _sb(es, nc, "po_lnw", [128, 1024], F32)
        lnb = _sb(es, nc, "po_lnb", [128, 1024], F32)
        snw = _sb(es, nc, "po_snw", [128, 1024], F32)
        dsk = _sb(es, nc, "po_dsk", [128, 16], F32)
        rnw = _sb(es, nc, "po_rnw", [128, 128], F32)
        dnw = _sb(es, nc, "po_dnw", [128, 128], F32)
        ys = [_sb(es, nc, "po_ys%d" % i, [128, 1024], F32) for i in range(2)]
        xs = [_sb(es, nc, "po_xs%d" % i, [128, 1024], BF16) for i in range(2)]
        z = [_sb(es, nc, "po_z%d" % i, [128, 1024], F32) for i in range(2)]
        t = [_sb(es, nc, "po_t%d" % i, [128, 1024], F32) for i in range(2)]
        yr = [_sb(es, nc, "po_yr%d" % i, [128, 512], F32) for i in range(2)]
        rg = [_sb(es, nc, "po_rg%d" % i, [128, 512], F32) for i in range(2)]
        od = [_sb(es, nc, "po_od%d" % i, [128, 512], F32) for i in range(2)]
        t5 = [_sb(es, nc, "po_t5%d" % i, [128, 512], F32) for i in range(2)]
        t6 = [_sb(es, nc, "po_t6%d" % i, [128, 512], F32) for i in range(2)]
        st = [_sb(es, nc, "po_st%d" % i, [128, 16], F32) for i in range(2)]
        std = [_sb(es, nc, "po_std%d" % i, [128, 16], F32) for i in range(2)]
        str_ = [_sb(es, nc, "po_str%d" % i, [128, 16], F32) for i in range(2)]
        ycat = [_sb(es, nc, "po_yc%d" % i, [128, 2048], BF16) for i in range(2)]
        ycT = [_sb(es, nc, "po_ycT%d" % i, [128, 16, 128], BF16) for i in range(2)]
        xr = [_sb(es, nc, "po_xr%d" % i, [128, 1024], F32) for i in range(2)]
        r = [_sb(es, nc, "po_r%d" % i, [128, 1024], F32) for i in range(2)]
        ob = [_sb(es, nc, "po_ob%d" % i, [128, 1024], F32) for i in range(2)]
        pt = [_ps(es, nc, "po_pt%d" % i, [128, 8, 128], BF16) for i in range(2)]
        py = [_ps(es, nc, "po_py%d" % i, [128, 512], F32) for i in range(4)]
        wres = load_weight_bf16(S, W, w_out[li], 16)
        K.dma("sp", identb.t[:], consts["identb"], [], [identb], identb)
        for s_ in range(2):
            K.dma("sp", gate[s_].t[:], MOD.t[s_, :, 2048:3072], [], [gate[s_]], gate[s_])
        K.dma("sp", lnw.t[:], P["ln1_w"][li], [], [lnw], lnw)
        K.dma("sp", lnb.t[:], P["ln1_b"][li], [], [lnb], lnb)
        K.dma("sp", snw.t[:], P["ssd_nw"][li], [], [snw], snw)
        K.dma("sp", dsk.t[:], P["ssd_d"][li], [], [dsk], dsk)
        K.dma("sp", rnw.t[:], P["ret_nw"][li], [], [rnw], rnw)
        K.dma("sp", dnw.t[:], P["diff_nw"][li], [], [dnw], dnw)
        def loads(c):
            b = c % 2
            rows = slice(c * 128, (c + 1) * 128)
            K.dma("sp", ys[b].t[:], YSS.t[rows, :], [], [ys[b]], ys[b])
            K.dma("sp", xs[b].t[:], XSB.t[rows, 0:1024], [], [xs[b]], xs[b])
            K.dma("sp", z[b].t[:], PTOK.t[rows, PT_Z:PT_Z + 1024], [], [z[b]], z[b])
            K.dma("sp", yr[b].t[:], YSR.t[rows, :], [], [yr[b]], yr[b])
            K.dma("sp", rg[b].t[:], PTOK.t[rows, PT_RG:PT_RG + 512], [], [rg[b]], rg[b])
            K.dma("sp", od[b].t[:], OD.t[rows, :], [], [od[b]], od[b])
            K.dma("sp", xr[b].t[:], X.t[rows, :], [], [xr[b]], xr[b])

        def stA(c):
            b = c % 2
            K.begin_chain()
            K.op("pool", "tensor_tensor", [xs[b], dsk], [t[b]], out=t[b].t[:].rearrange("p (h q) -> p h q", q=64),
                 in0=xs[b].t[:].rearrange("p (h q) -> p h q", q=64), in1=bc(dsk.t[:].unsqueeze(2), [128, 16, 64]), op=ALU.mult)
            K.op("dve", "tensor_tensor", [ys[b], t[b]], [ys[b]], out=ys[b].t[:], in0=ys[b].t[:], in1=t[b].t[:], op=ALU.add)
            K.op("act", "activation", [z[b]], [z[b]], out=z[b].t[:], in_=z[b].t[:], func=AF.Silu)
            K.op("dve", "tensor_tensor", [ys[b], z[b]], [ys[b]], out=ys[b].t[:], in0=ys[b].t[:], in1=z[b].t[:], op=ALU.mult)
            K.op("pool", "tensor_tensor", [ys[b]], [t[b]], out=t[b].t[:], in0=ys[b].t[:], in1=ys[b].t[:], op=ALU.mult)
            K.op("dve", "reduce_sum", [t[b]], [st[b]], out=st[b].t[:, 0:1], in_=t[b].t[:], axis=AX.X)
            K.op("act", "activation", [st[b]], [st[b]], out=st[b].t[:, 0:1], in_=st[b].t[:, 0:1], func=AF.Sqrt, scale=1.0 / 1024, bias=EPS)
            K.op("dve", "reciprocal", [st[b]], [st[b]], out=st[b].t[:, 0:1], in_=st[b].t[:, 0:1])
            K.op("dve", "scalar_tensor_tensor", [ys[b], st[b], snw], [ycat[b]], out=ycat[b].t[:, 0:1024], in0=ys[b].t[:], scalar=st[b].t[:, 0:1],
                 in1=snw.t[:], op0=ALU.mult, op1=ALU.mult)
            K.begin_chain()
            odv = od[b].t[:].rearrange("p (h d) -> p h d", d=128)
            t5v = t5[b].t[:].rearrange("p (h d) -> p h d", d=128)
            K.op("pool", "tensor_tensor", [od[b]], [t5[b]], out=t5[b].t[:], in0=od[b].t[:], in1=od[b].t[:], op=ALU.mult)
            K.op("dve", "reduce_sum", [t5[b]], [std[b]], out=std[b].t[:, 4:8], in_=t5v, axis=AX.X)
            K.op("act", "activation", [std[b]], [std[b]], out=std[b].t[:, 4:8], in_=std[b].t[:, 4:8], func=AF.Sqrt, scale=1.0 / 128, bias=EPS)
            K.op("dve", "reciprocal", [std[b]], [std[b]], out=std[b].t[:, 4:8], in_=std[b].t[:, 4:8])
            K.op("dve", "tensor_tensor", [od[b], std[b]], [od[b]], out=odv, in0=odv, in1=bc(std[b].t[:, 4:8].unsqueeze(2), [128, 4, 128]), op=ALU.mult)
            K.op("dve", "scalar_tensor_tensor", [od[b], dnw], [ycat[b]], out=ycat[b].t[:, 1024:1536].rearrange("p (h d) -> p h d", d=128), in0=odv,
                 scalar=1.0 - lam_init, in1=bc(dnw.t[:].unsqueeze(1), [128, 4, 128]), op0=ALU.mult, op1=ALU.mult)
            K.begin_chain()
            yrv = yr[b].t[:].rearrange("p (h d) -> p h d", d=128)
            t6v = t6[b].t[:].rearrange("p (h d) -> p h d", d=128)
            K.op("dve", "reduce_sum", [yr[b]], [str_[b]], out=str_[b].t[:, 8:12], in_=yrv, axis=AX.X)
            K.op("dve", "tensor_scalar", [str_[b]], [str_[b]], out=str_[b].t[:, 8:12], in0=str_[b].t[:, 8:12], scalar1=1.0 / 128, scalar2=None, op0=ALU.mult)
            K.op("dve", "tensor_tensor", [yr[b], str_[b]], [yr[b]], out=yrv, in0=yrv, in1=bc(str_[b].t[:, 8:12].unsqueeze(2), [128, 4, 128]), op=ALU.subtract)
            K.op("pool", "tensor_tensor", [yr[b]], [t6[b]], out=t6[b].t[:], in0=yr[b].t[:], in1=yr[b].t[:], op=ALU.mult)
            K.op("dve", "reduce_sum", [t6[b]], [str_[b]], out=str_[b].t[:, 12:16], in_=t6v, axis=AX.X)
            K.op("act", "activation", [str_[b]], [str_[b]], out=str_[b].t[:, 12:16], in_=str_[b].t[:, 12:16], func=AF.Sqrt, scale=1.0 / 128, bias=EPS)
            K.op("dve", "reciprocal", [str_[b]], [str_[b]], out=str_[b].t[:, 12:16], in_=str_[b].t[:, 12:16])
            K.op("dve", "tensor_tensor", [yr[b], str_[b]], [yr[b]], out=yrv, in0=yrv, in1=bc(str_[b].t[:, 12:16].unsqueeze(2), [128, 4, 128]), op=ALU.mult)
            K.op("pool", "tensor_tensor", [yr[b], rnw], [yr[b]], out=yrv, in0=yrv, in1=bc(rnw.t[:].unsqueeze(1), [128, 4, 128]), op=ALU.mult)
            K.op("act", "activation", [rg[b]], [rg[b]], out=rg[b].t[:], in_=rg[b].t[:], func=AF.Silu)
            K.op("dve", "tensor_tensor", [yr[b], rg[b]], [ycat[b]], out=ycat[b].t[:, 1536:2048], in0=yr[b].t[:], in1=rg[b].t[:], op=ALU.mult)
            K.flush_chains()

        def stB(c):
            b = c % 2
            for half in range(2):
                for k8 in range(8):
                    kc = half * 8 + k8
                    K.op("pe", "transpose", [ycat[b], identb], [pt[half]], pt[half].t[:, k8, :], ycat[b].t[:, kc * 128:(kc + 1) * 128], identb.t[:])
                if half == 0:
                    K.op("act", "activation", [pt[half]], [ycT[b]], out=ycT[b].t[:, 0:8, :], in_=pt[half].t[:], func=AF.Copy)
                else:
                    K.op("dve", "tensor_copy", [pt[half]], [ycT[b]], out=ycT[b].t[:, 8:16, :], in_=pt[half].t[:])
            pys = [py[(c % 2) * 2], py[(c % 2) * 2 + 1]]
            for hf in range(2):
                for kc in range(16):
                    K.op("pe", "matmul", [ycT[b]] + wres, [pys[hf]], pys[hf].t[:], ycT[b].t[:, kc, :], W.t[:, kc, hf * 512:(hf + 1) * 512],
                         start=(kc == 0), stop=(kc == 15))

        def stC(c):
            b = c % 2
            src = 1 if c < NCTX // 128 else 0
            rows = slice(c * 128, (c + 1) * 128)
            pys = [py[(c % 2) * 2], py[(c % 2) * 2 + 1]]
            resid_ln(K, pys, xr[b], gate[src], lnw, lnb, r[b], t[b], st[b], ob[b])
            K.dma("sp", X1.t[rows, :], ob[b].t[:], [ob[b]], [], ob[b])

        loads(0)
        if NCH > 1:
            loads(1)
        stA(0)
        for c in range(NCH):
            stB(c)
            if c + 1 < NCH:
                stA(c + 1)
            stC(c)
            if c + 2 < NCH:
                loads(c + 2)
        K.end_phase("post%d" % li)


def phase_ffn_down(K, li, X1, MOD, UVT, P, w_down, XN, out_ap, consts):
    nc = K.nc
    NCH = K.NCH
    S = K.new_phase()
    NT = DFF // 128
    with ExitStack() as es:
        W = _sb(es, nc, "fd_w", [128, NT, 1024], BF16)
        gate = [_sb(es, nc, "fd_g%d" % i, [128, 1024], F32) for i in range(2)]
        lnw = _sb(es, nc, "fd_lnw", [128, 1024], F32)
        lnb = _sb(es, nc, "fd_lnb", [128, 1024], F32)
        cw = _sb(es, nc, "fd_cw", [128, NT, 3], F32)
        cb = _sb(es, nc, "fd_cb", [128, NT], F32)
        uh = [_sb(es, nc, "fd_uh%d" % i, [128, NT, 130], F32) for i in range(3)]
        vv = [_sb(es, nc, "fd_v%d" % i, [128, NT, 128], F32) for i in range(2)]
        acc = [_sb(es, nc, "fd_acc%d" % i, [128, NT, 128], F32) for i in range(2)]
        accsl = [[Buf(acc[i].t, "accsl") for ct in range(NT)] for i in range(2)]
        gl = [_sb(es, nc, "fd_gl", [128, NT, 128], F32)] * 2
        gT = [_sb(es, nc, "fd_gT%d" % i, [128, NT, 128], BF16) for i in range(2)]
        xr = [_sb(es, nc, "fd_xr%d" % i, [128, 1024], F32) for i in range(2)]
        r = [_sb(es, nc, "fd_r", [128, 1024], F32)] * 2
        t = [_sb(es, nc, "fd_t", [128, 1024], F32)] * 2
        st = [_sb(es, nc, "fd_st%d" % i, [128, 2], F32) for i in range(2)]
        ob = [_sb(es, nc, "fd_ob%d" % i, [128, 1024], F32) for i in range(2)]
        py = [_ps(es, nc, "fd_py%d" % i, [128, 512], F32) for i in range(4)]
        wres = load_weight_bf16(S, W, w_down[li], NT)
        for s_ in range(2):
            K.dma("sp", gate[s_].t[:], MOD.t[s_, :, 5120:6144], [], [gate[s_]], gate[s_])
        K.dma("sp", lnw.t[:], P["ln2_w"][li], [], [lnw], lnw)
        K.dma("sp", lnb.t[:], P["ln2_b"][li], [], [lnb], lnb)
        K.dma("sp", cw.t[:], P["ffn_cw"][li], [], [cw], cw)
        K.dma("sp", cb.t[:], P["ffn_cb"][li], [], [cb], cb)
        def loadsA(c):
            K.dma("sp", uh[c % 3].t[:, :, 1:129], UVT.t[c, :, 0:NT, :], [], [uh[c % 3]], uh[c % 3])
            K.dma("sp", vv[c % 2].t[:], UVT.t[c, :, NT:2 * NT, :], [], [vv[c % 2]], vv[c % 2])

        def loadsC(c):
            K.dma("sp", xr[c % 2].t[:], X1.t[c * 128:(c + 1) * 128, :], [], [xr[c % 2]], xr[c % 2])

        def stA(c):
            b = c % 2
            ub = uh[c % 3]
            lv = c not in (0, 2)
            rv = c not in (1, NCH - 1)
            if lv:
                K.op("pool", "tensor_copy", [uh[(c - 1) % 3]], [ub], out=ub.t[:, :, 0:1], in_=uh[(c - 1) % 3].t[:, :, 128:129])
            else:
                K.op("pool", "memset", [], [ub], ub.t[:, :, 0:1], 0.0)
            if rv:
                K.op("pool", "tensor_copy", [uh[(c + 1) % 3]], [ub], out=ub.t[:, :, 129:130], in_=uh[(c + 1) % 3].t[:, :, 1:2])
            else:
                K.op("pool", "memset", [], [ub], ub.t[:, :, 129:130], 0.0)
            asl = accsl[b]
            for ct in range(NT):
                K.op("act", "activation", [ub, cw, cb], [asl[ct]], out=acc[b].t[:, ct, :], in_=ub.t[:, ct, 0:128], func=AF.Identity,
                     scale=cw.t[:, ct, 0:1], bias=cb.t[:, ct:ct + 1])
            for j in range(1, 3):
                for ct in range(NT):
                    K.op("dve", "scalar_tensor_tensor", [ub, cw, asl[ct]], [asl[ct]], out=acc[b].t[:, ct, :], in0=ub.t[:, ct, j:j + 128],
                         scalar=cw.t[:, ct, j:j + 1], in1=acc[b].t[:, ct, :], op0=ALU.mult, op1=ALU.add)
            K.op("act", "activation", asl, [gl[b]], out=gl[b].t[:], in_=acc[b].t[:], func=AF.Gelu)
            K.op("pool", "tensor_tensor", [gl[b], vv[b]], [gT[b]], out=gT[b].t[:], in0=gl[b].t[:], in1=vv[b].t[:], op=ALU.mult)

        def stB(c):
            b = c % 2
            pys = [py[(c % 2) * 2], py[(c % 2) * 2 + 1]]
            for hf in range(2):
                for kc in range(NT):
                    K.op("pe", "matmul", [gT[b]] + wres, [pys[hf]], pys[hf].t[:], gT[b].t[:, kc, :], W.t[:, kc, hf * 512:(hf + 1) * 512],
                         start=(kc == 0), stop=(kc == NT - 1))

        def stC(c):
            b = c % 2
            src = 1 if c < NCTX // 128 else 0
            rows = slice(c * 128, (c + 1) * 128)
            pys = [py[(c % 2) * 2], py[(c % 2) * 2 + 1]]
            resid_ln(K, pys, xr[b], gate[src], lnw, lnb, r[b], t[b], st[b], ob[b])
            if out_ap is None:
                K.dma("sp", XN.t[rows, :], ob[b].t[:], [ob[b]], [], ob[b])
            elif c >= NCTX // 128:
                K.dma("sp", out_ap[c * 128 - NCTX:(c + 1) * 128 - NCTX, :], ob[b].t[:], [ob[b]], [], ob[b])

        loadsA(0)
        loadsC(0)
        if NCH > 1:
            loadsA(1)
            loadsC(1)
        stA(0)
        for c in range(NCH):
            stB(c)
            if c + 2 < NCH:
                loadsA(c + 2)
            if c + 1 < NCH:
                stA(c + 1)
            stC(c)
            if c + 2 < NCH:
                loadsC(c + 2)
        K.end_phase("ffndown%d" % li)

PARAM_SHAPES = {
    "ssd_cw": [128, 12, 5], "ssd_cb": [128, 12], "ssd_dtb": [128, 32], "ssd_alog": [128, 32],
    "ret_dec": [128, 8], "diff_lam": [128, 256], "ln1_w": [128, 1024], "ln1_b": [128, 1024],
    "ln2_w": [128, 1024], "ln2_b": [128, 1024], "ssd_nw": [128, 1024], "ssd_d": [128, 16],
    "ret_nw": [128, 128], "diff_nw": [128, 128], "ffn_cw": [128, 22, 3], "ffn_cb": [128, 22],
}
PHASES = ["mod", "inproj", "ssdprep", "scans", "retprep", "scanr", "attnprep", "attn", "post", "ffnup", "ffndown"]


def build_program(NL, depth, dbg=False, stop_after=None):
    nc = bass.Bass("TRN2", target_bir_lowering=False)
    K = Ctx(nc, NL, depth, dbg)
    T = K.T
    inp = {}

    def din(name, shape, dt=F32):
        inp[name] = nc.dram_tensor(name, list(shape), dt, kind="ExternalInput").ap()
        return inp[name]

    x_in = din("x", [NL, D])
    ctx_in = din("ctx", [NCTX, D])
    c_fm = din("c_fm", [128, 8])
    cc_fm = din("cc_fm", [128, 8])
    w_ada = din("w_ada", [depth, D, 6 * D])
    b_ada = din("b_ada", [depth, 6 * D])
    w_in = din("w_in", [depth, D, IN_COLS])
    w_out = din("w_out", [depth, 2 * D, D])
    w_up = din("ffn_w_up", [depth, D, 2 * DFF])
    w_down = din("ffn_w_down", [depth, DFF, D])
    P = {k: din("p_" + k, [depth] + v) for k, v in PARAM_SHAPES.items()}
    consts = {
        "identb": din("identb", [128, 128], BF16), "identf": din("identf", [128, 128]),
        "ones": din("ones", [128, 128]), "tri": din("tri", [2, 128, 128]), "stri": din("stri", [2, 128, 128]),
        "mask": din("mask", [2, 128, 128]), "ret_tab": din("ret_tab", [T, 128]), "diff_tab": din("diff_tab", [T, 128]),
    }
    out = nc.dram_tensor("out", [NL, D], F32, kind="ExternalOutput").ap()

    X = K.dram_t("X", [T, D], F32)
    X1 = K.dram_t("X1", [T, D], F32)
    MOD = K.dram_t("MOD", [2, 128, 6 * D], F32)
    PTOK = K.dram_t("PTOK", [T, PT_W], F32)
    XBCT = K.dram_t("XBCT", [K.NCH, 128, 12, 128], F32)
    XSB = K.dram_t("XSB", [T, 1280], BF16)
    BCT = K.dram_t("BCT", [512, T], BF16)
    DTLA = K.dram_t("DTLA", [T, 64], F32)
    RQKT = K.dram_t("RQKT", [64, 8, T], BF16)
    RTOK = K.dram_t("RTOK", [T, 768], BF16)
    YSS = K.dram_t("YSS", [T, 1024], F32)
    YSR = K.dram_t("YSR", [T, 512], F32)
    KT = K.dram_t("KT", [128, 4, T], BF16)
    VA = K.dram_t("VA", [4, 128, K.NCH, 128], BF16)
    QT = K.dram_t("QT", [128, 4, T], BF16)
    KMAX = K.dram_t("KMAX", [128, 8], F32)
    KM8 = K.dram_t("KM8", [8, 1], F32)
    NB = K.dram_t("NB", [128, 4], F32)
    OD = K.dram_t("OD", [T, 512], F32)
    UVT = K.dram_t("UVT", [K.NCH, 128, 44, 128], F32)

    with ExitStack() as es:
        K.eng_sem = {e: es.enter_context(nc.semaphore("s_" + e)) for e in ENGS}
        K.dma_sems = [es.enter_context(nc.semaphore("d%d" % i)) for i in range(64)]
        K.eng_base = {e: 0 for e in ENGS}
        K.dma_base = [0] * 64
        phase_init(K, x_in, ctx_in, X)
        done = False
        for li in range(depth):
            last = li == depth - 1
            steps = [
                ("mod", lambda: phase_mod(K, li, c_fm, cc_fm, w_ada, b_ada, MOD, consts)),
                ("inproj", lambda: phase_proj(K, "inproj%d" % li, X, MOD, 1024, 0, w_in[li], IN_COLS, TOK_GROUPS, PTOK, 1024, 12, XBCT, consts)),
                ("ssdprep", lambda: phase_ssd_prep(K, li, PTOK, XBCT, P, XSB, BCT, DTLA, consts)),
                ("scans", lambda: [phase_scan(K, li, fam_ssd(), d, {"BCT": BCT, "XSB": XSB, "DTLA": DTLA}, P, YSS, consts) for d in range(2)]),
                ("retprep", lambda: phase_ret_prep(K, li, PTOK, P, RQKT, RTOK, consts)),
                ("scanr", lambda: [phase_scan(K, li, fam_ret(), d, {"RQKT": RQKT, "RTOK": RTOK}, P, YSR, consts) for d in range(2)]),
                ("attnprep", lambda: phase_attn_prep(K, li, PTOK, KT, VA, QT, KMAX, KM8, NB, consts)),
                ("attn", lambda: phase_attn(K, li, KT, VA, QT, NB, P, OD, consts)),
                ("post", lambda: phase_post_outproj(K, li, X, MOD, PTOK, XSB, YSS, YSR, OD, P, w_out, X1, consts)),
                ("ffnup", lambda: phase_proj(K, "ffnup%d" % li, X1, MOD, 4096, 3072, w_up[li], 2 * DFF, [], None, 0, 44, UVT, consts)),
                ("ffndown", lambda: phase_ffn_down(K, li, X1, MOD, UVT, P, w_down, X, out if last else None, consts)),
            ]
            for nm, fn in steps:
                fn()
                if stop_after == nm:
                    done = True
                    break
            if done:
                break
    return nc, K


def _tables(T):
    f32 = np.float32
    n_lat = T - NCTX
    inv_ax = (f32(1.0) / (f32(10000.0) ** (np.arange(16, dtype=f32) / f32(16)))).astype(f32)
    i = np.arange(n_lat)
    row = (i // GRID_W).astype(f32)
    col = (i % GRID_W).astype(f32)
    ar = (row[:, None] * inv_ax[None, :]).astype(f32).astype(np.float64)
    ac = (col[:, None] * inv_ax[None, :]).astype(f32).astype(np.float64)
    dt = np.zeros((T, 128), f32)
    dt[:NCTX, 0:64] = 1.0
    dt[NCTX:, 0:64] = np.concatenate([np.cos(ar), np.cos(ar), np.cos(ac), np.cos(ac)], 1)
    dt[NCTX:, 64:128] = np.concatenate([-np.sin(ar), np.sin(ar), -np.sin(ac), np.sin(ac)], 1)
    inv_ret = (f32(1.0) / (f32(10000.0) ** np.linspace(0.0, 1.0, 32, dtype=f32))).astype(f32)
    pos = np.arange(T).astype(f32)
    a = (pos[:, None] * inv_ret[None, :]).astype(f32).astype(np.float64)
    rt = np.concatenate([np.cos(a), np.cos(a), -np.sin(a), np.sin(a)], 1).astype(f32)
    return dt, rt


def make_consts(T):
    f32 = np.float32
    t = np.arange(128)
    le = (t[:, None] <= t[None, :]).astype(f32)
    ge = (t[:, None] >= t[None, :]).astype(f32)
    gt = (t[:, None] > t[None, :]).astype(f32)
    lt = (t[:, None] < t[None, :]).astype(f32)
    dtab, rtab = _tables(T)
    return {
        "identb": np.eye(128, dtype=f32).astype(ml_dtypes.bfloat16), "identf": np.eye(128, dtype=f32),
        "ones": np.ones((128, 128), f32), "tri": np.stack([le, ge]), "stri": np.stack([gt, lt]),
        "mask": np.stack([le, ge]), "ret_tab": rtab, "diff_tab": dtab,
    }


def _rep(a, depth):
    a = np.asarray(a[:depth], np.float32).reshape(depth, 1, -1)
    return np.ascontiguousarray(np.broadcast_to(a, (depth, 128, a.shape[2])))


def make_in_maps(inputs, NL, depth, ncores):
    T = NL + NCTX
    cs = make_consts(T)
    L = depth
    shared = {
        "w_ada": np.ascontiguousarray(inputs["w_ada"][:L]), "b_ada": np.ascontiguousarray(inputs["b_ada"][:L]),
        "w_in": np.ascontiguousarray(inputs["w_in"][:L]), "w_out": np.ascontiguousarray(inputs["w_out"][:L]),
        "ffn_w_up": np.ascontiguousarray(inputs["ffn_w_up"][:L]), "ffn_w_down": np.ascontiguousarray(inputs["ffn_w_down"][:L]),
        "cc_fm": np.ascontiguousarray(inputs["c_ctx"].reshape(8, 128).T),
        "p_ssd_cw": np.ascontiguousarray(inputs["ssd_conv_w"][:L].reshape(L, 5, 12, 128).transpose(0, 3, 2, 1)),
        "p_ssd_cb": np.ascontiguousarray(inputs["ssd_conv_b"][:L].reshape(L, 12, 128).transpose(0, 2, 1)),
        "p_ssd_dtb": _rep(inputs["ssd_dt_bias"].reshape(-1, 32), L), "p_ssd_alog": _rep(inputs["ssd_a_log"].reshape(-1, 32), L),
        "p_ret_dec": _rep(inputs["ret_decay"].reshape(-1, 8), L), "p_diff_lam": _rep(inputs["diff_lambda"].reshape(-1, 256), L),
        "p_ln1_w": _rep(inputs["ln1_w"], L), "p_ln1_b": _rep(inputs["ln1_b"], L),
        "p_ln2_w": _rep(inputs["ln2_w"], L), "p_ln2_b": _rep(inputs["ln2_b"], L),
        "p_ssd_nw": _rep(inputs["ssd_norm_w"], L), "p_ssd_d": _rep(inputs["ssd_d"], L),
        "p_ret_nw": _rep(inputs["ret_norm_w"], L), "p_diff_nw": _rep(inputs["diff_norm_w"], L),
        "p_ffn_cw": np.ascontiguousarray(inputs["ffn_conv_w"][:L].reshape(L, 3, 22, 128).transpose(0, 3, 2, 1)),
        "p_ffn_cb": np.ascontiguousarray(inputs["ffn_conv_b"][:L].reshape(L, 22, 128).transpose(0, 2, 1)),
    }
    shared.update(cs)
    maps = []
    for b in range(ncores):
        m = dict(shared)
        m["x"] = np.ascontiguousarray(inputs["x"][b, :NL])
        m["ctx"] = np.ascontiguousarray(inputs["ctx"][b])
        m["c_fm"] = np.ascontiguousarray(inputs["c"][b].reshape(8, 128).T)
        maps.append(m)
    return maps


def kernel(**inputs):
    inputs = {k: np.asarray(v) for k, v in inputs.items()}
    NL = inputs["x"].shape[1]
    depth = inputs["w_ada"].shape[0]
    nb = inputs["x"].shape[0]
    nc, K = build_program(NL, depth)
    maps = make_in_maps(inputs, NL, depth, nb)
    res = run_bass_kernel_spmd(nc, maps, core_ids=list(range(nb)))
    return np.stack([np.asarray(r["out"], np.float32) for r in res.results], axis=0)
```
